# Optimizing a Trainium2 kernel written in Bass

```python
import jax, jax.numpy as jnp
from jax import lax
import numpy as np

D_MODEL = 1024
BATCH = 8
SEQ = 2048
DEPTH = 2
DEC_BATCH = 16
DEC_SEQ = 64
PAST_LEN = 4096

CHUNK = 64
N_META = 16
N_MIXERS = 2
HEAD_DIM = 64
N_HEADS = D_MODEL // HEAD_DIM
D_FF = 2752
LORA_DECAY = 64
LORA_AAA = 64
LORA_GATE = 128
Q_BLOCK = 128
RMS_EPS = 1e-6
GN_EPS = 64e-5

kernel_name = "rwkv7_stickbreaking_macaron_stream"


def rms_norm(x, g):
    xf = x.astype(jnp.float32)
    y = xf * lax.rsqrt(jnp.mean(xf * xf, axis=-1, keepdims=True) + RMS_EPS)
    return (y * g.astype(jnp.float32)).astype(x.dtype)


def swiglu(x, w_in, w_out):
    gate, up = jnp.split(x @ w_in, 2, axis=-1)
    return (jax.nn.silu(gate) * up) @ w_out


def rwkv7_time_mix(h, shift_prev, S0, mu, w_rkv, w0, w1, w2, a0, a1, a2, g1, g2,
                   k_k, k_a, r_k, gn_w, gn_b, w_o):
    B, T, D = h.shape
    H, N = N_HEADS, HEAD_DIM
    f32 = jnp.float32
    h_prev = jnp.concatenate([shift_prev[:, None, :].astype(h.dtype), h[:, :-1]], axis=1)
    dx = h_prev - h
    xr, xw, xk, xv, xa, xg = (h + dx * mu[i] for i in range(6))
    r = xr @ w_rkv[0]
    k = xk @ w_rkv[1]
    v = xv @ w_rkv[2]
    log_rate = -jax.nn.softplus(-(w0 + jnp.tanh(xw @ w1) @ w2)) - 0.5
    decay = jnp.exp(-jnp.exp(log_rate.astype(f32)))
    a = jax.nn.sigmoid((a0 + (xa @ a1) @ a2).astype(f32))
    g = jax.nn.sigmoid(xg @ g1) @ g2
    heads = lambda t: t.astype(f32).reshape(B, T, H, N)
    r, k, v, decay, a = map(heads, (r, k, v, decay, a))
    kk = k * k_k.astype(f32).reshape(H, N)
    kk = kk / jnp.maximum(jnp.sqrt(jnp.sum(kk * kk, axis=-1, keepdims=True)), 1e-12)
    k = k * (1.0 + (a - 1.0) * k_a.astype(f32).reshape(H, N))

    def step(S, inp):
        r_t, w_t, k_t, v_t, kk_t, a_t = inp
        sa = jnp.einsum('bhvk,bhk->bhv', S, -kk_t)
        S = (S * w_t[:, :, None, :]
             + sa[..., None] * (kk_t * a_t)[:, :, None, :]
             + v_t[..., None] * k_t[:, :, None, :])
        return S, jnp.einsum('bhvk,bhk->bhv', S, r_t)

    seq = tuple(jnp.swapaxes(t, 0, 1) for t in (r, decay, k, v, kk, a))
    S_fin, o = lax.scan(step, S0.astype(f32), seq)
    o = jnp.swapaxes(o, 0, 1)
    mean = jnp.mean(o, axis=-1, keepdims=True)
    var = jnp.mean(jnp.square(o - mean), axis=-1, keepdims=True)
    o = ((o - mean) * lax.rsqrt(var + GN_EPS) * gn_w.astype(f32).reshape(H, N)
         + gn_b.astype(f32).reshape(H, N))
    o = o + jnp.sum(r * k * r_k.astype(f32), axis=-1, keepdims=True) * v
    out = (o.reshape(B, T, D).astype(h.dtype) * g) @ w_o
    return out, S_fin.astype(S0.dtype), h[:, -1]


def sb_block(q, q_pos, k, v, k_pos):
    z = jnp.einsum('bhqd,bhkd->bhqk', q, k, preferred_element_type=jnp.float32) * (HEAD_DIM ** -0.5)
    vis = k_pos[None, :] < q_pos[:, None]
    log_beta = jax.nn.log_sigmoid(z)
    log_1m = jnp.where(vis, jax.nn.log_sigmoid(-z), 0.0)
    later = lax.cumsum(log_1m, axis=3, reverse=True) - log_1m
    A = jnp.where(vis, jnp.exp(log_beta + later), 0.0)
    return jnp.einsum('bhqk,bhkd->bhqd', A.astype(v.dtype), v)


def stick_breaking_mix(h, k_past, v_past, w_qkv, q_norm_g, k_norm_g, w_o):
    B, T, D = h.shape
    qkv = (h @ w_qkv).reshape(B, T, 3, N_HEADS, HEAD_DIM)
    q = rms_norm(qkv[:, :, 0], q_norm_g).transpose(0, 2, 1, 3)
    k = rms_norm(qkv[:, :, 1], k_norm_g).transpose(0, 2, 1, 3)
    v = qkv[:, :, 2].transpose(0, 2, 1, 3)
    if k_past is None:
        n_blk = -(-T // Q_BLOCK)
        Lp = n_blk * Q_BLOCK
        pad = ((0, 0), (0, 0), (0, Lp - T), (0, 0))
        qp, kp, vp = jnp.pad(q, pad), jnp.pad(k, pad), jnp.pad(v, pad)
        pos = jnp.arange(Lp)
        q_blocks = qp.reshape(B, N_HEADS, n_blk, Q_BLOCK, HEAD_DIM).transpose(2, 0, 1, 3, 4)
        pos_blocks = pos.reshape(n_blk, Q_BLOCK)
        o = lax.map(lambda args: sb_block(args[0], args[1], kp, vp, pos), (q_blocks, pos_blocks))
        o = o.transpose(1, 2, 0, 3, 4).reshape(B, N_HEADS, Lp, HEAD_DIM)[:, :, :T]
    else:
        P = k_past.shape[2]
        k_all = jnp.concatenate([k_past.astype(k.dtype), k], axis=2)
        v_all = jnp.concatenate([v_past.astype(v.dtype), v], axis=2)
        o = sb_block(q, P + jnp.arange(T), k_all, v_all, jnp.arange(P + T))
    out = o.transpose(0, 2, 1, 3).reshape(B, T, D) @ w_o
    return out, k, v


def layer_stack(x, rwkv_S0, rwkv_shift0, k_past, v_past, ffn_norm_g, ffn_w_in, ffn_w_out,
                mix_norm_g, rwkv_params, sb_params):
    mu, w_rkv, w0, w1, w2, a0, a1, a2, g1, g2, k_k, k_a, r_k, gn_w, gn_b, rw_o = rwkv_params
    w_qkv, q_norm_g, k_norm_g, sb_o = sb_params
    rwkv_S, rwkv_shift, k_new, v_new = rwkv_S0, rwkv_shift0, None, None
    for i in range(DEPTH):
        x = x + 0.5 * swiglu(rms_norm(x, ffn_norm_g[i, 0]), ffn_w_in[i, 0], ffn_w_out[i, 0])
        h = rms_norm(x, mix_norm_g[i])
        if i % N_MIXERS == 0:
            y, rwkv_S, rwkv_shift = rwkv7_time_mix(h, rwkv_shift0, rwkv_S0, mu, w_rkv, w0, w1, w2,
                                                   a0, a1, a2, g1, g2, k_k, k_a, r_k, gn_w, gn_b, rw_o)
        else:
            y, k_new, v_new = stick_breaking_mix(h, k_past, v_past, w_qkv, q_norm_g, k_norm_g, sb_o)
        x = x + y
        x = x + 0.5 * swiglu(rms_norm(x, ffn_norm_g[i, 1]), ffn_w_in[i, 1], ffn_w_out[i, 1])
    return x, rwkv_S, rwkv_shift, k_new, v_new


def setup_inputs(seed: int = 0) -> dict:
    key = jax.random.key(seed)
    ks = jax.random.split(key, 32)
    f32 = jnp.float32
    D, H, N = D_MODEL, N_HEADS, HEAD_DIM
    nrm = lambda k, shape, s: jax.random.normal(k, shape, f32) * s
    return {
        "x_prompt": nrm(ks[0], (BATCH, SEQ, D), 1.0),
        "x_sample": nrm(ks[1], (DEC_BATCH, DEC_SEQ, D), 1.0),
        "state_rwkv_wkv": nrm(ks[2], (DEC_BATCH, H, N, N), 0.5),
        "state_rwkv_shift": nrm(ks[3], (DEC_BATCH, D), 1.0),
        "cache_sb_k": nrm(ks[4], (DEC_BATCH, H, PAST_LEN, N), 1.0),
        "cache_sb_v": nrm(ks[5], (DEC_BATCH, H, PAST_LEN, N), 1.0),
        "meta_tokens": nrm(ks[6], (N_META, D), 1.0),
        "ffn_norm_g": 1.0 + nrm(ks[7], (DEPTH, 2, D), 0.02),
        "ffn_w_in": nrm(ks[8], (DEPTH, 2, D, 2 * D_FF), D ** -0.5),
        "ffn_w_out": nrm(ks[9], (DEPTH, 2, D_FF, D), D_FF ** -0.5),
        "mix_norm_g": 1.0 + nrm(ks[10], (DEPTH, D), 0.02),
        "rwkv_mu": jax.random.uniform(ks[11], (6, D), f32),
        "rwkv_w_rkv": nrm(ks[12], (3, D, D), D ** -0.5),
        "rwkv_w0": jnp.linspace(-6.0, -1.0, D, dtype=f32) + nrm(ks[13], (D,), 0.1),
        "rwkv_w1": nrm(ks[14], (D, LORA_DECAY), D ** -0.5),
        "rwkv_w2": nrm(ks[15], (LORA_DECAY, D), 0.1 * LORA_DECAY ** -0.5),
        "rwkv_a0": nrm(ks[16], (D,), 0.1),
        "rwkv_a1": nrm(ks[17], (D, LORA_AAA), D ** -0.5),
        "rwkv_a2": nrm(ks[18], (LORA_AAA, D), 0.1 * LORA_AAA ** -0.5),
        "rwkv_g1": nrm(ks[19], (D, LORA_GATE), D ** -0.5),
        "rwkv_g2": nrm(ks[20], (LORA_GATE, D), LORA_GATE ** -0.5),
        "rwkv_k_k": 0.85 + nrm(ks[21], (D,), 0.02),
        "rwkv_k_a": 1.0 + nrm(ks[22], (D,), 0.02),
        "rwkv_r_k": nrm(ks[23], (H, N), 0.1),
        "rwkv_gn_w": 1.0 + nrm(ks[24], (D,), 0.02),
        "rwkv_gn_b": nrm(ks[25], (D,), 0.02),
        "rwkv_w_o": nrm(ks[26], (D, D), D ** -0.5),
        "sb_w_qkv": nrm(ks[27], (D, 3 * D), D ** -0.5),
        "sb_q_norm_g": 1.0 + nrm(ks[28], (N,), 0.02),
        "sb_k_norm_g": 1.0 + nrm(ks[29], (N,), 0.02),
        "sb_w_o": nrm(ks[30], (D, D), D ** -0.5),
    }


def reference(x_prompt, x_sample, state_rwkv_wkv, state_rwkv_shift, cache_sb_k, cache_sb_v,
              meta_tokens, ffn_norm_g, ffn_w_in, ffn_w_out, mix_norm_g,
              rwkv_mu, rwkv_w_rkv, rwkv_w0, rwkv_w1, rwkv_w2, rwkv_a0, rwkv_a1, rwkv_a2,
              rwkv_g1, rwkv_g2, rwkv_k_k, rwkv_k_a, rwkv_r_k, rwkv_gn_w, rwkv_gn_b, rwkv_w_o,
              sb_w_qkv, sb_q_norm_g, sb_k_norm_g, sb_w_o):
    rwkv_params = (rwkv_mu, rwkv_w_rkv, rwkv_w0, rwkv_w1, rwkv_w2, rwkv_a0, rwkv_a1, rwkv_a2,
                   rwkv_g1, rwkv_g2, rwkv_k_k, rwkv_k_a, rwkv_r_k, rwkv_gn_w, rwkv_gn_b, rwkv_w_o)
    sb_params = (sb_w_qkv, sb_q_norm_g, sb_k_norm_g, sb_w_o)

    B = x_prompt.shape[0]
    meta = jnp.broadcast_to(meta_tokens.astype(x_prompt.dtype)[None], (B, N_META, D_MODEL))
    xp = jnp.concatenate([meta, x_prompt], axis=1)
    S0_p = jnp.zeros((B, N_HEADS, HEAD_DIM, HEAD_DIM), x_prompt.dtype)
    shift0_p = jnp.zeros((B, D_MODEL), x_prompt.dtype)
    yp, S_p, shift_p, k_p, v_p = layer_stack(xp, S0_p, shift0_p, None, None, ffn_norm_g, ffn_w_in,
                                             ffn_w_out, mix_norm_g, rwkv_params, sb_params)
    y_prompt = yp[:, N_META:]

    ys, S_s, shift_s, k_s, v_s = layer_stack(x_sample, state_rwkv_wkv, state_rwkv_shift, cache_sb_k,
                                             cache_sb_v, ffn_norm_g, ffn_w_in, ffn_w_out, mix_norm_g,
                                             rwkv_params, sb_params)
    return (y_prompt, ys, S_p, shift_p, k_p, v_p, S_s, shift_s, k_s, v_s)
```

```python
import contextlib
import numpy as np
import concourse.bass as bass
import concourse.mybir as mybir
from concourse.bass_utils import run_bass_kernel_spmd

F32 = mybir.dt.float32
BF16 = mybir.dt.bfloat16
AF = mybir.ActivationFunctionType
ALU = mybir.AluOpType
AX = mybir.AxisListType

D = 1024
NCH = 8
SEQ = 2048
NMETA = 16
TP = NMETA + SEQ
DSEQ = 64
NTOK = TP + 2 * DSEQ
DFF = 2752
H = 16
N = 64
PAST = 4096
RMS_EPS = 1e-6
GN_EPS = 64e-5

PV = {}
_pv_names = ["ffn_g00", "ffn_g01", "ffn_g10", "ffn_g11", "mix_g0", "mix_g1",
             "mu0", "mu1", "mu2", "mu3", "mu4", "mu5", "w0", "a0", "k_k", "k_a", "r_k", "gn_w", "gn_b"]
for _i, _n in enumerate(_pv_names):
    PV[_n] = _i
NPV = len(_pv_names)

C_IDENT = 0
C_ONES = 128
C_TRI_INCL = 256
C_TRI_COMP = 384
C_EPS_RMS = 512
C_EPS_GN = 513
C_BD = 520
C_M1 = 648
C_M2 = 1160
C_ONE = 514
NCONST = 1672


_UN = [0]


def g_sbuf(nc, name, shape, dt):
    _UN[0] += 1
    return nc.sbuf_tensor("%s_u%d" % (name, _UN[0]), shape, dt)


class Buf:
    __slots__ = ("name", "w", "rs")

    def __init__(self, name=""):
        self.name = name
        self.w = None
        self.rs = {}


class Sched:
    ENG = ("pe", "dve", "act", "pool", "sp")

    def __init__(self, nc, ring=12):
        self.nc = nc
        self.q = {e: [] for e in self.ENG}
        self.cnt = {e: 0 for e in self.ENG}
        self.seen = {e: {} for e in self.ENG}
        self.sems = {}
        for e in self.ENG:
            self.sems[e] = nc.alloc_semaphore("c_" + e)
        self.ring = ring
        self.dma_n = {}
        self.dma_last = {}
        for qn in ("sp", "pool", "act"):
            self.dma_n[qn] = 0
            for s in range(ring):
                self.sems[("dma", qn, s)] = nc.alloc_semaphore("d_%s_%d" % (qn, s))
        self.ninstr = 0

    def _wait(self, eng, key, val):
        if self.seen[eng].get(key, 0) >= val:
            return
        self.seen[eng][key] = val
        self.q[eng].append(("w", key, val))
        self.ninstr += 1

    def _deps(self, eng, reads, writes):
        deps = {}
        for b in reads:
            if b.w is not None:
                k, v = b.w
                if deps.get(k, 0) < v:
                    deps[k] = v
        for b in writes:
            if b.w is not None:
                k, v = b.w
                if deps.get(k, 0) < v:
                    deps[k] = v
            for k, v in b.rs.items():
                if deps.get(k, 0) < v:
                    deps[k] = v
        for k, v in deps.items():
            if eng == "pe" and k == "pe":
                continue
            self._wait(eng, k, v)

    def _mark(self, tok, reads, writes):
        k, v = tok
        for b in reads:
            if b.rs.get(k, 0) < v:
                b.rs[k] = v
        for b in writes:
            b.w = tok
            b.rs = {}

    def op(self, eng, fn, reads=(), writes=()):
        self._deps(eng, reads, writes)
        self.cnt[eng] += 1
        tok = (eng, self.cnt[eng])
        self.q[eng].append(("op", fn, eng, 1))
        self.ninstr += 1
        self._mark(tok, reads, writes)
        return tok

    def dma(self, qn, out, in_, reads=(), writes=(), **kw):
        self._deps(qn, reads, writes)
        n = self.dma_n[qn]
        s = n % self.ring
        val = 16 * (n // self.ring + 1)
        key = ("dma", qn, s)
        if n >= self.ring:
            self._wait(qn, key, val - 16)
        self.dma_n[qn] = n + 1
        fn = lambda e, out=out, in_=in_, kw=kw: e.dma_start(out=out, in_=in_, **kw)
        self.q[qn].append(("op", fn, key, 16))
        self.ninstr += 1
        tok = (key, val)
        self.dma_last[key] = val
        self._mark(tok, reads, writes)
        return tok

    def barrier(self):
        toks = [(e, self.cnt[e]) for e in self.ENG if self.cnt[e] > 0]
        toks += list(self.dma_last.items())
        for e in self.ENG:
            for k, v in toks:
                if k == e and e == "pe":
                    continue
                self._wait(e, k, v)

    def finish(self):
        for k, v in self.dma_last.items():
            self._wait("sp", k, v)
        for e in self.ENG:
            if e != "sp" and self.cnt[e] > 0:
                self._wait("sp", e, self.cnt[e])

    def replay(self, block):
        sems = self.sems

        def mk(name):
            items = self.q[name]

            def body(e):
                for it in items:
                    if it[0] == "w":
                        e.wait_ge(sems[it[1]], it[2])
                    else:
                        ins = it[1](e)
                        ins.then_inc(sems[it[2]], it[3])
            return body

        block.tensor(mk("pe"))
        block.vector(mk("dve"))
        block.scalar(mk("act"))
        block.gpsimd(mk("pool"))
        block.sync(mk("sp"))


def col_tiles():
    t = []
    c = 0
    while c < NTOK:
        n = min(512, NTOK - c)
        t.append((c, n))
        c += n
    return t


COLT = col_tiles()
NCT = len(COLT)
TT = [(0, NMETA)] + [(NMETA + 128 * i, 128) for i in range(16)] + [(TP, 128)]


class Ctx:
    pass


def build(stages):
    nc = bass.Bass("TRN2", target_bir_lowering=False)
    P = Sched(nc)
    g = Ctx()
    g.nc, g.P = nc, P
    g.rwkv_tiles = None
    g.do_sample = "nosample" not in stages
    g.sb_pairs = None
    for st in stages:
        if isinstance(st, tuple) and st[0] == "sb_pairs":
            g.sb_pairs = list(st[1])
    for st in stages:
        if isinstance(st, tuple) and st[0] == "rwkv_tiles":
            g.rwkv_tiles = list(st[1])
        if isinstance(st, tuple) and st[0] == "sb_stop":
            g.sb_stop = float(st[1])
        if isinstance(st, tuple) and st[0] == "rwkv_stop":
            g.rwkv_stop = float(st[1])
    dram = {}

    def din(name, shape, dt=F32):
        dram[name] = nc.dram_tensor(name, list(shape), dt, kind="ExternalInput").ap()
        return dram[name]

    def dout(name, shape, dt=F32):
        dram[name] = nc.dram_tensor(name, list(shape), dt, kind="ExternalOutput").ap()
        return dram[name]

    g.dram = dram
    din("x_prompt", (SEQ, D))
    din("x_sample", (2 * DSEQ, D))
    din("meta_tokens", (NMETA, D))
    din("consts", (128, NCONST))
    din("pvec", (128, NPV * 8))
    din("ffn_w_in", (2, 2, D, 2 * DFF))
    din("ffn_w_out", (2, 2, DFF, D))
    din("rwkv_w_rkv", (3, D, D))
    din("rwkv_w_o", (D, D))
    din("rwkv_w1", (D, 64))
    din("rwkv_w2", (64, D))
    din("rwkv_a1", (D, 64))
    din("rwkv_a2", (64, D))
    din("rwkv_g1", (D, 128))
    din("rwkv_g2", (128, D))
    din("sb_w_qkv", (D, 3 * D))
    din("sb_w_o", (D, D))
    din("gqk", (128, 256))
    if g.do_sample:
        din("cache_k_in", (2, H, PAST, N))
        din("cache_v_in", (2, H, PAST, N))
    dout("cache_k_p", (H, TP, N))
    dout("cache_v_p", (H, TP, N))
    dout("cache_k_s", (2, H, DSEQ, N))
    dout("cache_v_s", (2, H, DSEQ, N))
    din("shift_in", (128, 16))
    din("state_wkv_in", (2, H, N, N))
    dout("state_wkv_p", (H, N, N))
    dout("shift_p", (8, 128))
    dout("state_wkv_s", (2, H, N, N))
    dout("shift_s", (2, 8, 128))
    dout("y_prompt", (SEQ, D))
    dout("y_sample", (2 * DSEQ, D))

    g.xT = nc.alloc_sbuf_tensor("xT", [128, NCH, NTOK], F32)
    g.xT_b = [[Buf("xT") for _ in range(NCT)] for _ in range(NCH)]
    g.consts = nc.alloc_sbuf_tensor("consts_sb", [128, NCONST], F32)
    g.consts_bf = nc.alloc_sbuf_tensor("consts_bf", [128, 512], BF16)
    g.pvec = nc.alloc_sbuf_tensor("pvec_sb", [128, NPV * 8], F32)
    g.cb = Buf("consts")
    g.banks = [nc.alloc_psum_tensor("bank%d" % i, [128, 512], F32) for i in range(8)]
    g.bank_b = [Buf("bank%d" % i) for i in range(8)]
    g.bank_rr = 0

    P.dma("sp", g.consts[:], dram["consts"], writes=[g.cb])
    P.dma("sp", g.pvec[:], dram["pvec"], writes=[g.cb])
    P.op("dve", lambda e: e.tensor_copy(out=g.consts_bf[:], in_=g.consts[:, 0:512]), reads=[g.cb], writes=[g.cb])

    load_x(g)
    for li in range(2):
        if ("ffn", li, 0) in stages:
            ffn(g, li, 0)
        if li == 0 and "rwkv" in stages:
            rwkv(g, tiles=g.rwkv_tiles)
        if li == 1 and "sb" in stages:
            sb_attn(g, do_sample=g.do_sample, pairs=g.sb_pairs)
        if ("ffn", li, 1) in stages:
            ffn(g, li, 1)
    store_y(g)
    P.finish()
    with nc.Block() as block:
        P.replay(block)
    print("instructions:", P.ninstr)
    return nc


def xt_bufs_for_cols(g, c, col0, n):
    out = []
    for t, (c0, nn) in enumerate(COLT):
        if c0 < col0 + n and col0 < c0 + nn:
            out.append(g.xT_b[c][t])
    return out


def load_x(g):
    nc, P = g.nc, g.P
    ident = g.consts[:, C_IDENT:C_IDENT + 128]
    with contextlib.ExitStack() as es:
        stg = [es.enter_context(g_sbuf(nc, "ldstg%d" % i, [128, D], F32)) for i in range(3)]
        stg_b = [Buf("ldstg") for _ in range(3)]
        for ti, (col0, n) in enumerate(TT):
            s = ti % 3
            if ti == 0:
                src = g.dram["meta_tokens"]
            elif ti <= 16:
                src = g.dram["x_prompt"][(ti - 1) * 128:ti * 128, :]
            else:
                src = g.dram["x_sample"]
            P.dma("sp", stg[s][0:n, :], src, writes=[stg_b[s]])
            for half in range(2):
                bk = (ti * 2 + half) % 2 + 6
                bank = g.banks[bk]
                for cc in range(4):
                    c = half * 4 + cc
                    P.op("pe", lambda e, bank=bank, cc=cc, c=c, s=s, n=n: e.transpose(
                        out=bank[:, cc * 128:cc * 128 + n], in_=stg[s][0:n, c * 128:(c + 1) * 128],
                        identity=ident[0:n, 0:n]),
                        reads=[stg_b[s], g.cb], writes=[g.bank_b[bk]])
                wb = []
                for cc in range(4):
                    wb += xt_bufs_for_cols(g, half * 4 + cc, col0, n)
                src_ap = bank[:].rearrange("p (a b) -> p a b", a=4)[:, :, 0:n]
                dst_ap = g.xT[:, half * 4:half * 4 + 4, col0:col0 + n]
                eng = "act" if half == 0 else "dve"
                if eng == "act":
                    P.op("act", lambda e, d=dst_ap, s_=src_ap: e.copy(out=d, in_=s_), reads=[g.bank_b[bk]], writes=wb)
                else:
                    P.op("dve", lambda e, d=dst_ap, s_=src_ap: e.tensor_copy(out=d, in_=s_), reads=[g.bank_b[bk]], writes=wb)
        P.barrier()


def store_y(g):
    nc, P = g.nc, g.P
    ident = g.consts[:, C_IDENT:C_IDENT + 128]
    with contextlib.ExitStack() as es:
        stg = [es.enter_context(g_sbuf(nc, "ststg%d" % i, [128, D], F32)) for i in range(3)]
        stg_b = [Buf("ststg") for _ in range(3)]
        k = 0
        for ti, (col0, n) in enumerate(TT):
            if ti == 0:
                continue
            s = k % 3
            k += 1
            for half in range(2):
                bk = (ti * 2 + half) % 2 + 6
                bank = g.banks[bk]
                for cc in range(4):
                    c = half * 4 + cc
                    P.op("pe", lambda e, bank=bank, cc=cc, c=c, col0=col0, n=n: e.transpose(
                        out=bank[0:n, cc * 128:(cc + 1) * 128], in_=g.xT[:, c, col0:col0 + n],
                        identity=ident),
                        reads=xt_bufs_for_cols(g, c, col0, n) + [g.cb], writes=[g.bank_b[bk]])
                dst_ap = stg[s][0:n, half * 512:(half + 1) * 512]
                src_ap = bank[0:n, :]
                if half == 0:
                    P.op("act", lambda e, d=dst_ap, s_=src_ap: e.copy(out=d, in_=s_), reads=[g.bank_b[bk]], writes=[stg_b[s]])
                else:
                    P.op("dve", lambda e, d=dst_ap, s_=src_ap: e.tensor_copy(out=d, in_=s_), reads=[g.bank_b[bk]], writes=[stg_b[s]])
            if ti <= 16:
                dst = g.dram["y_prompt"][(ti - 1) * 128:ti * 128, :]
            else:
                dst = g.dram["y_sample"]
            P.dma("sp", dst, stg[s][0:n, :], reads=[stg_b[s]])
        P.barrier()


def rmsnorm_T(g, es, gname, out_dt=BF16):
    nc, P = g.nc, g.P
    xn = es.enter_context(g_sbuf(nc, "xn", [128, NCH, NTOK], out_dt))
    xn_b = [[Buf("xn") for _ in range(NCT)] for _ in range(NCH)]
    sq = [es.enter_context(g_sbuf(nc, "sq%d" % i, [128, 512], BF16)) for i in range(2)]
    sq_b = [Buf("sq") for _ in range(2)]
    rt = [es.enter_context(g_sbuf(nc, "rt%d" % i, [128, 512], F32)) for i in range(2)]
    rt_b = [Buf("rt") for _ in range(2)]
    ones = g.consts_bf[:, C_ONES:C_ONES + 128]
    gcol = PV[gname] * 8
    k = 0
    for t, (c0, n) in enumerate(COLT):
        bk = 6 + (t % 2)
        bank = g.banks[bk]
        for c in range(NCH):
            s = k % 2
            k += 1
            P.op("act", lambda e, s=s, c=c, c0=c0, n=n: e.activation(out=sq[s][:, 0:n], in_=g.xT[:, c, c0:c0 + n], func=AF.Square),
                 reads=[g.xT_b[c][t]], writes=[sq_b[s]])
            P.op("pe", lambda e, s=s, c=c, n=n, bank=bank: e.matmul(bank[:, 0:n], lhsT=ones, rhs=sq[s][:, 0:n], start=(c == 0), stop=(c == NCH - 1)),
                 reads=[sq_b[s], g.cb], writes=[g.bank_b[bk]])
        r = t % 2
        P.op("act", lambda e, r=r, n=n, bank=bank: e.activation(out=rt[r][:, 0:n], in_=bank[:, 0:n], func=AF.Sqrt, scale=1.0 / D, bias=g.consts[:, C_EPS_RMS:C_EPS_RMS + 1]),
             reads=[g.bank_b[bk], g.cb], writes=[rt_b[r]])
        P.op("dve", lambda e, r=r, n=n: e.reciprocal(out=rt[r][:, 0:n], in_=rt[r][:, 0:n]), reads=[rt_b[r]], writes=[rt_b[r]])
        for c in range(NCH):
            P.op("dve", lambda e, r=r, c=c, c0=c0, n=n: e.scalar_tensor_tensor(
                out=xn[:, c, c0:c0 + n], in0=g.xT[:, c, c0:c0 + n], scalar=g.pvec[:, gcol + c:gcol + c + 1],
                in1=rt[r][:, 0:n], op0=ALU.mult, op1=ALU.mult),
                reads=[g.xT_b[c][t], rt_b[r], g.cb], writes=[xn_b[c][t]])
    return xn, xn_b


def ffn(g, li, fi):
    nc, P = g.nc, g.P
    w_in = g.dram["ffn_w_in"][li, fi]
    w_out = g.dram["ffn_w_out"][li, fi]
    GS = 4
    groups = []
    j = 0
    while j < 22:
        groups.append(list(range(j, min(j + GS, 22))))
        j += GS
    csize = lambda j: 128 if j < 21 else 64
    with contextlib.ExitStack() as es:
        xn, xn_b = rmsnorm_T(g, es, "ffn_g%d%d" % (li, fi))
        act = [es.enter_context(g_sbuf(nc, "act%d" % i, [128, GS, NTOK], BF16)) for i in range(2)]
        act_b = [[[Buf("act") for _ in range(NCT)] for _ in range(GS)] for _ in range(2)]
        wi = [es.enter_context(g_sbuf(nc, "wi%d" % i, [128, NCH, 2, GS * 128], BF16)) for i in range(2)]
        wi_b = [Buf("wi") for _ in range(2)]
        wo = [es.enter_context(g_sbuf(nc, "wo%d" % i, [128, GS, D], BF16)) for i in range(2)]
        wo_b = [Buf("wo") for _ in range(2)]
        sl = [es.enter_context(g_sbuf(nc, "sl%d" % i, [128, 512], F32)) for i in range(2)]
        sl_b = [Buf("sl") for _ in range(2)]
        w_in_v = w_in.rearrange("(c p) n -> p c n", p=128)

        def load_w(gi):
            grp = groups[gi]
            s = gi % 2
            col0 = grp[0] * 128
            ncols = sum(csize(j) for j in grp)
            P.dma("pool", wi[s][:, :, 0, 0:ncols], w_in_v[:, :, col0:col0 + ncols], writes=[wi_b[s]])
            P.dma("pool", wi[s][:, :, 1, 0:ncols], w_in_v[:, :, DFF + col0:DFF + col0 + ncols], writes=[wi_b[s]])
            nfull = sum(1 for j in grp if csize(j) == 128)
            if nfull:
                P.dma("pool", wo[s][:, 0:nfull, :],
                      w_out[col0:col0 + nfull * 128, :].rearrange("(g p) n -> p g n", p=128), writes=[wo_b[s]])
            if nfull < len(grp):
                r0 = col0 + nfull * 128
                P.dma("pool", wo[s][0:64, nfull, :], w_out[r0:r0 + 64, :], writes=[wo_b[s]])

        kk = [0]

        def phase_a(gi):
            grp = groups[gi]
            s = gi % 2
            for jj, j in enumerate(grp):
                m = csize(j)
                for t, (c0, n) in enumerate(COLT):
                    q = kk[0] % 2
                    kk[0] += 1
                    bg, bu = g.banks[q * 2], g.banks[q * 2 + 1]
                    for c in range(NCH):
                        P.op("pe", lambda e, bg=bg, c=c, jj=jj, m=m, c0=c0, n=n, s=s: e.matmul(
                            bg[0:m, 0:n], lhsT=wi[s][:, c, 0, jj * 128:jj * 128 + m], rhs=xn[:, c, c0:c0 + n],
                            start=(c == 0), stop=(c == NCH - 1)),
                            reads=[wi_b[s], xn_b[c][t]], writes=[g.bank_b[q * 2]])
                    for c in range(NCH):
                        P.op("pe", lambda e, bu=bu, c=c, jj=jj, m=m, c0=c0, n=n, s=s: e.matmul(
                            bu[0:m, 0:n], lhsT=wi[s][:, c, 1, jj * 128:jj * 128 + m], rhs=xn[:, c, c0:c0 + n],
                            start=(c == 0), stop=(c == NCH - 1)),
                            reads=[wi_b[s], xn_b[c][t]], writes=[g.bank_b[q * 2 + 1]])
                    P.op("act", lambda e, q=q, bg=bg, m=m, n=n: e.activation(out=sl[q][0:m, 0:n], in_=bg[0:m, 0:n], func=AF.Silu),
                         reads=[g.bank_b[q * 2]], writes=[sl_b[q]])
                    P.op("dve", lambda e, q=q, bu=bu, m=m, n=n, s=s, jj=jj, c0=c0: e.tensor_tensor(
                        out=act[s][0:m, jj, c0:c0 + n], in0=sl[q][0:m, 0:n], in1=bu[0:m, 0:n], op=ALU.mult),
                        reads=[sl_b[q], g.bank_b[q * 2 + 1]], writes=[act_b[s][jj][t]])

        ko = [0]

        def phase_b(gi):
            grp = groups[gi]
            s = gi % 2
            for dc in range(NCH):
                for t, (c0, n) in enumerate(COLT):
                    bk = 4 + ko[0] % 2
                    ko[0] += 1
                    bank = g.banks[bk]
                    for jj, j in enumerate(grp):
                        m = csize(j)
                        P.op("pe", lambda e, bank=bank, jj=jj, m=m, dc=dc, c0=c0, n=n, s=s: e.matmul(
                            bank[:, 0:n], lhsT=wo[s][0:m, jj, dc * 128:(dc + 1) * 128], rhs=act[s][0:m, jj, c0:c0 + n],
                            start=(jj == 0), stop=(jj == len(grp) - 1)),
                            reads=[wo_b[s], act_b[s][jj][t]], writes=[g.bank_b[bk]])
                    P.op("dve", lambda e, bank=bank, dc=dc, c0=c0, n=n: e.scalar_tensor_tensor(
                        out=g.xT[:, dc, c0:c0 + n], in0=bank[:, 0:n], scalar=0.5, in1=g.xT[:, dc, c0:c0 + n],
                        op0=ALU.mult, op1=ALU.add),
                        reads=[g.bank_b[bk], g.xT_b[dc][t]], writes=[g.xT_b[dc][t]])

        ng = len(groups)
        load_w(0)
        load_w(1)
        phase_a(0)
        for gi in range(ng):
            if gi + 1 < ng:
                phase_a(gi + 1)
            phase_b(gi)
            if gi + 2 < ng:
                load_w(gi + 2)
        P.barrier()


def MM(g, out, lhsT, rhs, r, w, start=True, stop=True, skip=False):
    return g.P.op("pe", lambda e: e.matmul(out, lhsT=lhsT, rhs=rhs, start=start, stop=stop, skip_group_check=skip), reads=r, writes=w)


def TRP(g, out, in_, ident, r, w):
    return g.P.op("pe", lambda e: e.transpose(out=out, in_=in_, identity=ident), reads=r, writes=w)


def ACTF(g, out, in_, func, r, w, bias=None, scale=None):
    kw = {}
    if bias is not None:
        kw["bias"] = bias
    if scale is not None:
        kw["scale"] = scale
    return g.P.op("act", lambda e: e.activation(out=out, in_=in_, func=func, **kw), reads=r, writes=w)


def CPY(g, eng, out, in_, r, w):
    if eng == "act":
        return g.P.op("act", lambda e: e.copy(out=out, in_=in_), reads=r, writes=w)
    return g.P.op(eng, lambda e: e.tensor_copy(out=out, in_=in_), reads=r, writes=w)


def TTO(g, eng, out, in0, in1, op, r, w):
    return g.P.op(eng, lambda e: e.tensor_tensor(out=out, in0=in0, in1=in1, op=op), reads=r, writes=w)


def TSC(g, eng, out, in0, s1, s2, op0, op1, r, w):
    if s2 is None:
        return g.P.op(eng, lambda e: e.tensor_scalar(out=out, in0=in0, scalar1=s1, scalar2=None, op0=op0), reads=r, writes=w)
    return g.P.op(eng, lambda e: e.tensor_scalar(out=out, in0=in0, scalar1=s1, scalar2=s2, op0=op0, op1=op1), reads=r, writes=w)


def STT(g, out, in0, scalar, in1, op0, op1, r, w):
    return g.P.op("dve", lambda e: e.scalar_tensor_tensor(out=out, in0=in0, scalar=scalar, in1=in1, op0=op0, op1=op1), reads=r, writes=w)


def run_interleaved(gens):
    gens = list(gens)
    while gens:
        for gen in list(gens):
            try:
                next(gen)
            except StopIteration:
                gens.remove(gen)


def next_bank(g):
    i = g.bank_rr % 8
    g.bank_rr += 1
    return g.banks[i], g.bank_b[i]


def pvc(g, name, c):
    j = PV[name] * 8 + c
    return g.pvec[:, j:j + 1]


class StopRwkv(Exception):
    pass


def rwkv(g, tiles=None):
    try:
        _rwkv(g, tiles)
    except StopRwkv:
        pass
    g.P.barrier()


def _rwkv(g, tiles=None):
    stop = getattr(g, "rwkv_stop", 99)
    nc, P, d = g.nc, g.P, g.dram
    RTT = TT[:17] + [(TP, 128), (TP, 128)]
    tiles = list(range(len(RTT))) if tiles is None else tiles
    C0 = float(np.exp(-0.5))
    cb = g.cb
    identb = g.consts_bf[:, C_IDENT:C_IDENT + 128]
    identf = g.consts[:, C_IDENT:C_IDENT + 128]
    onesb = g.consts_bf[:, C_ONES:C_ONES + 128]
    onesf = g.consts[:, C_ONES:C_ONES + 128]
    BDf = g.consts[:, C_BD:C_BD + 128]
    with contextlib.ExitStack() as es:
        def sb(name, shape, dt):
            return es.enter_context(g_sbuf(nc, "rk_" + name, shape, dt))
        wb = Buf("rwkv_w")
        Wr, Wk, Wv, Wo = (sb(nm, [128, NCH, D], BF16) for nm in ("Wr", "Wk", "Wv", "Wo"))
        for i, W in enumerate((Wr, Wk, Wv)):
            P.dma("pool", W[:], d["rwkv_w_rkv"][i].rearrange("(c p) n -> p c n", p=128), writes=[wb])
        P.dma("pool", Wo[:], d["rwkv_w_o"].rearrange("(c p) n -> p c n", p=128), writes=[wb])
        w1 = sb("w1", [128, NCH, 64], BF16)
        a1 = sb("a1", [128, NCH, 64], BF16)
        g1 = sb("g1", [128, NCH, 128], BF16)
        w2 = sb("w2", [64, D], BF16)
        a2 = sb("a2", [64, D], BF16)
        g2 = sb("g2", [128, D], BF16)
        P.dma("pool", w1[:], d["rwkv_w1"].rearrange("(c p) n -> p c n", p=128), writes=[wb])
        P.dma("pool", a1[:], d["rwkv_a1"].rearrange("(c p) n -> p c n", p=128), writes=[wb])
        P.dma("pool", g1[:], d["rwkv_g1"].rearrange("(c p) n -> p c n", p=128), writes=[wb])
        P.dma("pool", w2[:], d["rwkv_w2"], writes=[wb])
        P.dma("pool", a2[:], d["rwkv_a2"], writes=[wb])
        P.dma("pool", g2[:], d["rwkv_g2"], writes=[wb])
        shift_sb = sb("shift", [128, 16], F32)
        P.dma("sp", shift_sb[:], d["shift_in"], writes=[wb])
        omka = sb("omka", [128, 8], F32)
        ka0 = PV["k_a"] * 8
        TSC(g, "dve", omka[:], g.pvec[:, ka0:ka0 + 8], -1.0, 1.0, ALU.mult, ALU.add, [cb], [wb])

        M32 = [sb("M32_%d" % c, [128, 8, 64], F32) for c in range(3)]
        Mb = [sb("Mb_%d" % c, [128, 8, 64], BF16) for c in range(3)]
        M_b = [[Buf("M") for _ in range(8)] for _ in range(3)]
        P.op("pool", lambda e: e.memset(M32[0][:], 0.0), writes=M_b[0])
        P.op("pool", lambda e: e.memset(Mb[0][:], 0.0), writes=M_b[0])
        s0stg = sb("s0stg", [64, 16, 64], F32)
        s0_b = Buf("s0stg")
        for sq_ in range(2):
            P.dma("sp", s0stg[:], d["state_wkv_in"][sq_].rearrange("h v k -> v h k"), writes=[s0_b])
            for p in range(8):
                bk, bb = next_bank(g)
                TRP(g, bk[:, 0:64], s0stg[:, 2 * p:2 * p + 2, :].rearrange("v h k -> v (h k)"), identf[0:64, 0:64], [s0_b, cb], [bb])
                CPY(g, "act", M32[1 + sq_][:, p, :], bk[:, 0:64], [bb], [M_b[1 + sq_][p]])
                CPY(g, "dve", Mb[1 + sq_][:, p, :], bk[:, 0:64], [bb], [M_b[1 + sq_][p]])

        hb1 = sb("hb", [128, NCH, 132], F32)
        hb1_b = Buf("hb")
        hlast = sb("hlast", [128, NCH, 1], F32)
        hlast_b = Buf("hlast")
        dx = sb("dx", [128, NCH, 128], F32)
        dx_b = Buf("dx")
        sqt = sb("sqt", [128, NCH, 128], BF16)
        sq_b = Buf("sq")
        rt = sb("rt", [128, 128], F32)
        rt_b = Buf("rt")
        mixL = sb("mixL", [128, NCH, 128], BF16)
        mixL_b = Buf("mixL")
        mix = {i: sb("mix%d" % i, [128, NCH, 128], BF16) for i in (0, 2, 3)}
        mix_b = {i: Buf("mix") for i in (0, 2, 3)}
        for i in (1, 4, 5):
            mix[i] = mixL
            mix_b[i] = mixL_b
        tw = sb("tw", [64, 128], BF16)
        ta = sb("ta", [64, 128], BF16)
        tg = sb("tg", [128, 128], BF16)
        tw_b, ta_b, tg_b = Buf("tw"), Buf("ta"), Buf("tg")
        ogT = sb("ogT", [128, 8, 128], BF16)
        og_b = [Buf("og") for _ in range(8)]

        def F(name):
            return sb(name, [128, 128], F32), Buf(name)
        sg, sg_b = F("sg")
        av, av_b = F("av")
        gsb, gsb_b = F("gsb")
        vsb, vsb_b = F("vsb")
        kkraw, kkraw_b = F("kkraw")
        kksq, kksq_b = F("kksq")
        nrm, nrm_b = F("nrm")
        kkn, kkn_b = F("kkn")
        t1, t1_b = F("t1")
        kmod, kmod_b = F("kmod")
        bb_, bb_b = F("bb")
        cs, cs_b = F("cs")
        csx, csx_b = F("csx")
        E1, E1_b = F("E1")
        E2, E2_b = F("E2")
        E3, E3_b = F("E3")
        E4, E4_b = F("E4")
        nb = sb("nb", [128, 2], F32)
        nb_b = Buf("nb")
        rkr, rkr_b = F("rkr")
        bonus, bonus_b = F("bonus")
        kr = sb("kr", [128, 2, 128], BF16)
        kr_b = Buf("kr")
        Bh = sb("Bh", [128, 128], BF16)
        Kh = sb("Kh", [128, 128], BF16)
        Bh_b, Kh_b = Buf("Bh"), Buf("Kh")
        src4 = sb("src4", [128, 4, 128], BF16)
        src4_b = Buf("src4")
        tok4 = sb("tok4", [128, 4, 128], BF16)
        tok4_b = Buf("tok4")
        XX = [[sb("XX%d_%d" % (h, i), [128, 2, 128], BF16) for i in range(2)] for h in range(2)]
        XX_b = [[Buf("XX") for _ in range(2)] for _ in range(2)]
        YY = [[sb("YY%d_%d" % (h, i), [128, 128], BF16) for i in range(2)] for h in range(2)]
        YY_b = [[Buf("YY") for _ in range(2)] for _ in range(2)]
        LkTm = [sb("LkTm%d" % h, [128, 128], BF16) for h in range(2)]
        AbTm = [sb("AbTm%d" % h, [128, 128], BF16) for h in range(2)]
        AkTm = [sb("AkTm%d" % h, [128, 128], BF16) for h in range(2)]
        Lm_b = [Buf("Lm") for _ in range(2)]
        Wt = [sb("Wt%d" % h, [128, 64], BF16) for h in range(2)]
        Uv = [sb("Uv%d" % h, [128, 64], BF16) for h in range(2)]
        WU_b = [Buf("WU") for _ in range(2)]
        QtT = sb("QtT", [128, 128], BF16)
        QtT_b = [Buf("QtT") for _ in range(2)]
        Gneg = sb("Gneg", [128, 2, 64], BF16)
        Gneg_b = [Buf("Gneg") for _ in range(2)]
        osb = sb("osb", [128, 2, 128], F32)
        osb_b = Buf("osb")
        gm, gm_b = kkraw, kkraw_b
        gm2, gm2_b = kksq, kksq_b
        gv, gv_b = nrm, nrm_b
        outstg = s0stg
        outstg_b = s0_b
        shstg = sb("shstg", [8, 128], F32)
        shstg_b = Buf("shstg")

        if stop <= 1:
            return
        for ti in tiles:
            col0, n = RTT[ti]
            sample = (ti >= 17)
            segs = [(0, n, ti - 16)] if sample else [(0, n, 0)]
            real0, realn = ((ti - 17) * 64, 64) if sample else (0, n)
            pad0 = (64 - real0) if sample else None
            maxseg = max(s[1] for s in segs)
            L = max(1, int(np.ceil(np.log2(maxseg))))
            mofs = C_M1
            MSN = g.consts[0:n, mofs + 0:mofs + n]
            MSNT = g.consts[0:n, mofs + 128:mofs + 128 + n]
            MS = g.consts[0:n, mofs + 256:mofs + 256 + n]
            MI = g.consts[0:n, mofs + 384:mofs + 384 + n]
            hcur = hb1
            hcur_b = hb1_b
            xb_all = []
            for c in range(NCH):
                xb_all += xt_bufs_for_cols(g, c, col0, n)
            ACTF(g, sqt[:, :, 0:n], g.xT[:, :, col0:col0 + n], AF.Square, xb_all, [sq_b])
            bk, bkb = next_bank(g)
            for c in range(NCH):
                MM(g, bk[:, 0:n], onesb, sqt[:, c, 0:n], [sq_b, cb], [bkb], start=(c == 0), stop=(c == NCH - 1))
            ACTF(g, rt[:, 0:n], bk[:, 0:n], AF.Sqrt, [bkb, cb], [rt_b], bias=g.consts[:, C_EPS_RMS:C_EPS_RMS + 1], scale=1.0 / D)
            P.op("dve", lambda e, n=n: e.reciprocal(out=rt[:, 0:n], in_=rt[:, 0:n]), reads=[rt_b], writes=[rt_b])
            for c in range(NCH):
                STT(g, hcur[:, c, 1:n + 1], g.xT[:, c, col0:col0 + n], pvc(g, "mix_g0", c), rt[:, 0:n], ALU.mult, ALU.mult,
                    xt_bufs_for_cols(g, c, col0, n) + [rt_b, cb], [hcur_b])
            if ti == 0:
                P.op("pool", lambda e, hcur=hcur: e.memset(hcur[:, :, 0:1], 0.0), writes=[hcur_b])
            elif sample:
                CPY(g, "pool", hcur[:, :, 0:1], shift_sb[:, 0:8].rearrange("p (c o) -> p c o", o=1), [wb], [hcur_b])
            else:
                CPY(g, "pool", hcur[:, :, 0:1], hlast[:], [hlast_b], [hcur_b])
            TTO(g, "dve", dx[:, :, 0:n], hcur[:, :, 0:n], hcur[:, :, 1:n + 1], ALU.subtract, [hcur_b], [dx_b])
            if not sample:
                CPY(g, "pool", hlast[:], hcur[:, :, n:n + 1], [hcur_b], [hlast_b])
            if ti == 18:
                TTO(g, "dve", dx[:, :, 64:65], shift_sb[:, 8:16].rearrange("p (c o) -> p c o", o=1), hcur[:, :, 65:66], ALU.subtract,
                    [hcur_b, wb], [dx_b])
            def do_mix(i):
                for c in range(NCH):
                    STT(g, mix[i][:, c, 0:n], dx[:, c, 0:n], pvc(g, "mu%d" % i, c), hcur[:, c, 1:n + 1], ALU.mult, ALU.add,
                        [dx_b, hcur_b, cb], [mix_b[i]])
            for i in (0, 2, 3):
                do_mix(i)
            if ti == 16 or sample:
                for (s0, sn, ch) in segs:
                    bk, bkb = next_bank(g)
                    lc = real0 + realn
                    TRP(g, bk[0:8, 0:128], hcur[:, :, lc:lc + 1].rearrange("p c o -> p (c o)"), identf, [hcur_b, cb], [bkb])
                    CPY(g, "act", shstg[:], bk[0:8, 0:128], [bkb], [shstg_b])
                    dst = d["shift_p"] if ch == 0 else d["shift_s"][ch - 1]
                    P.dma("sp", dst, shstg[:], reads=[shstg_b])
            do_mix(1)
            bk, bkb = next_bank(g)
            for c in range(NCH):
                MM(g, bk[0:64, 0:n], w1[:, c, :], mix[1][:, c, 0:n], [wb, mix_b[1]], [bkb], start=(c == 0), stop=(c == NCH - 1))
            ACTF(g, tw[:, 0:n], bk[0:64, 0:n], AF.Tanh, [bkb], [tw_b])
            do_mix(4)
            bk, bkb = next_bank(g)
            for c in range(NCH):
                MM(g, bk[0:64, 0:n], a1[:, c, :], mix[4][:, c, 0:n], [wb, mix_b[4]], [bkb], start=(c == 0), stop=(c == NCH - 1))
            CPY(g, "act", ta[:, 0:n], bk[0:64, 0:n], [bkb], [ta_b])
            do_mix(5)
            bk, bkb = next_bank(g)
            for c in range(NCH):
                MM(g, bk[:, 0:n], g1[:, c, :], mix[5][:, c, 0:n], [wb, mix_b[5]], [bkb], start=(c == 0), stop=(c == NCH - 1))
            ACTF(g, tg[:, 0:n], bk[:, 0:n], AF.Sigmoid, [bkb], [tg_b])

            if stop <= 2:
                return
            for p in range(8):
                pc = p * 128
                bkA, bkA_b = next_bank(g)
                for j, (W, mi) in enumerate(((Wr, 0), (Wk, 2), (Wv, 3))):
                    for c in range(NCH):
                        MM(g, bkA[:, j * 128:j * 128 + n], W[:, c, pc:pc + 128], mix[mi][:, c, 0:n], [wb, mix_b[mi]], [bkA_b],
                           start=(c == 0), stop=(c == NCH - 1))
                MM(g, bkA[:, 384:384 + n], w2[:, pc:pc + 128], tw[:, 0:n], [wb, tw_b], [bkA_b])
                bkB, bkB_b = next_bank(g)
                MM(g, bkB[:, 0:n], a2[:, pc:pc + 128], ta[:, 0:n], [wb, ta_b], [bkB_b])
                MM(g, bkB[:, 128:128 + n], g2[:, pc:pc + 128], tg[:, 0:n], [wb, tg_b], [bkB_b])
                if stop <= 2.1:
                    return
                r_ps = bkA[:, 0:n]
                k_ps = bkA[:, 128:128 + n]
                v_ps = bkA[:, 256:256 + n]
                ACTF(g, sg[:, 0:n], bkA[:, 384:384 + n], AF.Sigmoid, [bkA_b, cb], [sg_b], bias=pvc(g, "w0", p))
                if sample:
                    P.op("pool", lambda e, pad0=pad0: e.memset(sg[:, pad0:pad0 + 64], 0.0), writes=[sg_b])
                ACTF(g, av[:, 0:n], bkB[:, 0:n], AF.Sigmoid, [bkB_b, cb], [av_b], bias=pvc(g, "a0", p))
                if stop <= 2.15:
                    return
                CPY(g, "act", gsb[:, 0:n], bkB[:, 128:128 + n], [bkB_b], [gsb_b])
                CPY(g, "act", vsb[:, 0:n], v_ps, [bkA_b], [vsb_b])
                if stop <= 2.17:
                    return
                CPY(g, "pool", src4[:, 3, 0:n], vsb[:, 0:n], [vsb_b], [src4_b])
                if stop <= 2.18:
                    return
                ACTF(g, kkraw[:, 0:n], k_ps, AF.Copy, [bkA_b, cb], [kkraw_b], scale=pvc(g, "k_k", p))
                if stop <= 2.2:
                    return
                ACTF(g, kksq[:, 0:n], kkraw[:, 0:n], AF.Square, [kkraw_b], [kksq_b])
                bkC, bkC_b = next_bank(g)
                MM(g, bkC[:, 0:n], BDf, kksq[:, 0:n], [cb, kksq_b], [bkC_b])
                ACTF(g, nrm[:, 0:n], bkC[:, 0:n], AF.Sqrt, [bkC_b], [nrm_b])
                TSC(g, "dve", nrm[:, 0:n], nrm[:, 0:n], 1e-12, None, ALU.max, None, [nrm_b], [nrm_b])
                P.op("dve", lambda e, n=n: e.reciprocal(out=nrm[:, 0:n], in_=nrm[:, 0:n]), reads=[nrm_b], writes=[nrm_b])
                if stop <= 2.4:
                    return
                TTO(g, "dve", kkn[:, 0:n], kkraw[:, 0:n], nrm[:, 0:n], ALU.mult, [kkraw_b, nrm_b], [kkn_b])
                TSC(g, "dve", t1[:, 0:n], av[:, 0:n], pvc(g, "k_a", p), omka[:, p:p + 1], ALU.mult, ALU.add, [av_b, cb, wb], [t1_b])
                TTO(g, "dve", kmod[:, 0:n], k_ps, t1[:, 0:n], ALU.mult, [bkA_b, t1_b], [kmod_b])
                TTO(g, "pool", bb_[:, 0:n], kkn[:, 0:n], av[:, 0:n], ALU.mult, [kkn_b, av_b], [bb_b])
                if sample:
                    P.op("pool", lambda e, pad0=pad0: e.memset(bb_[:, pad0:pad0 + 64], 0.0), writes=[bb_b])
                    P.op("pool", lambda e, pad0=pad0: e.memset(kmod[:, pad0:pad0 + 64], 0.0), writes=[kmod_b])
                for (s0, sn, ch) in segs:
                    P.op("dve", lambda e, s0=s0, sn=sn: e.tensor_tensor_scan(cs[:, s0:s0 + sn], onesf[:, 0:sn], sg[:, s0:s0 + sn], 0.0, ALU.mult, ALU.add),
                         reads=[sg_b, cb], writes=[cs_b])
                TTO(g, "pool", csx[:, 0:n], cs[:, 0:n], sg[:, 0:n], ALU.subtract, [cs_b, sg_b], [csx_b])
                if stop <= 2.5:
                    return
                ACTF(g, E1[:, 0:n], cs[:, 0:n], AF.Exp, [cs_b], [E1_b], scale=-C0)
                ACTF(g, E2[:, 0:n], csx[:, 0:n], AF.Exp, [csx_b], [E2_b], scale=-C0)
                ACTF(g, E3[:, 0:n], cs[:, 0:n], AF.Exp, [cs_b], [E3_b], scale=C0)
                for si, (s0, sn, ch) in enumerate(segs):
                    TSC(g, "dve", nb[:, si:si + 1], cs[:, s0 + sn - 1:s0 + sn], -C0, None, ALU.mult, None, [cs_b], [nb_b])
                    ACTF(g, E4[:, s0:s0 + sn], cs[:, s0:s0 + sn], AF.Exp, [cs_b, nb_b], [E4_b], bias=nb[:, si:si + 1], scale=C0)
                TTO(g, "dve", kr[:, 0, 0:n], kkn[:, 0:n], E2[:, 0:n], ALU.mult, [kkn_b, E2_b], [kr_b])
                TTO(g, "dve", kr[:, 1, 0:n], r_ps, E1[:, 0:n], ALU.mult, [bkA_b, E1_b], [kr_b])
                TTO(g, "pool", Bh[:, 0:n], bb_[:, 0:n], E3[:, 0:n], ALU.mult, [bb_b, E3_b], [Bh_b])
                TTO(g, "pool", Kh[:, 0:n], kmod[:, 0:n], E3[:, 0:n], ALU.mult, [kmod_b, E3_b], [Kh_b])
                TTO(g, "pool", src4[:, 1, 0:n], bb_[:, 0:n], E4[:, 0:n], ALU.mult, [bb_b, E4_b], [src4_b])
                TTO(g, "pool", src4[:, 2, 0:n], kmod[:, 0:n], E4[:, 0:n], ALU.mult, [kmod_b, E4_b], [src4_b])
                STT(g, rkr[:, 0:n], r_ps, pvc(g, "r_k", p), kmod[:, 0:n], ALU.mult, ALU.mult, [bkA_b, kmod_b, cb], [rkr_b])
                MM(g, bkC[:, 128:128 + n], BDf, rkr[:, 0:n], [cb, rkr_b], [bkC_b])
                TTO(g, "dve", bonus[:, 0:n], bkC[:, 128:128 + n], vsb[:, 0:n], ALU.mult, [bkC_b, vsb_b], [bonus_b])
                if stop <= 2.6:
                    return
                bkT, bkT_b = next_bank(g)
                bkT16 = bkT[:].bitcast(BF16)
                TRP(g, bkT16[0:n, 0:128], kr[:, 0, 0:n], identb, [kr_b, cb], [bkT_b])
                for j in (1, 2, 3):
                    TRP(g, bkT16[0:n, j * 128:(j + 1) * 128], src4[:, j, 0:n], identb, [src4_b, cb], [bkT_b])
                CPY(g, "act", tok4[0:n, :, :].rearrange("p a b -> p (a b)"), bkT16[0:n, 0:512], [bkT_b], [tok4_b])

                if stop <= 3:
                    return
                def head_gen(hh, p=p, n=n, segs=segs, L=L, MSN=MSN, MSNT=MSNT, MS=MS, MI=MI):
                    h0 = hh * 64
                    hs = slice(h0, h0 + 64)
                    bkU, bkU_b = next_bank(g)
                    bkW, bkW_b = next_bank(g)
                    krh = kr[hs, :, 0:n]
                    for a_ in range(2):
                        MM(g, bkU[0:n, a_ * 128:a_ * 128 + n], Bh[hs, 0:n], kr[hs, a_, 0:n], [Bh_b, kr_b], [bkU_b])
                        MM(g, bkU[0:n, 256 + a_ * 128:256 + a_ * 128 + n], Kh[hs, 0:n], kr[hs, a_, 0:n], [Kh_b, kr_b], [bkU_b])
                    MM(g, bkW[0:n, 0:n], kr[hs, 0, 0:n], Bh[hs, 0:n], [Bh_b, kr_b], [bkW_b])
                    X0 = XX[hh][0]
                    TTO(g, "dve", X0[0:n, 1, 0:n], bkU[0:n, 0:n], MSN, ALU.mult, [bkU_b, cb], [XX_b[hh][0]])
                    TTO(g, "dve", X0[0:n, 0, 0:n], bkW[0:n, 0:n], MSNT, ALU.mult, [bkW_b, cb], [XX_b[hh][0]])
                    TTO(g, "dve", AbTm[hh][0:n, 0:n], bkU[0:n, 128:128 + n], MI, ALU.mult, [bkU_b, cb], [Lm_b[hh]])
                    TTO(g, "dve", LkTm[hh][0:n, 0:n], bkU[0:n, 256:256 + n], MS, ALU.mult, [bkU_b, cb], [Lm_b[hh]])
                    TTO(g, "dve", AkTm[hh][0:n, 0:n], bkU[0:n, 384:384 + n], MI, ALU.mult, [bkU_b, cb], [Lm_b[hh]])
                    Vh = tok4[0:n, 3, h0:h0 + 64]
                    Bch = tok4[0:n, 1, h0:h0 + 64]
                    Kch = tok4[0:n, 2, h0:h0 + 64]
                    MM(g, bkW[0:n, 128:192], LkTm[hh][0:n, 0:n], Vh, [Lm_b[hh], tok4_b], [bkW_b])
                    Y = YY[hh][0]
                    CPY(g, "pool", Y[0:n, 0:64], tok4[0:n, 0, h0:h0 + 64], [tok4_b], [YY_b[hh][0]])
                    CPY(g, "act", Y[0:n, 64:128], bkW[0:n, 128:192], [bkW_b], [YY_b[hh][0]])
                    for lv in range(L):
                        yield
                        cur, nxt = lv % 2, (lv + 1) % 2
                        Xc = XX[hh][cur]
                        bkY, bkY_b = next_bank(g)
                        MM(g, bkY[0:n, 0:128], Xc[0:n, 1, 0:n], YY[hh][cur][0:n, :], [XX_b[hh][cur], YY_b[hh][cur]], [bkY_b], start=True, stop=False)
                        MM(g, bkY[0:n, 0:128], identb[0:n, 0:n], YY[hh][cur][0:n, :], [cb, YY_b[hh][cur]], [bkY_b], start=False, stop=True)
                        if lv < L - 1:
                            bkX, bkX_b = next_bank(g)
                            MM(g, bkX[0:n, 0:n], Xc[0:n, 1, 0:n], Xc[0:n, 0, 0:n], [XX_b[hh][cur]], [bkX_b])
                            MM(g, bkX[0:n, 128:128 + n], Xc[0:n, 0, 0:n], Xc[0:n, 1, 0:n], [XX_b[hh][cur]], [bkX_b])
                            CPY(g, "act", YY[hh][nxt][0:n, :], bkY[0:n, 0:128], [bkY_b], [YY_b[hh][nxt]])
                            CPY(g, "dve", XX[hh][nxt][0:n, :, 0:n], bkX[0:n, 0:256].rearrange("p (a b) -> p a b", a=2)[:, :, 0:n], [bkX_b], [XX_b[hh][nxt]])
                        else:
                            CPY(g, "act", Wt[hh][0:n, :], bkY[0:n, 0:64], [bkY_b], [WU_b[hh]])
                            P.op("act", lambda e, hh=hh, n=n, bkY=bkY: e.mul(Uv[hh][0:n, :], bkY[0:n, 64:128], -1.0), reads=[bkY_b], writes=[WU_b[hh]])
                    yield
                    if stop <= 4:
                        return
                    bkQ, bkQ_b = next_bank(g)
                    MM(g, bkQ[hs, 0:n], Wt[hh][0:n, :], AbTm[hh][0:n, 0:n], [WU_b[hh], Lm_b[hh]], [bkQ_b])
                    TTO(g, "dve", QtT[hs, 0:n], kr[hs, 1, 0:n], bkQ[hs, 0:n], ALU.subtract, [kr_b, bkQ_b], [QtT_b[hh]])
                    for si, (s0, sn, ch) in enumerate(segs):
                        MM(g, bkQ[hs, 128 + si * 64:192 + si * 64], Wt[hh][s0:s0 + sn, :], tok4[s0:s0 + sn, 1, h0:h0 + 64], [WU_b[hh], tok4_b], [bkQ_b])
                    nsg = len(segs)
                    P.op("act", lambda e, hs=hs, nsg=nsg, bkQ=bkQ: e.mul(Gneg[hs, 0:nsg, :].rearrange("p a b -> p (a b)"), bkQ[hs, 128:128 + 64 * nsg], -1.0),
                         reads=[bkQ_b], writes=[Gneg_b[hh]])
                    yield
                    if stop <= 5:
                        return
                    bkO, bkO_b = next_bank(g)
                    MM(g, bkO[hs, 0:n], Uv[hh][0:n, :], AbTm[hh][0:n, 0:n], [WU_b[hh], Lm_b[hh]], [bkO_b], start=True, stop=False)
                    MM(g, bkO[hs, 0:n], Vh, AkTm[hh][0:n, 0:n], [tok4_b, Lm_b[hh]], [bkO_b], start=False, stop=False)
                    for si, (s0, sn, ch) in enumerate(segs):
                        MM(g, bkO[hs, s0:s0 + sn], Mb[ch][hs, p, :], QtT[hs, s0:s0 + sn], [M_b[ch][p], QtT_b[hh]], [bkO_b],
                           start=False, stop=(si == len(segs) - 1))
                    CPY(g, "act", osb[hs, 0, 0:n], bkO[hs, 0:n], [bkO_b], [osb_b])
                    bkS, bkS_b = next_bank(g)
                    for si, (s0, sn, ch) in enumerate(segs):
                        so = bkS[hs, si * 64:si * 64 + 64]
                        MM(g, so, Gneg[hs, si, :], Mb[ch][hs, p, :], [Gneg_b[hh], M_b[ch][p]], [bkS_b], start=True, stop=False)
                        MM(g, so, tok4[s0:s0 + sn, 1, h0:h0 + 64], Uv[hh][s0:s0 + sn, :], [tok4_b, WU_b[hh]], [bkS_b], start=False, stop=False)
                        MM(g, so, tok4[s0:s0 + sn, 2, h0:h0 + 64], tok4[s0:s0 + sn, 3, h0:h0 + 64], [tok4_b], [bkS_b], start=False, stop=True)
                    for si, (s0, sn, ch) in enumerate(segs):
                        STT(g, M32[ch][hs, p, :], M32[ch][hs, p, :], E1[hs, s0 + sn - 1:s0 + sn], bkS[hs, si * 64:si * 64 + 64], ALU.mult, ALU.add,
                            [M_b[ch][p], E1_b, bkS_b], [M_b[ch][p]])
                        CPY(g, "act", Mb[ch][hs, p, :], M32[ch][hs, p, :], [M_b[ch][p]], [M_b[ch][p]])
                run_interleaved([head_gen(0), head_gen(1)])
                if stop <= 6:
                    return
                ACTF(g, osb[:, 1, 0:n], osb[:, 0, 0:n], AF.Square, [osb_b], [osb_b])
                bkG, bkG_b = next_bank(g)
                for a_ in range(2):
                    MM(g, bkG[:, a_ * 128:a_ * 128 + n], BDf, osb[:, a_, 0:n], [cb, osb_b], [bkG_b])
                P.op("act", lambda e, n=n, bkG=bkG: e.mul(gm[:, 0:n], bkG[:, 0:n], 1.0 / 64), reads=[bkG_b], writes=[gm_b])
                TTO(g, "pool", gm2[:, 0:n], gm[:, 0:n], gm[:, 0:n], ALU.mult, [gm_b], [gm2_b])
                STT(g, gv[:, 0:n], bkG[:, 128:128 + n], 1.0 / 64, gm2[:, 0:n], ALU.mult, ALU.subtract, [bkG_b, gm2_b], [gv_b])
                ACTF(g, gv[:, 0:n], gv[:, 0:n], AF.Sqrt, [gv_b, cb], [gv_b], bias=g.consts[:, C_EPS_GN:C_EPS_GN + 1])
                P.op("dve", lambda e, n=n: e.reciprocal(out=gv[:, 0:n], in_=gv[:, 0:n]), reads=[gv_b], writes=[gv_b])
                TTO(g, "dve", gm2[:, 0:n], osb[:, 0, 0:n], gm[:, 0:n], ALU.subtract, [osb_b, gm_b, gm2_b], [gm2_b])
                TTO(g, "dve", gm2[:, 0:n], gm2[:, 0:n], gv[:, 0:n], ALU.mult, [gm2_b, gv_b], [gm2_b])
                TSC(g, "dve", gm2[:, 0:n], gm2[:, 0:n], pvc(g, "gn_w", p), pvc(g, "gn_b", p), ALU.mult, ALU.add, [gm2_b, cb], [gm2_b])
                TTO(g, "dve", gm2[:, 0:n], gm2[:, 0:n], bonus[:, 0:n], ALU.add, [gm2_b, bonus_b], [gm2_b])
                TTO(g, "dve", ogT[:, p, 0:n], gm2[:, 0:n], gsb[:, 0:n], ALU.mult, [gm2_b, gsb_b], [og_b[p]])
            for dc in range(NCH):
                bk, bkb = next_bank(g)
                for p in range(8):
                    MM(g, bk[:, 0:n], Wo[:, p, dc * 128:(dc + 1) * 128], ogT[:, p, 0:n], [wb, og_b[p]], [bkb], start=(p == 0), stop=(p == 7))
                xb = xt_bufs_for_cols(g, dc, col0, n)
                ca, cn_ = col0 + real0, realn
                TTO(g, "dve", g.xT[:, dc, ca:ca + cn_], g.xT[:, dc, ca:ca + cn_], bk[:, real0:real0 + realn], ALU.add, xb + [bkb], xb)
            if ti == 16 or sample:
                for (s0, sn, ch) in segs:
                    for p in range(8):
                        bk, bkb = next_bank(g)
                        TRP(g, bk[0:64, 0:128], M32[ch][:, p, :], identf, [M_b[ch][p], cb], [bkb])
                        CPY(g, "act", outstg[:, 2 * p:2 * p + 2, :].rearrange("v h k -> v (h k)"), bk[0:64, 0:128], [bkb], [outstg_b])
                    dst = d["state_wkv_p"] if ch == 0 else d["state_wkv_s"][ch - 1]
                    P.dma("sp", dst.rearrange("h v k -> v h k"), outstg[:], reads=[outstg_b])
        P.barrier()


def sb_attn(g, do_sample=True, pairs=None):
    _sb_attn(g, do_sample, pairs)
    g.P.barrier()


def _sb_attn(g, do_sample=True, pairs=None):
    nc, P, d = g.nc, g.P, g.dram
    stop = getattr(g, "sb_stop", 99)
    cb = g.cb
    pairs = list(range(8)) if pairs is None else pairs
    identb = g.consts_bf[:, C_IDENT:C_IDENT + 128]
    TINCL = g.consts_bf[:, C_TRI_INCL:C_TRI_INCL + 128]
    TCOMP = g.consts_bf[:, C_TRI_COMP:C_TRI_COMP + 128]
    MS1 = g.consts[:, C_M1 + 256:C_M1 + 384]
    MS2 = g.consts[:, C_M2 + 256:C_M2 + 384]
    with contextlib.ExitStack() as es:
        def sb(name, shape, dt):
            return es.enter_context(g_sbuf(nc, "sb_" + name, shape, dt))
        xn, xn_b = rmsnorm_T(g, es, "mix_g1")
        gqk = sb("gqk", [128, 256], F32)
        gqk_b = Buf("gqk")
        P.dma("sp", gqk[:], d["gqk"], writes=[gqk_b])
        wq = [sb("wq%d" % i, [128, NCH, 3, 128], BF16) for i in range(2)]
        wq_b = [Buf("wq") for _ in range(2)]
        wo = [sb("wo%d" % i, [128, D], BF16) for i in range(2)]
        wo_b = [Buf("wo") for _ in range(2)]
        QT = sb("QT", [128, NTOK], BF16)
        KT = sb("KT", [128, NTOK], BF16)
        Vt = sb("Vt", [128, 18, 128], BF16)
        oT = sb("oT", [128, NTOK], BF16)
        QT_b, KT_b, Vt_b = Buf("QT"), Buf("KT"), Buf("Vt")
        oT_b = [Buf("oT") for _ in range(NCT)]
        qksb = sb("qksb", [128, 256], F32)
        qksb_b = Buf("qksb")
        sqf = sb("sqf", [128, 256], F32)
        sqf_b = Buf("sqf")
        ss = sb("ss", [128, 4], F32)
        ss_b = Buf("ss")
        tq = sb("tq", [128, 256], F32)
        tq_b = Buf("tq")
        qkn = [sb("qkn%d" % i, [128, 256], F32) for i in range(2)]
        qkn_b = [Buf("qkn") for _ in range(2)]
        qkb = sb("qkb", [128, 256], BF16)
        qkb_b = Buf("qkb")
        P.op("pool", lambda e: e.memset(qkb[:], 0.0), writes=[qkb_b])
        vf = [sb("vf%d" % i, [128, 128], F32) for i in range(2)]
        vf_b = [Buf("vf") for _ in range(2)]
        ef = [sb("ef%d" % i, [128, 512], F32) for i in range(2)]
        ef_b = [Buf("ef") for _ in range(2)]
        spf = [sb("spf%d" % i, [128, 512], F32) for i in range(2)]
        spf_b = [Buf("spf") for _ in range(2)]
        Xf = [sb("Xf%d" % i, [128, 512], F32) for i in range(2)]
        Xf_b = [Buf("Xf") for _ in range(2)]
        Ab = [sb("Ab%d" % i, [128, 512], BF16) for i in range(2)]
        Ab_b = [Buf("Ab") for _ in range(2)]
        if do_sample:
            Kp = sb("Kp", [128, 32, 2, 64], BF16)
            Vp = sb("Vp", [128, 32, 2, 64], BF16)
            KTp = sb("KTp", [128, PAST], BF16)
            Kp_b, Vp_b, KTp_b = Buf("Kp"), Buf("Vp"), Buf("KTp")
        w_qkv = d["sb_w_qkv"].rearrange("(c p) (t n) -> p c t n", p=128, t=3)

        def load_w(p):
            s = p % 2
            pc = p * 128
            for t_ in range(3):
                P.dma("pool", wq[s][:, :, t_, :], w_qkv[:, :, t_, pc:pc + 128], writes=[wq_b[s]])
            P.dma("pool", wo[s][:], d["sb_w_o"][pc:pc + 128, :], writes=[wo_b[s]])

        stepk = [0]

        reserved = set()

        def free_bank():
            bk_, bb_ = next_bank(g)
            while id(bb_) in reserved:
                bk_, bb_ = next_bank(g)
            return bk_, bb_

        shi = [sb("shi%d" % i, [128, 512], BF16) for i in range(2)]
        slo = [sb("slo%d" % i, [128, 512], BF16) for i in range(2)]
        NSL = 8
        his_b = [[Buf("hi") for _ in range(8)] for _ in range(2)]
        los_b = [[Buf("lo") for _ in range(8)] for _ in range(2)]
        efs_b = [[Buf("ef") for _ in range(NSL)] for _ in range(2)]
        sps_b = [[Buf("sp") for _ in range(NSL)] for _ in range(2)]
        Xs_b = [[Buf("X") for _ in range(NSL)] for _ in range(2)]
        As_b = [[Buf("A") for _ in range(NSL)] for _ in range(2)]

        def attn_steps(hh, qsrc_cols, nq, steps, out_dst, out_bufs, nslots=1):
            h0 = hh * 64
            hs = slice(h0, h0 + 64)
            accbk, accb = free_bank()
            reserved.add(id(accb))
            obk, obb = free_bank()
            reserved.add(id(obb))
            i = hh
            for si, (kT_ap, v_ap, nk, lo, mask_ap, rds) in enumerate(steps):
                sl = si % nslots
                so = sl * 64 if nslots > 1 else 0
                e_b, s_b, x_b, a_b = efs_b[i][sl], sps_b[i][sl], Xs_b[i][sl], As_b[i][sl]
                zbk, zbb = free_bank()
                MM(g, zbk[0:nk, lo:nq], kT_ap, QT[hs, qsrc_cols + lo:qsrc_cols + nq], rds + [QT_b], [zbb])
                ACTF(g, ef[i][0:nk, so + lo:so + nq], zbk[0:nk, lo:nq], AF.Exp, [zbb], [e_b], scale=0.125)
                if mask_ap is not None:
                    mw = mask_ap.shape[1]
                    TTO(g, "dve", ef[i][0:nk, so + lo:so + lo + mw], ef[i][0:nk, so + lo:so + lo + mw], mask_ap, ALU.mult, [e_b, cb], [e_b])
                ACTF(g, spf[i][0:nk, so + lo:so + nq], ef[i][0:nk, so + lo:so + nq], AF.Ln, [e_b], [s_b], bias=g.consts[0:nk, C_ONE:C_ONE + 1])
                h_b, l_b = his_b[i][sl], los_b[i][sl]
                hi_ap = shi[i][0:nk, so + lo:so + nq]
                lo_ap = slo[i][0:nk, so + lo:so + nq]
                CPY(g, "act", hi_ap, spf[i][0:nk, so + lo:so + nq], [s_b], [h_b])
                TTO(g, "pool", lo_ap, spf[i][0:nk, so + lo:so + nq], hi_ap, ALU.subtract, [s_b, h_b], [l_b])
                MM(g, accbk[:, lo:nq], TINCL[0:nk, :], hi_ap, [cb, h_b], [accb], start=(si == 0), stop=True, skip=True)
                MM(g, accbk[:, lo:nq], TINCL[0:nk, :], lo_ap, [cb, l_b], [accb], start=False, stop=True, skip=True)
                ACTF(g, Xf[i][0:nk, so + lo:so + nq], accbk[0:nk, lo:nq], AF.Exp, [accb], [x_b], scale=-1.0)
                MM(g, accbk[:, lo:nq], TCOMP[0:nk, :], hi_ap, [cb, h_b], [accb], start=False, stop=True, skip=True)
                MM(g, accbk[:, lo:nq], TCOMP[0:nk, :], lo_ap, [cb, l_b], [accb], start=False, stop=True, skip=True)
                TTO(g, "dve", Ab[i][0:nk, so + lo:so + nq], ef[i][0:nk, so + lo:so + nq], Xf[i][0:nk, so + lo:so + nq], ALU.mult, [e_b, x_b], [a_b])
                MM(g, obk[hs, lo:nq], v_ap, Ab[i][0:nk, so + lo:so + nq], rds + [a_b], [obb], start=(si == 0), stop=True, skip=True)
                yield
            CPY(g, "act", out_dst, obk[hs, 0:nq], [obb], out_bufs)
            reserved.discard(id(accb))
            reserved.discard(id(obb))

        load_w(pairs[0])
        for pi, p in enumerate(pairs):
            s = p % 2
            if pi + 1 < len(pairs):
                load_w(pairs[pi + 1])
            for ti, (col0, n) in enumerate(TT):
                bk, bkb = next_bank(g)
                xb = []
                for c in range(NCH):
                    xb += [xn_b[c][t] for t, (c0, nn) in enumerate(COLT) if c0 < col0 + n and col0 < c0 + nn]
                for c in range(NCH):
                    MM(g, bk[0:n, 0:384], xn[:, c, col0:col0 + n], wq[s][:, c, :, :].rearrange("p t n -> p (t n)"), xb + [wq_b[s]], [bkb],
                       start=(c == 0), stop=(c == NCH - 1))
                if stop <= 1:
                    return
                CPY(g, "act", qksb[0:n, :], bk[0:n, 0:256], [bkb], [qksb_b])
                ACTF(g, sqf[0:n, :], qksb[0:n, :], AF.Square, [qksb_b], [sqf_b])
                P.op("dve", lambda e, n=n: e.tensor_reduce(out=ss[0:n, :], in_=sqf[0:n, :].rearrange("p (a b) -> p a b", a=4), axis=AX.X, op=ALU.add),
                     reads=[sqf_b], writes=[ss_b])
                ACTF(g, ss[0:n, :], ss[0:n, :], AF.Sqrt, [ss_b, cb], [ss_b], bias=g.consts[0:n, C_EPS_RMS:C_EPS_RMS + 1], scale=1.0 / 64)
                P.op("dve", lambda e, n=n: e.reciprocal(out=ss[0:n, :], in_=ss[0:n, :]), reads=[ss_b], writes=[ss_b])
                if stop <= 2:
                    return
                for a_ in range(4):
                    STT(g, tq[0:n, a_ * 64:(a_ + 1) * 64], qksb[0:n, a_ * 64:(a_ + 1) * 64], ss[0:n, a_:a_ + 1], gqk[0:n, a_ * 64:(a_ + 1) * 64],
                        ALU.mult, ALU.mult, [qksb_b, ss_b, gqk_b], [tq_b])
                if stop <= 3:
                    return
                j = ti % 2
                CPY(g, "pool", qkn[j][0:n, :], tq[0:n, :], [tq_b], [qkn_b[j]])
                CPY(g, "act", qkb[0:n, :], tq[0:n, :], [tq_b], [qkb_b])
                CPY(g, "act", vf[j][0:n, :], bk[0:n, 256:384], [bkb], [vf_b[j]])
                CPY(g, "pool", Vt[0:n, ti, :], vf[j][0:n, :], [vf_b[j]], [Vt_b])
                if stop <= 4:
                    return
                bkT, bkT_b = next_bank(g)
                bkT16 = bkT[:].bitcast(BF16)
                TRP(g, bkT16[:, 0:128], qkb[:, 0:128], identb, [qkb_b, cb], [bkT_b])
                TRP(g, bkT16[:, 128:256], qkb[:, 128:256], identb, [qkb_b, cb], [bkT_b])
                CPY(g, "act", QT[:, col0:col0 + n], bkT16[:, 0:n], [bkT_b], [QT_b])
                CPY(g, "act", KT[:, col0:col0 + n], bkT16[:, 128:128 + n], [bkT_b], [KT_b])
                if stop <= 5:
                    return
                if ti <= 16:
                    kd = d["cache_k_p"][2 * p:2 * p + 2, col0:col0 + n, :].rearrange("h t d -> t h d")
                    vd = d["cache_v_p"][2 * p:2 * p + 2, col0:col0 + n, :].rearrange("h t d -> t h d")
                    P.dma("sp", kd, qkn[j][0:n, 128:256].rearrange("p (h d) -> p h d", h=2), reads=[qkn_b[j]])
                    P.dma("sp", vd, vf[j][0:n, :].rearrange("p (h d) -> p h d", h=2), reads=[vf_b[j]])
                else:
                    for sq_ in range(2):
                        r0 = sq_ * 64
                        kd = d["cache_k_s"][sq_, 2 * p:2 * p + 2, :, :].rearrange("h t d -> t h d")
                        vd = d["cache_v_s"][sq_, 2 * p:2 * p + 2, :, :].rearrange("h t d -> t h d")
                        P.dma("sp", kd, qkn[j][r0:r0 + 64, 128:256].rearrange("p (h d) -> p h d", h=2), reads=[qkn_b[j]])
                        P.dma("sp", vd, vf[j][r0:r0 + 64, :].rearrange("p (h d) -> p h d", h=2), reads=[vf_b[j]])
                if stop <= 5.5 or (stop <= 5.7 and ti == 1):
                    return
            if stop <= 6:
                return
            if not do_sample:
                P.op("pool", lambda e: e.memset(oT[:, TP:NTOK], 0.0), writes=[oT_b[NCT - 1]])
            chunks = [(0, 0, 0)] + [(1 + 4 * i, 4 + 4 * i, 1) for i in range(4)]
            for (t_a, t_b, _) in chunks:
                gens = []
                for hh in range(2):
                    h0 = hh * 64
                    hs = slice(h0, h0 + 64)
                    qc0 = TT[t_a][0]
                    nq = TT[t_b][0] + TT[t_b][1] - qc0
                    steps = []
                    for kb in range(t_b, -1, -1):
                        k0, nk = TT[kb]
                        if kb >= t_a:
                            lo = k0 - qc0
                            mask = MS1[0:nk, 0:nk]
                        else:
                            lo = 0
                            mask = None
                        steps.append((KT[hs, k0:k0 + nk], Vt[0:nk, kb, h0:h0 + 64], nk, lo, mask, [KT_b, Vt_b]))
                    ob_ = [oT_b[t] for t, (c0, nn) in enumerate(COLT) if c0 < qc0 + nq and qc0 < c0 + nn]
                    gens.append(attn_steps(hh, qc0, nq, steps, oT[hs, qc0:qc0 + nq], ob_))
                run_interleaved(gens)
                if stop <= 7:
                    return
            if do_sample:
                for sq_ in range(2):
                    for h_ in range(2):
                        P.dma("pool", Kp[:, :, h_, :], d["cache_k_in"][sq_, 2 * p + h_, :, :].rearrange("(t k) d -> k t d", k=128), writes=[Kp_b])
                        P.dma("pool", Vp[:, :, h_, :], d["cache_v_in"][sq_, 2 * p + h_, :, :].rearrange("(t k) d -> k t d", k=128), writes=[Vp_b])
                    for t8 in range(4):
                        bkT, bkT_b = next_bank(g)
                        bkT16 = bkT[:].bitcast(BF16)
                        for j in range(8):
                            t = t8 * 8 + j
                            TRP(g, bkT16[:, j * 128:(j + 1) * 128], Kp[:, t, :, :].rearrange("p h d -> p (h d)"), identb, [Kp_b, cb], [bkT_b])
                        CPY(g, "act", KTp[:, t8 * 1024:(t8 + 1) * 1024], bkT16[:, 0:1024], [bkT_b], [KTp_b])
                    gens = []
                    for hh in range(2):
                        h0 = hh * 64
                        hs = slice(h0, h0 + 64)
                        qc0 = TP + sq_ * 64
                        steps = [(KT[hs, TP:TP + 128], Vt[:, 17, h0:h0 + 64], 128, 0, MS2[:, sq_ * 64:sq_ * 64 + 64], [KT_b, Vt_b])]
                        for t in range(31, -1, -1):
                            steps.append((KTp[hs, t * 128:(t + 1) * 128], Vp[:, t, hh, :], 128, 0, None, [KTp_b, Vp_b]))
                        gens.append(attn_steps(hh, qc0, 64, steps, oT[hs, qc0:qc0 + 64], [oT_b[NCT - 1]], nslots=NSL))
                    run_interleaved(gens)
            for dc in range(NCH):
                for t, (c0, n) in enumerate(COLT):
                    bk, bkb = next_bank(g)
                    MM(g, bk[:, 0:n], wo[s][:, dc * 128:(dc + 1) * 128], oT[:, c0:c0 + n], [wo_b[s], oT_b[t]], [bkb])
                    TTO(g, "dve", g.xT[:, dc, c0:c0 + n], g.xT[:, dc, c0:c0 + n], bk[:, 0:n], ALU.add, [g.xT_b[dc][t], bkb], [g.xT_b[dc][t]])
    P.barrier()


def make_consts():
    c = np.zeros((128, NCONST), np.float32)
    c[:, C_IDENT:C_IDENT + 128] = np.eye(128, dtype=np.float32)
    c[:, C_ONES:C_ONES + 128] = 1.0
    kp = np.arange(128)[:, None]
    k = np.arange(128)[None, :]
    c[:, C_TRI_INCL:C_TRI_INCL + 128] = (kp >= k).astype(np.float32)
    c[:, C_TRI_COMP:C_TRI_COMP + 128] = (kp < k).astype(np.float32)
    c[:, C_EPS_RMS] = RMS_EPS
    c[:, C_ONE] = 1.0
    pp = np.arange(128)
    c[:, C_BD:C_BD + 128] = (pp[:, None] // 64 == pp[None, :] // 64).astype(np.float32)
    for ofs, seg in ((C_M1, np.zeros(128, int)), (C_M2, pp // 64)):
        same = (seg[:, None] == seg[None, :])
        s_lt_t = ((pp[:, None] < pp[None, :]) & same).astype(np.float32)
        s_le_t = ((pp[:, None] <= pp[None, :]) & same).astype(np.float32)
        c[:, ofs:ofs + 128] = -s_lt_t
        c[:, ofs + 128:ofs + 256] = -s_lt_t.T
        c[:, ofs + 256:ofs + 384] = s_lt_t
        c[:, ofs + 384:ofs + 512] = s_le_t
    c[:, C_EPS_GN] = GN_EPS
    return c


def fm(vec):
    return np.ascontiguousarray(np.asarray(vec, np.float32).reshape(NCH, 128).T)


def make_pvec(inp):
    cols = [fm(inp["ffn_norm_g"][0, 0]), fm(inp["ffn_norm_g"][0, 1]), fm(inp["ffn_norm_g"][1, 0]), fm(inp["ffn_norm_g"][1, 1]),
            fm(inp["mix_norm_g"][0]), fm(inp["mix_norm_g"][1])]
    for i in range(6):
        cols.append(fm(inp["rwkv_mu"][i]))
    for nm in ("rwkv_w0", "rwkv_a0", "rwkv_k_k", "rwkv_k_a", "rwkv_r_k", "rwkv_gn_w", "rwkv_gn_b"):
        cols.append(fm(np.asarray(inp[nm]).reshape(-1)))
    return np.ascontiguousarray(np.concatenate(cols, axis=1))


ALL_STAGES = {("ffn", 0, 0), ("ffn", 0, 1), ("ffn", 1, 0), ("ffn", 1, 1), "rwkv", "sb"}
_NC_CACHE = {}


def make_in_maps(inputs, ncores=8):
    consts = make_consts()
    pvec = make_pvec(inputs)
    f = lambda a: np.ascontiguousarray(np.asarray(a, np.float32))
    in_maps = []
    gq = np.asarray(inputs["sb_q_norm_g"], np.float32)
    gk = np.asarray(inputs["sb_k_norm_g"], np.float32)
    gqk = np.ascontiguousarray(np.broadcast_to(np.concatenate([gq, gq, gk, gk])[None, :], (128, 256)))
    for i in range(ncores):
        in_maps.append({
            "x_prompt": f(inputs["x_prompt"][i]),
            "x_sample": f(inputs["x_sample"][2 * i:2 * i + 2]).reshape(2 * DSEQ, D),
            "meta_tokens": f(inputs["meta_tokens"]),
            "consts": consts,
            "pvec": pvec,
            "ffn_w_in": f(inputs["ffn_w_in"]),
            "ffn_w_out": f(inputs["ffn_w_out"]),
            "rwkv_w_rkv": f(inputs["rwkv_w_rkv"]), "rwkv_w_o": f(inputs["rwkv_w_o"]),
            "rwkv_w1": f(inputs["rwkv_w1"]), "rwkv_w2": f(inputs["rwkv_w2"]),
            "rwkv_a1": f(inputs["rwkv_a1"]), "rwkv_a2": f(inputs["rwkv_a2"]),
            "rwkv_g1": f(inputs["rwkv_g1"]), "rwkv_g2": f(inputs["rwkv_g2"]),
            "shift_in": np.ascontiguousarray(np.concatenate([fm(inputs["state_rwkv_shift"][2 * i]), fm(inputs["state_rwkv_shift"][2 * i + 1])], 1)),
            "state_wkv_in": f(inputs["state_rwkv_wkv"][2 * i:2 * i + 2]),
            "sb_w_qkv": f(inputs["sb_w_qkv"]), "sb_w_o": f(inputs["sb_w_o"]),
            "gqk": gqk,
        })
        if "cache_sb_k" in inputs:
            in_maps[-1]["cache_k_in"] = f(inputs["cache_sb_k"][2 * i:2 * i + 2])
            in_maps[-1]["cache_v_in"] = f(inputs["cache_sb_v"][2 * i:2 * i + 2])
    return in_maps


def run(inputs, stages=None, ncores=8, trace=False):
    stages = ALL_STAGES if stages is None else stages
    key = tuple(sorted(stages, key=repr))
    if key not in _NC_CACHE:
        _NC_CACHE[key] = build(stages)
    nc = _NC_CACHE[key]
    in_maps = make_in_maps(inputs, ncores)
    res = run_bass_kernel_spmd(nc, in_maps, core_ids=list(range(ncores)), trace=trace)
    if trace:
        print('exec_time_ns', res.exec_time_ns)
    return res.results


def kernel(**inputs):
    r = run(inputs)
    f32 = np.float32
    y_prompt = np.stack([r[i]["y_prompt"] for i in range(8)], 0).astype(f32)
    y_sample = np.concatenate([r[i]["y_sample"].reshape(2, DSEQ, D) for i in range(8)], 0).astype(f32)
    S_p = np.stack([r[i]["state_wkv_p"] for i in range(8)], 0).astype(f32)
    sh_p = np.stack([r[i]["shift_p"].reshape(D) for i in range(8)], 0).astype(f32)
    k_p = np.stack([r[i]["cache_k_p"] for i in range(8)], 0).astype(f32)
    v_p = np.stack([r[i]["cache_v_p"] for i in range(8)], 0).astype(f32)
    S_s = np.concatenate([r[i]["state_wkv_s"] for i in range(8)], 0).astype(f32)
    sh_s = np.concatenate([r[i]["shift_s"].reshape(2, D) for i in range(8)], 0).astype(f32)
    k_s = np.concatenate([r[i]["cache_k_s"] for i in range(8)], 0).astype(f32)
    v_s = np.concatenate([r[i]["cache_v_s"] for i in range(8)], 0).astype(f32)
    return (y_prompt, y_sample, S_p, sh_p, k_p, v_p, S_s, sh_s, k_s, v_s)
```

```python
import contextlib
import numpy as np
import concourse.bass as bass
import concourse.mybir as mybir
from concourse.bass_utils import run_bass_kernel_spmd

F32 = mybir.dt.float32
BF16 = mybir.dt.bfloat16
AF = mybir.ActivationFunctionType
ALU = mybir.AluOpType
AX = mybir.AxisListType

D = 1024
NCH = 8
SEQ = 2048
NMETA = 16
TP = NMETA + SEQ
DSEQ = 64
NTOK = TP + 2 * DSEQ
DFF = 2752
H = 16
N = 64
PAST = 4096
RMS_EPS = 1e-6
GN_EPS = 64e-5

PV = {}
_pv_names = ["ffn_g00", "ffn_g01", "ffn_g10", "ffn_g11", "mix_g0", "mix_g1",
             "mu0", "mu1", "mu2", "mu3", "mu4", "mu5", "w0", "a0", "k_k", "k_a", "r_k", "gn_w", "gn_b"]
for _i, _n in enumerate(_pv_names):
    PV[_n] = _i
NPV = len(_pv_names)

C_IDENT = 0
C_ONES = 128
C_TRI_INCL = 256
C_TRI_COMP = 384
C_EPS_RMS = 512
C_EPS_GN = 513
C_BD = 520
C_M1 = 648
C_M2 = 1160
C_ONE = 514
NCONST = 1672


_UN = [0]


def g_sbuf(nc, name, shape, dt):
    _UN[0] += 1
    return nc.sbuf_tensor("%s_u%d" % (name, _UN[0]), shape, dt)


class Buf:
    __slots__ = ("name", "w", "rs")

    def __init__(self, name=""):
        self.name = name
        self.w = None
        self.rs = {}


class Sched:
    ENG = ("pe", "dve", "act", "pool", "sp")

    def __init__(self, nc, ring=12):
        self.nc = nc
        self.q = {e: [] for e in self.ENG}
        self.cnt = {e: 0 for e in self.ENG}
        self.seen = {e: {} for e in self.ENG}
        self.sems = {}
        for e in self.ENG:
            self.sems[e] = nc.alloc_semaphore("c_" + e)
        self.ring = ring
        self.dma_n = {}
        self.dma_last = {}
        for qn in ("sp", "pool", "act"):
            self.dma_n[qn] = 0
            for s in range(ring):
                self.sems[("dma", qn, s)] = nc.alloc_semaphore("d_%s_%d" % (qn, s))
        self.ninstr = 0

    def _wait(self, eng, key, val):
        if self.seen[eng].get(key, 0) >= val:
            return
        self.seen[eng][key] = val
        self.q[eng].append(("w", key, val))
        self.ninstr += 1

    def _deps(self, eng, reads, writes):
        deps = {}
        for b in reads:
            if b.w is not None:
                k, v = b.w
                if deps.get(k, 0) < v:
                    deps[k] = v
        for b in writes:
            if b.w is not None:
                k, v = b.w
                if deps.get(k, 0) < v:
                    deps[k] = v
            for k, v in b.rs.items():
                if deps.get(k, 0) < v:
                    deps[k] = v
        for k, v in deps.items():
            if eng == "pe" and k == "pe":
                continue
            self._wait(eng, k, v)

    def _mark(self, tok, reads, writes):
        k, v = tok
        for b in reads:
            if b.rs.get(k, 0) < v:
                b.rs[k] = v
        for b in writes:
            b.w = tok
            b.rs = {}

    def op(self, eng, fn, reads=(), writes=()):
        self._deps(eng, reads, writes)
        self.cnt[eng] += 1
        tok = (eng, self.cnt[eng])
        self.q[eng].append(("op", fn, eng, 1))
        self.ninstr += 1
        self._mark(tok, reads, writes)
        return tok

    def dma(self, qn, out, in_, reads=(), writes=(), **kw):
        self._deps(qn, reads, writes)
        n = self.dma_n[qn]
        s = n % self.ring
        val = 16 * (n // self.ring + 1)
        key = ("dma", qn, s)
        if n >= self.ring:
            self._wait(qn, key, val - 16)
        self.dma_n[qn] = n + 1
        fn = lambda e, out=out, in_=in_, kw=kw: e.dma_start(out=out, in_=in_, **kw)
        self.q[qn].append(("op", fn, key, 16))
        self.ninstr += 1
        tok = (key, val)
        self.dma_last[key] = val
        self._mark(tok, reads, writes)
        return tok

    def barrier(self):
        toks = [(e, self.cnt[e]) for e in self.ENG if self.cnt[e] > 0]
        toks += list(self.dma_last.items())
        for e in self.ENG:
            for k, v in toks:
                if k == e and e == "pe":
                    continue
                self._wait(e, k, v)

    def finish(self):
        for k, v in self.dma_last.items():
            self._wait("sp", k, v)
        for e in self.ENG:
            if e != "sp" and self.cnt[e] > 0:
                self._wait("sp", e, self.cnt[e])

    def replay(self, block):
        sems = self.sems

        def mk(name):
            items = self.q[name]

            def body(e):
                for it in items:
                    if it[0] == "w":
                        e.wait_ge(sems[it[1]], it[2])
                    else:
                        ins = it[1](e)
                        ins.then_inc(sems[it[2]], it[3])
            return body

        block.tensor(mk("pe"))
        block.vector(mk("dve"))
        block.scalar(mk("act"))
        block.gpsimd(mk("pool"))
        block.sync(mk("sp"))


def col_tiles():
    t = []
    c = 0
    while c < NTOK:
        n = min(512, NTOK - c)
        t.append((c, n))
        c += n
    return t


COLT = col_tiles()
NCT = len(COLT)
TT = [(0, NMETA)] + [(NMETA + 128 * i, 128) for i in range(16)] + [(TP, 128)]


class Ctx:
    pass


def build(stages):
    nc = bass.Bass("TRN2", target_bir_lowering=False)
    P = Sched(nc)
    g = Ctx()
    g.nc, g.P = nc, P
    g.rwkv_tiles = None
    g.do_sample = "nosample" not in stages
    g.sb_pairs = None
    for st in stages:
        if isinstance(st, tuple) and st[0] == "sb_pairs":
            g.sb_pairs = list(st[1])
    for st in stages:
        if isinstance(st, tuple) and st[0] == "rwkv_tiles":
            g.rwkv_tiles = list(st[1])
        if isinstance(st, tuple) and st[0] == "sb_stop":
            g.sb_stop = float(st[1])
        if isinstance(st, tuple) and st[0] == "rwkv_stop":
            g.rwkv_stop = float(st[1])
    dram = {}

    def din(name, shape, dt=F32):
        dram[name] = nc.dram_tensor(name, list(shape), dt, kind="ExternalInput").ap()
        return dram[name]

    def dout(name, shape, dt=F32):
        dram[name] = nc.dram_tensor(name, list(shape), dt, kind="ExternalOutput").ap()
        return dram[name]

    g.dram = dram
    din("x_prompt", (SEQ, D))
    din("x_sample", (2 * DSEQ, D))
    din("meta_tokens", (NMETA, D))
    din("consts", (128, NCONST))
    din("pvec", (128, NPV * 8))
    din("ffn_w_in", (2, 2, D, 2 * DFF))
    din("ffn_w_out", (2, 2, DFF, D))
    din("rwkv_w_rkv", (3, D, D))
    din("rwkv_w_o", (D, D))
    din("rwkv_w1", (D, 64))
    din("rwkv_w2", (64, D))
    din("rwkv_a1", (D, 64))
    din("rwkv_a2", (64, D))
    din("rwkv_g1", (D, 128))
    din("rwkv_g2", (128, D))
    din("sb_w_qkv", (D, 3 * D))
    din("sb_w_o", (D, D))
    din("gqk", (128, 256))
    if g.do_sample:
        din("cache_k_in", (2, H, PAST, N))
        din("cache_v_in", (2, H, PAST, N))
    dout("cache_k_p", (H, TP, N))
    dout("cache_v_p", (H, TP, N))
    dout("cache_k_s", (2, H, DSEQ, N))
    dout("cache_v_s", (2, H, DSEQ, N))
    din("shift_in", (128, 16))
    din("state_wkv_in", (2, H, N, N))
    dout("state_wkv_p", (H, N, N))
    dout("shift_p", (8, 128))
    dout("state_wkv_s", (2, H, N, N))
    dout("shift_s", (2, 8, 128))
    dout("y_prompt", (SEQ, D))
    dout("y_sample", (2 * DSEQ, D))

    g.xT = nc.alloc_sbuf_tensor("xT", [128, NCH, NTOK], F32)
    g.xT_b = [[Buf("xT") for _ in range(NCT)] for _ in range(NCH)]
    g.consts = nc.alloc_sbuf_tensor("consts_sb", [128, NCONST], F32)
    g.consts_bf = nc.alloc_sbuf_tensor("consts_bf", [128, 256], BF16)
    g.pvec = nc.alloc_sbuf_tensor("pvec_sb", [128, NPV * 8], F32)
    g.cb = Buf("consts")
    g.banks = [nc.alloc_psum_tensor("bank%d" % i, [128, 512], F32) for i in range(8)]
    g.bank_b = [Buf("bank%d" % i) for i in range(8)]
    g.bank_rr = 0

    P.dma("sp", g.consts[:], dram["consts"], writes=[g.cb])
    P.dma("sp", g.pvec[:], dram["pvec"], writes=[g.cb])
    P.op("dve", lambda e: e.tensor_copy(out=g.consts_bf[:], in_=g.consts[:, 0:256]), reads=[g.cb], writes=[g.cb])

    load_x(g)
    for li in range(2):
        if ("ffn", li, 0) in stages:
            ffn(g, li, 0)
        if li == 0 and "rwkv" in stages:
            rwkv(g, tiles=g.rwkv_tiles)
        if li == 1 and "sb" in stages:
            sb_attn(g, do_sample=g.do_sample, pairs=g.sb_pairs)
        if ("ffn", li, 1) in stages:
            ffn(g, li, 1)
    store_y(g)
    P.finish()
    with nc.Block() as block:
        P.replay(block)
    print("instructions:", P.ninstr)
    return nc


def xt_bufs_for_cols(g, c, col0, n):
    out = []
    for t, (c0, nn) in enumerate(COLT):
        if c0 < col0 + n and col0 < c0 + nn:
            out.append(g.xT_b[c][t])
    return out


def load_x(g):
    nc, P = g.nc, g.P
    ident = g.consts[:, C_IDENT:C_IDENT + 128]
    with contextlib.ExitStack() as es:
        stg = [es.enter_context(g_sbuf(nc, "ldstg%d" % i, [128, D], F32)) for i in range(3)]
        stg_b = [Buf("ldstg") for _ in range(3)]
        for ti, (col0, n) in enumerate(TT):
            s = ti % 3
            if ti == 0:
                src = g.dram["meta_tokens"]
            elif ti <= 16:
                src = g.dram["x_prompt"][(ti - 1) * 128:ti * 128, :]
            else:
                src = g.dram["x_sample"]
            P.dma("sp", stg[s][0:n, :], src, writes=[stg_b[s]])
            for half in range(2):
                bk = (ti * 2 + half) % 2 + 6
                bank = g.banks[bk]
                for cc in range(4):
                    c = half * 4 + cc
                    P.op("pe", lambda e, bank=bank, cc=cc, c=c, s=s, n=n: e.transpose(
                        out=bank[:, cc * 128:cc * 128 + n], in_=stg[s][0:n, c * 128:(c + 1) * 128],
                        identity=ident[0:n, 0:n]),
                        reads=[stg_b[s], g.cb], writes=[g.bank_b[bk]])
                wb = []
                for cc in range(4):
                    wb += xt_bufs_for_cols(g, half * 4 + cc, col0, n)
                src_ap = bank[:].rearrange("p (a b) -> p a b", a=4)[:, :, 0:n]
                dst_ap = g.xT[:, half * 4:half * 4 + 4, col0:col0 + n]
                eng = "act" if half == 0 else "dve"
                if eng == "act":
                    P.op("act", lambda e, d=dst_ap, s_=src_ap: e.copy(out=d, in_=s_), reads=[g.bank_b[bk]], writes=wb)
                else:
                    P.op("dve", lambda e, d=dst_ap, s_=src_ap: e.tensor_copy(out=d, in_=s_), reads=[g.bank_b[bk]], writes=wb)
        P.barrier()


def store_y(g):
    nc, P = g.nc, g.P
    ident = g.consts[:, C_IDENT:C_IDENT + 128]
    with contextlib.ExitStack() as es:
        stg = [es.enter_context(g_sbuf(nc, "ststg%d" % i, [128, D], F32)) for i in range(3)]
        stg_b = [Buf("ststg") for _ in range(3)]
        k = 0
        for ti, (col0, n) in enumerate(TT):
            if ti == 0:
                continue
            s = k % 3
            k += 1
            for half in range(2):
                bk = (ti * 2 + half) % 2 + 6
                bank = g.banks[bk]
                for cc in range(4):
                    c = half * 4 + cc
                    P.op("pe", lambda e, bank=bank, cc=cc, c=c, col0=col0, n=n: e.transpose(
                        out=bank[0:n, cc * 128:(cc + 1) * 128], in_=g.xT[:, c, col0:col0 + n],
                        identity=ident),
                        reads=xt_bufs_for_cols(g, c, col0, n) + [g.cb], writes=[g.bank_b[bk]])
                dst_ap = stg[s][0:n, half * 512:(half + 1) * 512]
                src_ap = bank[0:n, :]
                if half == 0:
                    P.op("act", lambda e, d=dst_ap, s_=src_ap: e.copy(out=d, in_=s_), reads=[g.bank_b[bk]], writes=[stg_b[s]])
                else:
                    P.op("dve", lambda e, d=dst_ap, s_=src_ap: e.tensor_copy(out=d, in_=s_), reads=[g.bank_b[bk]], writes=[stg_b[s]])
            if ti <= 16:
                dst = g.dram["y_prompt"][(ti - 1) * 128:ti * 128, :]
            else:
                dst = g.dram["y_sample"]
            P.dma("sp", dst, stg[s][0:n, :], reads=[stg_b[s]])
        P.barrier()


def rmsnorm_T(g, es, gname, out_dt=BF16):
    nc, P = g.nc, g.P
    xn = es.enter_context(g_sbuf(nc, "xn", [128, NCH, NTOK], out_dt))
    xn_b = [[Buf("xn") for _ in range(NCT)] for _ in range(NCH)]
    sq = [es.enter_context(g_sbuf(nc, "sq%d" % i, [128, 512], BF16)) for i in range(2)]
    sq_b = [Buf("sq") for _ in range(2)]
    rt = [es.enter_context(g_sbuf(nc, "rt%d" % i, [128, 512], F32)) for i in range(2)]
    rt_b = [Buf("rt") for _ in range(2)]
    ones = g.consts_bf[:, C_ONES:C_ONES + 128]
    gcol = PV[gname] * 8
    k = 0
    for t, (c0, n) in enumerate(COLT):
        bk = 6 + (t % 2)
        bank = g.banks[bk]
        for c in range(NCH):
            s = k % 2
            k += 1
            P.op("act", lambda e, s=s, c=c, c0=c0, n=n: e.activation(out=sq[s][:, 0:n], in_=g.xT[:, c, c0:c0 + n], func=AF.Square),
                 reads=[g.xT_b[c][t]], writes=[sq_b[s]])
            P.op("pe", lambda e, s=s, c=c, n=n, bank=bank: e.matmul(bank[:, 0:n], lhsT=ones, rhs=sq[s][:, 0:n], start=(c == 0), stop=(c == NCH - 1)),
                 reads=[sq_b[s], g.cb], writes=[g.bank_b[bk]])
        r = t % 2
        P.op("act", lambda e, r=r, n=n, bank=bank: e.activation(out=rt[r][:, 0:n], in_=bank[:, 0:n], func=AF.Sqrt, scale=1.0 / D, bias=g.consts[:, C_EPS_RMS:C_EPS_RMS + 1]),
             reads=[g.bank_b[bk], g.cb], writes=[rt_b[r]])
        P.op("dve", lambda e, r=r, n=n: e.reciprocal(out=rt[r][:, 0:n], in_=rt[r][:, 0:n]), reads=[rt_b[r]], writes=[rt_b[r]])
        for c in range(NCH):
            P.op("dve", lambda e, r=r, c=c, c0=c0, n=n: e.scalar_tensor_tensor(
                out=xn[:, c, c0:c0 + n], in0=g.xT[:, c, c0:c0 + n], scalar=g.pvec[:, gcol + c:gcol + c + 1],
                in1=rt[r][:, 0:n], op0=ALU.mult, op1=ALU.mult),
                reads=[g.xT_b[c][t], rt_b[r], g.cb], writes=[xn_b[c][t]])
    return xn, xn_b


def ffn(g, li, fi):
    nc, P = g.nc, g.P
    w_in = g.dram["ffn_w_in"][li, fi]
    w_out = g.dram["ffn_w_out"][li, fi]
    GS = 4
    groups = []
    j = 0
    while j < 22:
        groups.append(list(range(j, min(j + GS, 22))))
        j += GS
    csize = lambda j: 128 if j < 21 else 64
    with contextlib.ExitStack() as es:
        xn, xn_b = rmsnorm_T(g, es, "ffn_g%d%d" % (li, fi))
        act = [es.enter_context(g_sbuf(nc, "act%d" % i, [128, GS, NTOK], BF16)) for i in range(2)]
        act_b = [[[Buf("act") for _ in range(NCT)] for _ in range(GS)] for _ in range(2)]
        wi = [es.enter_context(g_sbuf(nc, "wi%d" % i, [128, NCH, 2, GS * 128], BF16)) for i in range(2)]
        wi_b = [Buf("wi") for _ in range(2)]
        wo = [es.enter_context(g_sbuf(nc, "wo%d" % i, [128, GS, D], BF16)) for i in range(2)]
        wo_b = [Buf("wo") for _ in range(2)]
        sl = [es.enter_context(g_sbuf(nc, "sl%d" % i, [128, 512], F32)) for i in range(2)]
        sl_b = [Buf("sl") for _ in range(2)]
        w_in_v = w_in.rearrange("(c p) n -> p c n", p=128)

        def load_w(gi):
            grp = groups[gi]
            s = gi % 2
            col0 = grp[0] * 128
            ncols = sum(csize(j) for j in grp)
            P.dma("pool", wi[s][:, :, 0, 0:ncols], w_in_v[:, :, col0:col0 + ncols], writes=[wi_b[s]])
            P.dma("pool", wi[s][:, :, 1, 0:ncols], w_in_v[:, :, DFF + col0:DFF + col0 + ncols], writes=[wi_b[s]])
            nfull = sum(1 for j in grp if csize(j) == 128)
            if nfull:
                P.dma("pool", wo[s][:, 0:nfull, :],
                      w_out[col0:col0 + nfull * 128, :].rearrange("(g p) n -> p g n", p=128), writes=[wo_b[s]])
            if nfull < len(grp):
                r0 = col0 + nfull * 128
                P.dma("pool", wo[s][0:64, nfull, :], w_out[r0:r0 + 64, :], writes=[wo_b[s]])

        kk = [0]

        def phase_a(gi):
            grp = groups[gi]
            s = gi % 2
            for jj, j in enumerate(grp):
                m = csize(j)
                for t, (c0, n) in enumerate(COLT):
                    q = kk[0] % 2
                    kk[0] += 1
                    bg, bu = g.banks[q * 2], g.banks[q * 2 + 1]
                    for c in range(NCH):
                        P.op("pe", lambda e, bg=bg, c=c, jj=jj, m=m, c0=c0, n=n, s=s: e.matmul(
                            bg[0:m, 0:n], lhsT=wi[s][:, c, 0, jj * 128:jj * 128 + m], rhs=xn[:, c, c0:c0 + n],
                            start=(c == 0), stop=(c == NCH - 1)),
                            reads=[wi_b[s], xn_b[c][t]], writes=[g.bank_b[q * 2]])
                    for c in range(NCH):
                        P.op("pe", lambda e, bu=bu, c=c, jj=jj, m=m, c0=c0, n=n, s=s: e.matmul(
                            bu[0:m, 0:n], lhsT=wi[s][:, c, 1, jj * 128:jj * 128 + m], rhs=xn[:, c, c0:c0 + n],
                            start=(c == 0), stop=(c == NCH - 1)),
                            reads=[wi_b[s], xn_b[c][t]], writes=[g.bank_b[q * 2 + 1]])
                    P.op("act", lambda e, q=q, bg=bg, m=m, n=n: e.activation(out=sl[q][0:m, 0:n], in_=bg[0:m, 0:n], func=AF.Silu),
                         reads=[g.bank_b[q * 2]], writes=[sl_b[q]])
                    P.op("dve", lambda e, q=q, bu=bu, m=m, n=n, s=s, jj=jj, c0=c0: e.tensor_tensor(
                        out=act[s][0:m, jj, c0:c0 + n], in0=sl[q][0:m, 0:n], in1=bu[0:m, 0:n], op=ALU.mult),
                        reads=[sl_b[q], g.bank_b[q * 2 + 1]], writes=[act_b[s][jj][t]])

        ko = [0]

        def phase_b(gi):
            grp = groups[gi]
            s = gi % 2
            for dc in range(NCH):
                for t, (c0, n) in enumerate(COLT):
                    bk = 4 + ko[0] % 2
                    ko[0] += 1
                    bank = g.banks[bk]
                    for jj, j in enumerate(grp):
                        m = csize(j)
                        P.op("pe", lambda e, bank=bank, jj=jj, m=m, dc=dc, c0=c0, n=n, s=s: e.matmul(
                            bank[:, 0:n], lhsT=wo[s][0:m, jj, dc * 128:(dc + 1) * 128], rhs=act[s][0:m, jj, c0:c0 + n],
                            start=(jj == 0), stop=(jj == len(grp) - 1)),
                            reads=[wo_b[s], act_b[s][jj][t]], writes=[g.bank_b[bk]])
                    P.op("dve", lambda e, bank=bank, dc=dc, c0=c0, n=n: e.scalar_tensor_tensor(
                        out=g.xT[:, dc, c0:c0 + n], in0=bank[:, 0:n], scalar=0.5, in1=g.xT[:, dc, c0:c0 + n],
                        op0=ALU.mult, op1=ALU.add),
                        reads=[g.bank_b[bk], g.xT_b[dc][t]], writes=[g.xT_b[dc][t]])

        ng = len(groups)
        load_w(0)
        load_w(1)
        phase_a(0)
        for gi in range(ng):
            if gi + 1 < ng:
                phase_a(gi + 1)
            phase_b(gi)
            if gi + 2 < ng:
                load_w(gi + 2)
        P.barrier()


def MM(g, out, lhsT, rhs, r, w, start=True, stop=True, skip=False):
    return g.P.op("pe", lambda e: e.matmul(out, lhsT=lhsT, rhs=rhs, start=start, stop=stop, skip_group_check=skip), reads=r, writes=w)


def TRP(g, out, in_, ident, r, w):
    return g.P.op("pe", lambda e: e.transpose(out=out, in_=in_, identity=ident), reads=r, writes=w)


def ACTF(g, out, in_, func, r, w, bias=None, scale=None):
    kw = {}
    if bias is not None:
        kw["bias"] = bias
    if scale is not None:
        kw["scale"] = scale
    return g.P.op("act", lambda e: e.activation(out=out, in_=in_, func=func, **kw), reads=r, writes=w)


def CPY(g, eng, out, in_, r, w):
    if eng == "act":
        return g.P.op("act", lambda e: e.copy(out=out, in_=in_), reads=r, writes=w)
    return g.P.op(eng, lambda e: e.tensor_copy(out=out, in_=in_), reads=r, writes=w)


def TTO(g, eng, out, in0, in1, op, r, w):
    return g.P.op(eng, lambda e: e.tensor_tensor(out=out, in0=in0, in1=in1, op=op), reads=r, writes=w)


def TSC(g, eng, out, in0, s1, s2, op0, op1, r, w):
    if s2 is None:
        return g.P.op(eng, lambda e: e.tensor_scalar(out=out, in0=in0, scalar1=s1, scalar2=None, op0=op0), reads=r, writes=w)
    return g.P.op(eng, lambda e: e.tensor_scalar(out=out, in0=in0, scalar1=s1, scalar2=s2, op0=op0, op1=op1), reads=r, writes=w)


def STT(g, out, in0, scalar, in1, op0, op1, r, w):
    return g.P.op("dve", lambda e: e.scalar_tensor_tensor(out=out, in0=in0, scalar=scalar, in1=in1, op0=op0, op1=op1), reads=r, writes=w)


def run_interleaved(gens):
    gens = list(gens)
    while gens:
        for gen in list(gens):
            try:
                next(gen)
            except StopIteration:
                gens.remove(gen)


def next_bank(g):
    i = g.bank_rr % 8
    g.bank_rr += 1
    return g.banks[i], g.bank_b[i]


def pvc(g, name, c):
    j = PV[name] * 8 + c
    return g.pvec[:, j:j + 1]


class StopRwkv(Exception):
    pass


def rwkv(g, tiles=None):
    try:
        _rwkv(g, tiles)
    except StopRwkv:
        pass
    g.P.barrier()


def _rwkv(g, tiles=None):
    stop = getattr(g, "rwkv_stop", 99)
    nc, P, d = g.nc, g.P, g.dram
    RTT = TT[:17] + [(TP, 128), (TP, 128)]
    tiles = list(range(len(RTT))) if tiles is None else tiles
    C0 = float(np.exp(-0.5))
    cb = g.cb
    identb = g.consts_bf[:, C_IDENT:C_IDENT + 128]
    identf = g.consts[:, C_IDENT:C_IDENT + 128]
    onesb = g.consts_bf[:, C_ONES:C_ONES + 128]
    onesf = g.consts[:, C_ONES:C_ONES + 128]
    BDf = g.consts[:, C_BD:C_BD + 128]
    with contextlib.ExitStack() as es:
        def sb(name, shape, dt):
            return es.enter_context(g_sbuf(nc, "rk_" + name, shape, dt))
        wb = Buf("rwkv_w")
        Wr, Wk, Wv, Wo = (sb(nm, [128, NCH, D], BF16) for nm in ("Wr", "Wk", "Wv", "Wo"))
        for i, W in enumerate((Wr, Wk, Wv)):
            P.dma("pool", W[:], d["rwkv_w_rkv"][i].rearrange("(c p) n -> p c n", p=128), writes=[wb])
        P.dma("pool", Wo[:], d["rwkv_w_o"].rearrange("(c p) n -> p c n", p=128), writes=[wb])
        w1 = sb("w1", [128, NCH, 64], BF16)
        a1 = sb("a1", [128, NCH, 64], BF16)
        g1 = sb("g1", [128, NCH, 128], BF16)
        w2 = sb("w2", [64, D], BF16)
        a2 = sb("a2", [64, D], BF16)
        g2 = sb("g2", [128, D], BF16)
        P.dma("pool", w1[:], d["rwkv_w1"].rearrange("(c p) n -> p c n", p=128), writes=[wb])
        P.dma("pool", a1[:], d["rwkv_a1"].rearrange("(c p) n -> p c n", p=128), writes=[wb])
        P.dma("pool", g1[:], d["rwkv_g1"].rearrange("(c p) n -> p c n", p=128), writes=[wb])
        P.dma("pool", w2[:], d["rwkv_w2"], writes=[wb])
        P.dma("pool", a2[:], d["rwkv_a2"], writes=[wb])
        P.dma("pool", g2[:], d["rwkv_g2"], writes=[wb])
        shift_sb = sb("shift", [128, 16], F32)
        P.dma("sp", shift_sb[:], d["shift_in"], writes=[wb])
        omka = sb("omka", [128, 8], F32)
        ka0 = PV["k_a"] * 8
        TSC(g, "dve", omka[:], g.pvec[:, ka0:ka0 + 8], -1.0, 1.0, ALU.mult, ALU.add, [cb], [wb])

        M32 = [sb("M32_%d" % c, [128, 8, 64], F32) for c in range(3)]
        Mb = [sb("Mb_%d" % c, [128, 8, 64], BF16) for c in range(3)]
        M_b = [[Buf("M") for _ in range(8)] for _ in range(3)]
        P.op("pool", lambda e: e.memset(M32[0][:], 0.0), writes=M_b[0])
        P.op("pool", lambda e: e.memset(Mb[0][:], 0.0), writes=M_b[0])
        s0stg = sb("s0stg", [64, 16, 64], F32)
        s0_b = Buf("s0stg")
        for sq_ in range(2):
            P.dma("sp", s0stg[:], d["state_wkv_in"][sq_].rearrange("h v k -> v h k"), writes=[s0_b])
            for p in range(8):
                bk, bb = next_bank(g)
                TRP(g, bk[:, 0:64], s0stg[:, 2 * p:2 * p + 2, :].rearrange("v h k -> v (h k)"), identf[0:64, 0:64], [s0_b, cb], [bb])
                CPY(g, "act", M32[1 + sq_][:, p, :], bk[:, 0:64], [bb], [M_b[1 + sq_][p]])
                CPY(g, "dve", Mb[1 + sq_][:, p, :], bk[:, 0:64], [bb], [M_b[1 + sq_][p]])

        hb1 = sb("hb", [128, NCH, 132], F32)
        hb1_b = Buf("hb")
        hlast = sb("hlast", [128, NCH, 1], F32)
        hlast_b = Buf("hlast")
        dx = sb("dx", [128, NCH, 128], F32)
        dx_b = Buf("dx")
        sqt = sb("sqt", [128, NCH, 128], BF16)
        sq_b = Buf("sq")
        rt = sb("rt", [128, 128], F32)
        rt_b = Buf("rt")
        mixL = sb("mixL", [128, NCH, 128], BF16)
        mixL_b = [Buf("mixL") for _ in range(NCH)]
        mix = {i: sb("mix%d" % i, [128, NCH, 128], BF16) for i in (0, 2, 3)}
        mix_b = {i: [Buf("mix") for _ in range(NCH)] for i in (0, 2, 3)}
        for i in (1, 4, 5):
            mix[i] = mixL
            mix_b[i] = mixL_b
        tw = sb("tw", [64, 128], BF16)
        ta = sb("ta", [64, 128], BF16)
        tg = sb("tg", [128, 128], BF16)
        tw_b, ta_b, tg_b = Buf("tw"), Buf("ta"), Buf("tg")
        ogT = sb("ogT", [128, 8, 128], BF16)
        og_b = [Buf("og") for _ in range(8)]

        def F(name):
            return sb(name, [128, 128], F32), Buf(name)
        sg, sg_b = F("sg")
        av, av_b = F("av")
        gsb, gsb_b = F("gsb")
        vsb, vsb_b = F("vsb")
        kkraw, kkraw_b = F("kkraw")
        kksq, kksq_b = F("kksq")
        nrm, nrm_b = F("nrm")
        kkn, kkn_b = F("kkn")
        t1, t1_b = F("t1")
        kmod, kmod_b = F("kmod")
        bb_, bb_b = F("bb")
        cs, cs_b = F("cs")
        csx, csx_b = F("csx")
        E1, E1_b = F("E1")
        E2, E2_b = F("E2")
        E3, E3_b = F("E3")
        E4, E4_b = F("E4")
        nb = sb("nb", [128, 2], F32)
        nb_b = Buf("nb")
        rkr, rkr_b = F("rkr")
        bonus, bonus_b = F("bonus")
        kr = sb("kr", [128, 2, 128], BF16)
        kr_b = Buf("kr")
        Bh = sb("Bh", [128, 128], BF16)
        Kh = sb("Kh", [128, 128], BF16)
        Bh_b, Kh_b = Buf("Bh"), Buf("Kh")
        src4 = sb("src4", [128, 4, 128], BF16)
        src4_b = Buf("src4")
        tok4 = sb("tok4", [128, 4, 128], BF16)
        tok4_b = Buf("tok4")
        XX = [[sb("XX%d_%d" % (h, i), [128, 2, 128], BF16) for i in range(2)] for h in range(2)]
        XX_b = [[Buf("XX") for _ in range(2)] for _ in range(2)]
        YY = [[sb("YY%d_%d" % (h, i), [128, 128], BF16) for i in range(2)] for h in range(2)]
        YY_b = [[Buf("YY") for _ in range(2)] for _ in range(2)]
        LkTm = [sb("LkTm%d" % h, [128, 128], BF16) for h in range(2)]
        AbTm = [sb("AbTm%d" % h, [128, 128], BF16) for h in range(2)]
        AkTm = [sb("AkTm%d" % h, [128, 128], BF16) for h in range(2)]
        Lm_b = [Buf("Lm") for _ in range(2)]
        Wt = [sb("Wt%d" % h, [128, 64], BF16) for h in range(2)]
        Uv = [sb("Uv%d" % h, [128, 64], BF16) for h in range(2)]
        WU_b = [Buf("WU") for _ in range(2)]
        QtT = sb("QtT", [128, 128], BF16)
        QtT_b = [Buf("QtT") for _ in range(2)]
        Gneg = sb("Gneg", [128, 2, 64], BF16)
        Gneg_b = [Buf("Gneg") for _ in range(2)]
        osb = sb("osb", [128, 2, 128], F32)
        osb_b = Buf("osb")
        gm, gm_b = kkraw, kkraw_b
        gm2, gm2_b = kksq, kksq_b
        gv, gv_b = nrm, nrm_b
        outstg = s0stg
        outstg_b = s0_b
        shstg = sb("shstg", [8, 128], F32)
        shstg_b = Buf("shstg")

        if stop <= 1:
            return
        for ti in tiles:
            col0, n = RTT[ti]
            sample = (ti >= 17)
            segs = [(0, n, ti - 16)] if sample else [(0, n, 0)]
            real0, realn = ((ti - 17) * 64, 64) if sample else (0, n)
            pad0 = (64 - real0) if sample else None
            maxseg = max(s[1] for s in segs)
            L = max(1, int(np.ceil(np.log2(maxseg))))
            mofs = C_M1
            MSN = g.consts[0:n, mofs + 0:mofs + n]
            MSNT = g.consts[0:n, mofs + 128:mofs + 128 + n]
            MS = g.consts[0:n, mofs + 256:mofs + 256 + n]
            MI = g.consts[0:n, mofs + 384:mofs + 384 + n]
            hcur = hb1
            hcur_b = hb1_b
            xb_all = []
            for c in range(NCH):
                xb_all += xt_bufs_for_cols(g, c, col0, n)
            ACTF(g, sqt[:, :, 0:n], g.xT[:, :, col0:col0 + n], AF.Square, xb_all, [sq_b])
            bk, bkb = next_bank(g)
            for c in range(NCH):
                MM(g, bk[:, 0:n], onesb, sqt[:, c, 0:n], [sq_b, cb], [bkb], start=(c == 0), stop=(c == NCH - 1))
            ACTF(g, rt[:, 0:n], bk[:, 0:n], AF.Sqrt, [bkb, cb], [rt_b], bias=g.consts[:, C_EPS_RMS:C_EPS_RMS + 1], scale=1.0 / D)
            P.op("dve", lambda e, n=n: e.reciprocal(out=rt[:, 0:n], in_=rt[:, 0:n]), reads=[rt_b], writes=[rt_b])
            for c in range(NCH):
                STT(g, hcur[:, c, 1:n + 1], g.xT[:, c, col0:col0 + n], pvc(g, "mix_g0", c), rt[:, 0:n], ALU.mult, ALU.mult,
                    xt_bufs_for_cols(g, c, col0, n) + [rt_b, cb], [hcur_b])
            if ti == 0:
                P.op("pool", lambda e, hcur=hcur: e.memset(hcur[:, :, 0:1], 0.0), writes=[hcur_b])
            elif sample:
                CPY(g, "pool", hcur[:, :, 0:1], shift_sb[:, 0:8].rearrange("p (c o) -> p c o", o=1), [wb], [hcur_b])
            else:
                CPY(g, "pool", hcur[:, :, 0:1], hlast[:], [hlast_b], [hcur_b])
            TTO(g, "dve", dx[:, :, 0:n], hcur[:, :, 0:n], hcur[:, :, 1:n + 1], ALU.subtract, [hcur_b], [dx_b])
            if not sample:
                CPY(g, "pool", hlast[:], hcur[:, :, n:n + 1], [hcur_b], [hlast_b])
            if ti == 18:
                TTO(g, "dve", dx[:, :, 64:65], shift_sb[:, 8:16].rearrange("p (c o) -> p c o", o=1), hcur[:, :, 65:66], ALU.subtract,
                    [hcur_b, wb], [dx_b])
            def do_mix(i):
                for c in range(NCH):
                    STT(g, mix[i][:, c, 0:n], dx[:, c, 0:n], pvc(g, "mu%d" % i, c), hcur[:, c, 1:n + 1], ALU.mult, ALU.add,
                        [dx_b, hcur_b, cb], [mix_b[i][c]])
            for i in (0, 2, 3):
                do_mix(i)
            if ti == 16 or sample:
                for (s0, sn, ch) in segs:
                    bk, bkb = next_bank(g)
                    lc = real0 + realn
                    TRP(g, bk[0:8, 0:128], hcur[:, :, lc:lc + 1].rearrange("p c o -> p (c o)"), identf, [hcur_b, cb], [bkb])
                    CPY(g, "act", shstg[:], bk[0:8, 0:128], [bkb], [shstg_b])
                    dst = d["shift_p"] if ch == 0 else d["shift_s"][ch - 1]
                    P.dma("sp", dst, shstg[:], reads=[shstg_b])
            do_mix(1)
            bk, bkb = next_bank(g)
            for c in range(NCH):
                MM(g, bk[0:64, 0:n], w1[:, c, :], mix[1][:, c, 0:n], [wb, mix_b[1][c]], [bkb], start=(c == 0), stop=(c == NCH - 1))
            ACTF(g, tw[:, 0:n], bk[0:64, 0:n], AF.Tanh, [bkb], [tw_b])
            do_mix(4)
            bk, bkb = next_bank(g)
            for c in range(NCH):
                MM(g, bk[0:64, 0:n], a1[:, c, :], mix[4][:, c, 0:n], [wb, mix_b[4][c]], [bkb], start=(c == 0), stop=(c == NCH - 1))
            CPY(g, "act", ta[:, 0:n], bk[0:64, 0:n], [bkb], [ta_b])
            do_mix(5)
            bk, bkb = next_bank(g)
            for c in range(NCH):
                MM(g, bk[:, 0:n], g1[:, c, :], mix[5][:, c, 0:n], [wb, mix_b[5][c]], [bkb], start=(c == 0), stop=(c == NCH - 1))
            ACTF(g, tg[:, 0:n], bk[:, 0:n], AF.Sigmoid, [bkb], [tg_b])

            if stop <= 2:
                return
            for p in range(8):
                pc = p * 128
                bkA, bkA_b = next_bank(g)
                for j, (W, mi) in enumerate(((Wr, 0), (Wk, 2), (Wv, 3))):
                    for c in range(NCH):
                        MM(g, bkA[:, j * 128:j * 128 + n], W[:, c, pc:pc + 128], mix[mi][:, c, 0:n], [wb, mix_b[mi][c]], [bkA_b],
                           start=(c == 0), stop=(c == NCH - 1))
                MM(g, bkA[:, 384:384 + n], w2[:, pc:pc + 128], tw[:, 0:n], [wb, tw_b], [bkA_b])
                bkB, bkB_b = next_bank(g)
                MM(g, bkB[:, 0:n], a2[:, pc:pc + 128], ta[:, 0:n], [wb, ta_b], [bkB_b])
                MM(g, bkB[:, 128:128 + n], g2[:, pc:pc + 128], tg[:, 0:n], [wb, tg_b], [bkB_b])
                if stop <= 2.1:
                    return
                r_ps = bkA[:, 0:n]
                k_ps = bkA[:, 128:128 + n]
                v_ps = bkA[:, 256:256 + n]
                ACTF(g, sg[:, 0:n], bkA[:, 384:384 + n], AF.Sigmoid, [bkA_b, cb], [sg_b], bias=pvc(g, "w0", p))
                if sample:
                    P.op("pool", lambda e, pad0=pad0: e.memset(sg[:, pad0:pad0 + 64], 0.0), writes=[sg_b])
                ACTF(g, av[:, 0:n], bkB[:, 0:n], AF.Sigmoid, [bkB_b, cb], [av_b], bias=pvc(g, "a0", p))
                if stop <= 2.15:
                    return
                CPY(g, "act", gsb[:, 0:n], bkB[:, 128:128 + n], [bkB_b], [gsb_b])
                CPY(g, "act", vsb[:, 0:n], v_ps, [bkA_b], [vsb_b])
                if stop <= 2.17:
                    return
                CPY(g, "pool", src4[:, 3, 0:n], vsb[:, 0:n], [vsb_b], [src4_b])
                if stop <= 2.18:
                    return
                ACTF(g, kkraw[:, 0:n], k_ps, AF.Copy, [bkA_b, cb], [kkraw_b], scale=pvc(g, "k_k", p))
                if stop <= 2.2:
                    return
                ACTF(g, kksq[:, 0:n], kkraw[:, 0:n], AF.Square, [kkraw_b], [kksq_b])
                bkC, bkC_b = next_bank(g)
                MM(g, bkC[:, 0:n], BDf, kksq[:, 0:n], [cb, kksq_b], [bkC_b])
                ACTF(g, nrm[:, 0:n], bkC[:, 0:n], AF.Sqrt, [bkC_b], [nrm_b])
                TSC(g, "dve", nrm[:, 0:n], nrm[:, 0:n], 1e-12, None, ALU.max, None, [nrm_b], [nrm_b])
                P.op("dve", lambda e, n=n: e.reciprocal(out=nrm[:, 0:n], in_=nrm[:, 0:n]), reads=[nrm_b], writes=[nrm_b])
                if stop <= 2.4:
                    return
                TTO(g, "dve", kkn[:, 0:n], kkraw[:, 0:n], nrm[:, 0:n], ALU.mult, [kkraw_b, nrm_b], [kkn_b])
                TSC(g, "dve", t1[:, 0:n], av[:, 0:n], pvc(g, "k_a", p), omka[:, p:p + 1], ALU.mult, ALU.add, [av_b, cb, wb], [t1_b])
                TTO(g, "dve", kmod[:, 0:n], k_ps, t1[:, 0:n], ALU.mult, [bkA_b, t1_b], [kmod_b])
                TTO(g, "pool", bb_[:, 0:n], kkn[:, 0:n], av[:, 0:n], ALU.mult, [kkn_b, av_b], [bb_b])
                if sample:
                    P.op("pool", lambda e, pad0=pad0: e.memset(bb_[:, pad0:pad0 + 64], 0.0), writes=[bb_b])
                    P.op("pool", lambda e, pad0=pad0: e.memset(kmod[:, pad0:pad0 + 64], 0.0), writes=[kmod_b])
                for (s0, sn, ch) in segs:
                    P.op("dve", lambda e, s0=s0, sn=sn: e.tensor_tensor_scan(cs[:, s0:s0 + sn], onesf[:, 0:sn], sg[:, s0:s0 + sn], 0.0, ALU.mult, ALU.add),
                         reads=[sg_b, cb], writes=[cs_b])
                TTO(g, "pool", csx[:, 0:n], cs[:, 0:n], sg[:, 0:n], ALU.subtract, [cs_b, sg_b], [csx_b])
                if stop <= 2.5:
                    return
                ACTF(g, E1[:, 0:n], cs[:, 0:n], AF.Exp, [cs_b], [E1_b], scale=-C0)
                ACTF(g, E2[:, 0:n], csx[:, 0:n], AF.Exp, [csx_b], [E2_b], scale=-C0)
                ACTF(g, E3[:, 0:n], cs[:, 0:n], AF.Exp, [cs_b], [E3_b], scale=C0)
                for si, (s0, sn, ch) in enumerate(segs):
                    TSC(g, "dve", nb[:, si:si + 1], cs[:, s0 + sn - 1:s0 + sn], -C0, None, ALU.mult, None, [cs_b], [nb_b])
                    ACTF(g, E4[:, s0:s0 + sn], cs[:, s0:s0 + sn], AF.Exp, [cs_b, nb_b], [E4_b], bias=nb[:, si:si + 1], scale=C0)
                TTO(g, "dve", kr[:, 0, 0:n], kkn[:, 0:n], E2[:, 0:n], ALU.mult, [kkn_b, E2_b], [kr_b])
                TTO(g, "dve", kr[:, 1, 0:n], r_ps, E1[:, 0:n], ALU.mult, [bkA_b, E1_b], [kr_b])
                TTO(g, "pool", Bh[:, 0:n], bb_[:, 0:n], E3[:, 0:n], ALU.mult, [bb_b, E3_b], [Bh_b])
                TTO(g, "pool", Kh[:, 0:n], kmod[:, 0:n], E3[:, 0:n], ALU.mult, [kmod_b, E3_b], [Kh_b])
                TTO(g, "pool", src4[:, 1, 0:n], bb_[:, 0:n], E4[:, 0:n], ALU.mult, [bb_b, E4_b], [src4_b])
                TTO(g, "pool", src4[:, 2, 0:n], kmod[:, 0:n], E4[:, 0:n], ALU.mult, [kmod_b, E4_b], [src4_b])
                STT(g, rkr[:, 0:n], r_ps, pvc(g, "r_k", p), kmod[:, 0:n], ALU.mult, ALU.mult, [bkA_b, kmod_b, cb], [rkr_b])
                MM(g, bkC[:, 128:128 + n], BDf, rkr[:, 0:n], [cb, rkr_b], [bkC_b])
                TTO(g, "dve", bonus[:, 0:n], bkC[:, 128:128 + n], vsb[:, 0:n], ALU.mult, [bkC_b, vsb_b], [bonus_b])
                if stop <= 2.6:
                    return
                bkT, bkT_b = next_bank(g)
                bkT16 = bkT[:].bitcast(BF16)
                TRP(g, bkT16[0:n, 0:128], kr[:, 0, 0:n], identb, [kr_b, cb], [bkT_b])
                for j in (1, 2, 3):
                    TRP(g, bkT16[0:n, j * 128:(j + 1) * 128], src4[:, j, 0:n], identb, [src4_b, cb], [bkT_b])
                CPY(g, "act", tok4[0:n, :, :].rearrange("p a b -> p (a b)"), bkT16[0:n, 0:512], [bkT_b], [tok4_b])

                if stop <= 3:
                    return
                def head_gen(hh, p=p, n=n, segs=segs, L=L, MSN=MSN, MSNT=MSNT, MS=MS, MI=MI):
                    h0 = hh * 64
                    hs = slice(h0, h0 + 64)
                    bkU, bkU_b = next_bank(g)
                    bkW, bkW_b = next_bank(g)
                    krh = kr[hs, :, 0:n]
                    for a_ in range(2):
                        MM(g, bkU[0:n, a_ * 128:a_ * 128 + n], Bh[hs, 0:n], kr[hs, a_, 0:n], [Bh_b, kr_b], [bkU_b])
                        MM(g, bkU[0:n, 256 + a_ * 128:256 + a_ * 128 + n], Kh[hs, 0:n], kr[hs, a_, 0:n], [Kh_b, kr_b], [bkU_b])
                    MM(g, bkW[0:n, 0:n], kr[hs, 0, 0:n], Bh[hs, 0:n], [Bh_b, kr_b], [bkW_b])
                    X0 = XX[hh][0]
                    TTO(g, "dve", X0[0:n, 1, 0:n], bkU[0:n, 0:n], MSN, ALU.mult, [bkU_b, cb], [XX_b[hh][0]])
                    TTO(g, "dve", X0[0:n, 0, 0:n], bkW[0:n, 0:n], MSNT, ALU.mult, [bkW_b, cb], [XX_b[hh][0]])
                    TTO(g, "dve", AbTm[hh][0:n, 0:n], bkU[0:n, 128:128 + n], MI, ALU.mult, [bkU_b, cb], [Lm_b[hh]])
                    TTO(g, "dve", LkTm[hh][0:n, 0:n], bkU[0:n, 256:256 + n], MS, ALU.mult, [bkU_b, cb], [Lm_b[hh]])
                    TTO(g, "dve", AkTm[hh][0:n, 0:n], bkU[0:n, 384:384 + n], MI, ALU.mult, [bkU_b, cb], [Lm_b[hh]])
                    Vh = tok4[0:n, 3, h0:h0 + 64]
                    Bch = tok4[0:n, 1, h0:h0 + 64]
                    Kch = tok4[0:n, 2, h0:h0 + 64]
                    MM(g, bkW[0:n, 128:192], LkTm[hh][0:n, 0:n], Vh, [Lm_b[hh], tok4_b], [bkW_b])
                    Y = YY[hh][0]
                    CPY(g, "pool", Y[0:n, 0:64], tok4[0:n, 0, h0:h0 + 64], [tok4_b], [YY_b[hh][0]])
                    CPY(g, "act", Y[0:n, 64:128], bkW[0:n, 128:192], [bkW_b], [YY_b[hh][0]])
                    for lv in range(L):
                        yield
                        cur, nxt = lv % 2, (lv + 1) % 2
                        Xc = XX[hh][cur]
                        bkY, bkY_b = next_bank(g)
                        MM(g, bkY[0:n, 0:128], Xc[0:n, 1, 0:n], YY[hh][cur][0:n, :], [XX_b[hh][cur], YY_b[hh][cur]], [bkY_b], start=True, stop=False)
                        MM(g, bkY[0:n, 0:128], identb[0:n, 0:n], YY[hh][cur][0:n, :], [cb, YY_b[hh][cur]], [bkY_b], start=False, stop=True)
                        if lv < L - 1:
                            bkX, bkX_b = next_bank(g)
                            MM(g, bkX[0:n, 0:n], Xc[0:n, 1, 0:n], Xc[0:n, 0, 0:n], [XX_b[hh][cur]], [bkX_b])
                            MM(g, bkX[0:n, 128:128 + n], Xc[0:n, 0, 0:n], Xc[0:n, 1, 0:n], [XX_b[hh][cur]], [bkX_b])
                            CPY(g, "act", YY[hh][nxt][0:n, :], bkY[0:n, 0:128], [bkY_b], [YY_b[hh][nxt]])
                            CPY(g, "dve", XX[hh][nxt][0:n, :, 0:n], bkX[0:n, 0:256].rearrange("p (a b) -> p a b", a=2)[:, :, 0:n], [bkX_b], [XX_b[hh][nxt]])
                        else:
                            CPY(g, "act", Wt[hh][0:n, :], bkY[0:n, 0:64], [bkY_b], [WU_b[hh]])
                            P.op("act", lambda e, hh=hh, n=n, bkY=bkY: e.mul(Uv[hh][0:n, :], bkY[0:n, 64:128], -1.0), reads=[bkY_b], writes=[WU_b[hh]])
                    yield
                    if stop <= 4:
                        return
                    bkQ, bkQ_b = next_bank(g)
                    MM(g, bkQ[hs, 0:n], Wt[hh][0:n, :], AbTm[hh][0:n, 0:n], [WU_b[hh], Lm_b[hh]], [bkQ_b])
                    TTO(g, "dve", QtT[hs, 0:n], kr[hs, 1, 0:n], bkQ[hs, 0:n], ALU.subtract, [kr_b, bkQ_b], [QtT_b[hh]])
                    for si, (s0, sn, ch) in enumerate(segs):
                        MM(g, bkQ[hs, 128 + si * 64:192 + si * 64], Wt[hh][s0:s0 + sn, :], tok4[s0:s0 + sn, 1, h0:h0 + 64], [WU_b[hh], tok4_b], [bkQ_b])
                    nsg = len(segs)
                    P.op("act", lambda e, hs=hs, nsg=nsg, bkQ=bkQ: e.mul(Gneg[hs, 0:nsg, :].rearrange("p a b -> p (a b)"), bkQ[hs, 128:128 + 64 * nsg], -1.0),
                         reads=[bkQ_b], writes=[Gneg_b[hh]])
                    yield
                    if stop <= 5:
                        return
                    bkO, bkO_b = next_bank(g)
                    MM(g, bkO[hs, 0:n], Uv[hh][0:n, :], AbTm[hh][0:n, 0:n], [WU_b[hh], Lm_b[hh]], [bkO_b], start=True, stop=False)
                    MM(g, bkO[hs, 0:n], Vh, AkTm[hh][0:n, 0:n], [tok4_b, Lm_b[hh]], [bkO_b], start=False, stop=False)
                    for si, (s0, sn, ch) in enumerate(segs):
                        MM(g, bkO[hs, s0:s0 + sn], Mb[ch][hs, p, :], QtT[hs, s0:s0 + sn], [M_b[ch][p], QtT_b[hh]], [bkO_b],
                           start=False, stop=(si == len(segs) - 1))
                    CPY(g, "act", osb[hs, 0, 0:n], bkO[hs, 0:n], [bkO_b], [osb_b])
                    bkS, bkS_b = next_bank(g)
                    for si, (s0, sn, ch) in enumerate(segs):
                        so = bkS[hs, si * 64:si * 64 + 64]
                        MM(g, so, Gneg[hs, si, :], Mb[ch][hs, p, :], [Gneg_b[hh], M_b[ch][p]], [bkS_b], start=True, stop=False)
                        MM(g, so, tok4[s0:s0 + sn, 1, h0:h0 + 64], Uv[hh][s0:s0 + sn, :], [tok4_b, WU_b[hh]], [bkS_b], start=False, stop=False)
                        MM(g, so, tok4[s0:s0 + sn, 2, h0:h0 + 64], tok4[s0:s0 + sn, 3, h0:h0 + 64], [tok4_b], [bkS_b], start=False, stop=True)
                    for si, (s0, sn, ch) in enumerate(segs):
                        STT(g, M32[ch][hs, p, :], M32[ch][hs, p, :], E1[hs, s0 + sn - 1:s0 + sn], bkS[hs, si * 64:si * 64 + 64], ALU.mult, ALU.add,
                            [M_b[ch][p], E1_b, bkS_b], [M_b[ch][p]])
                        CPY(g, "act", Mb[ch][hs, p, :], M32[ch][hs, p, :], [M_b[ch][p]], [M_b[ch][p]])
                run_interleaved([head_gen(0), head_gen(1)])
                if stop <= 6:
                    return
                ACTF(g, osb[:, 1, 0:n], osb[:, 0, 0:n], AF.Square, [osb_b], [osb_b])
                bkG, bkG_b = next_bank(g)
                for a_ in range(2):
                    MM(g, bkG[:, a_ * 128:a_ * 128 + n], BDf, osb[:, a_, 0:n], [cb, osb_b], [bkG_b])
                P.op("act", lambda e, n=n, bkG=bkG: e.mul(gm[:, 0:n], bkG[:, 0:n], 1.0 / 64), reads=[bkG_b], writes=[gm_b])
                TTO(g, "pool", gm2[:, 0:n], gm[:, 0:n], gm[:, 0:n], ALU.mult, [gm_b], [gm2_b])
                STT(g, gv[:, 0:n], bkG[:, 128:128 + n], 1.0 / 64, gm2[:, 0:n], ALU.mult, ALU.subtract, [bkG_b, gm2_b], [gv_b])
                ACTF(g, gv[:, 0:n], gv[:, 0:n], AF.Sqrt, [gv_b, cb], [gv_b], bias=g.consts[:, C_EPS_GN:C_EPS_GN + 1])
                P.op("dve", lambda e, n=n: e.reciprocal(out=gv[:, 0:n], in_=gv[:, 0:n]), reads=[gv_b], writes=[gv_b])
                TTO(g, "dve", gm2[:, 0:n], osb[:, 0, 0:n], gm[:, 0:n], ALU.subtract, [osb_b, gm_b, gm2_b], [gm2_b])
                TTO(g, "dve", gm2[:, 0:n], gm2[:, 0:n], gv[:, 0:n], ALU.mult, [gm2_b, gv_b], [gm2_b])
                TSC(g, "dve", gm2[:, 0:n], gm2[:, 0:n], pvc(g, "gn_w", p), pvc(g, "gn_b", p), ALU.mult, ALU.add, [gm2_b, cb], [gm2_b])
                TTO(g, "dve", gm2[:, 0:n], gm2[:, 0:n], bonus[:, 0:n], ALU.add, [gm2_b, bonus_b], [gm2_b])
                TTO(g, "dve", ogT[:, p, 0:n], gm2[:, 0:n], gsb[:, 0:n], ALU.mult, [gm2_b, gsb_b], [og_b[p]])
            for dc in range(NCH):
                bk, bkb = next_bank(g)
                for p in range(8):
                    MM(g, bk[:, 0:n], Wo[:, p, dc * 128:(dc + 1) * 128], ogT[:, p, 0:n], [wb, og_b[p]], [bkb], start=(p == 0), stop=(p == 7))
                xb = xt_bufs_for_cols(g, dc, col0, n)
                ca, cn_ = col0 + real0, realn
                TTO(g, "dve", g.xT[:, dc, ca:ca + cn_], g.xT[:, dc, ca:ca + cn_], bk[:, real0:real0 + realn], ALU.add, xb + [bkb], xb)
            if ti == 16 or sample:
                for (s0, sn, ch) in segs:
                    for p in range(8):
                        bk, bkb = next_bank(g)
                        TRP(g, bk[0:64, 0:128], M32[ch][:, p, :], identf, [M_b[ch][p], cb], [bkb])
                        CPY(g, "act", outstg[:, 2 * p:2 * p + 2, :].rearrange("v h k -> v (h k)"), bk[0:64, 0:128], [bkb], [outstg_b])
                    dst = d["state_wkv_p"] if ch == 0 else d["state_wkv_s"][ch - 1]
                    P.dma("sp", dst.rearrange("h v k -> v h k"), outstg[:], reads=[outstg_b])
        P.barrier()


def sb_attn(g, do_sample=True, pairs=None):
    _sb_attn(g, do_sample, pairs)
    g.P.barrier()


def _sb_attn(g, do_sample=True, pairs=None):
    nc, P, d = g.nc, g.P, g.dram
    stop = getattr(g, "sb_stop", 99)
    cb = g.cb
    pairs = list(range(8)) if pairs is None else pairs
    identb = g.consts_bf[:, C_IDENT:C_IDENT + 128]
    TINCL = g.consts[:, C_TRI_INCL:C_TRI_INCL + 128]
    TCOMP = g.consts[:, C_TRI_COMP:C_TRI_COMP + 128]
    MS1 = g.consts[:, C_M1 + 256:C_M1 + 384]
    MS2 = g.consts[:, C_M2 + 256:C_M2 + 384]
    with contextlib.ExitStack() as es:
        def sb(name, shape, dt):
            return es.enter_context(g_sbuf(nc, "sb_" + name, shape, dt))
        xn, xn_b = rmsnorm_T(g, es, "mix_g1")
        gqk = sb("gqk", [128, 256], F32)
        gqk_b = Buf("gqk")
        P.dma("sp", gqk[:], d["gqk"], writes=[gqk_b])
        wq = [sb("wq%d" % i, [128, NCH, 3, 128], BF16) for i in range(2)]
        wq_b = [Buf("wq") for _ in range(2)]
        wo = [sb("wo%d" % i, [128, D], BF16) for i in range(2)]
        wo_b = [Buf("wo") for _ in range(2)]
        QT = sb("QT", [128, NTOK], BF16)
        KT = sb("KT", [128, NTOK], BF16)
        Vt = sb("Vt", [128, 18, 128], BF16)
        oT = sb("oT", [128, NTOK], BF16)
        QT_b, KT_b, Vt_b = Buf("QT"), Buf("KT"), Buf("Vt")
        oT_b = [Buf("oT") for _ in range(NCT)]
        qksb = sb("qksb", [128, 256], F32)
        qksb_b = Buf("qksb")
        sqf = sb("sqf", [128, 256], F32)
        sqf_b = Buf("sqf")
        ss = sb("ss", [128, 4], F32)
        ss_b = Buf("ss")
        tq = sb("tq", [128, 256], F32)
        tq_b = Buf("tq")
        qkn = [sb("qkn%d" % i, [128, 256], F32) for i in range(2)]
        qkn_b = [Buf("qkn") for _ in range(2)]
        qkb = sb("qkb", [128, 256], BF16)
        qkb_b = Buf("qkb")
        P.op("pool", lambda e: e.memset(qkb[:], 0.0), writes=[qkb_b])
        vf = [sb("vf%d" % i, [128, 128], F32) for i in range(2)]
        vf_b = [Buf("vf") for _ in range(2)]
        ef = [sb("ef%d" % i, [128, 512], F32) for i in range(2)]
        ef_b = [Buf("ef") for _ in range(2)]
        spf = [sb("spf%d" % i, [128, 512], F32) for i in range(2)]
        spf_b = [Buf("spf") for _ in range(2)]
        Xf = [sb("Xf%d" % i, [128, 512], F32) for i in range(2)]
        Xf_b = [Buf("Xf") for _ in range(2)]
        Ab = [sb("Ab%d" % i, [128, 512], BF16) for i in range(2)]
        Ab_b = [Buf("Ab") for _ in range(2)]
        if do_sample:
            Kp = sb("Kp", [128, 32, 2, 64], BF16)
            Vp = sb("Vp", [128, 32, 2, 64], BF16)
            KTp = sb("KTp", [128, PAST], BF16)
            Kp_b, Vp_b, KTp_b = Buf("Kp"), Buf("Vp"), Buf("KTp")
        w_qkv = d["sb_w_qkv"].rearrange("(c p) (t n) -> p c t n", p=128, t=3)

        def load_w(p):
            s = p % 2
            pc = p * 128
            for t_ in range(3):
                P.dma("pool", wq[s][:, :, t_, :], w_qkv[:, :, t_, pc:pc + 128], writes=[wq_b[s]])
            P.dma("pool", wo[s][:], d["sb_w_o"][pc:pc + 128, :], writes=[wo_b[s]])

        stepk = [0]

        reserved = set()

        def free_bank():
            bk_, bb_ = next_bank(g)
            while id(bb_) in reserved:
                bk_, bb_ = next_bank(g)
            return bk_, bb_

        NSL = 8
        efs_b = [[Buf("ef") for _ in range(NSL)] for _ in range(2)]
        sps_b = [[Buf("sp") for _ in range(NSL)] for _ in range(2)]
        Xs_b = [[Buf("X") for _ in range(NSL)] for _ in range(2)]
        As_b = [[Buf("A") for _ in range(NSL)] for _ in range(2)]

        def attn_steps(hh, qsrc_cols, nq, steps, out_dst, out_bufs, nslots=1):
            h0 = hh * 64
            hs = slice(h0, h0 + 64)
            accbk, accb = free_bank()
            reserved.add(id(accb))
            obk, obb = free_bank()
            reserved.add(id(obb))
            i = hh
            for si, (kT_ap, v_ap, nk, lo, mask_ap, rds) in enumerate(steps):
                sl = si % nslots
                so = sl * 64 if nslots > 1 else 0
                e_b, s_b, x_b, a_b = efs_b[i][sl], sps_b[i][sl], Xs_b[i][sl], As_b[i][sl]
                zbk, zbb = free_bank()
                MM(g, zbk[0:nk, lo:nq], kT_ap, QT[hs, qsrc_cols + lo:qsrc_cols + nq], rds + [QT_b], [zbb])
                ACTF(g, ef[i][0:nk, so + lo:so + nq], zbk[0:nk, lo:nq], AF.Exp, [zbb], [e_b], scale=0.125)
                if mask_ap is not None:
                    mw = mask_ap.shape[1]
                    TTO(g, "dve", ef[i][0:nk, so + lo:so + lo + mw], ef[i][0:nk, so + lo:so + lo + mw], mask_ap, ALU.mult, [e_b, cb], [e_b])
                ACTF(g, spf[i][0:nk, so + lo:so + nq], ef[i][0:nk, so + lo:so + nq], AF.Ln, [e_b], [s_b], bias=g.consts[0:nk, C_ONE:C_ONE + 1])
                MM(g, accbk[:, lo:nq], TINCL[0:nk, :], spf[i][0:nk, so + lo:so + nq], [cb, s_b], [accb], start=(si == 0), stop=True, skip=True)
                ACTF(g, Xf[i][0:nk, so + lo:so + nq], accbk[0:nk, lo:nq], AF.Exp, [accb], [x_b], scale=-1.0)
                MM(g, accbk[:, lo:nq], TCOMP[0:nk, :], spf[i][0:nk, so + lo:so + nq], [cb, s_b], [accb], start=False, stop=True, skip=True)
                TTO(g, "dve", Ab[i][0:nk, so + lo:so + nq], ef[i][0:nk, so + lo:so + nq], Xf[i][0:nk, so + lo:so + nq], ALU.mult, [e_b, x_b], [a_b])
                MM(g, obk[hs, lo:nq], v_ap, Ab[i][0:nk, so + lo:so + nq], rds + [a_b], [obb], start=(si == 0), stop=True, skip=True)
                yield
            CPY(g, "act", out_dst, obk[hs, 0:nq], [obb], out_bufs)
            reserved.discard(id(accb))
            reserved.discard(id(obb))

        load_w(pairs[0])
        for pi, p in enumerate(pairs):
            s = p % 2
            if pi + 1 < len(pairs):
                load_w(pairs[pi + 1])
            for ti, (col0, n) in enumerate(TT):
                bk, bkb = next_bank(g)
                xb = []
                for c in range(NCH):
                    xb += [xn_b[c][t] for t, (c0, nn) in enumerate(COLT) if c0 < col0 + n and col0 < c0 + nn]
                for c in range(NCH):
                    MM(g, bk[0:n, 0:384], xn[:, c, col0:col0 + n], wq[s][:, c, :, :].rearrange("p t n -> p (t n)"), xb + [wq_b[s]], [bkb],
                       start=(c == 0), stop=(c == NCH - 1))
                if stop <= 1:
                    return
                CPY(g, "act", qksb[0:n, :], bk[0:n, 0:256], [bkb], [qksb_b])
                ACTF(g, sqf[0:n, :], qksb[0:n, :], AF.Square, [qksb_b], [sqf_b])
                P.op("dve", lambda e, n=n: e.tensor_reduce(out=ss[0:n, :], in_=sqf[0:n, :].rearrange("p (a b) -> p a b", a=4), axis=AX.X, op=ALU.add),
                     reads=[sqf_b], writes=[ss_b])
                ACTF(g, ss[0:n, :], ss[0:n, :], AF.Sqrt, [ss_b, cb], [ss_b], bias=g.consts[0:n, C_EPS_RMS:C_EPS_RMS + 1], scale=1.0 / 64)
                P.op("dve", lambda e, n=n: e.reciprocal(out=ss[0:n, :], in_=ss[0:n, :]), reads=[ss_b], writes=[ss_b])
                if stop <= 2:
                    return
                for a_ in range(4):
                    STT(g, tq[0:n, a_ * 64:(a_ + 1) * 64], qksb[0:n, a_ * 64:(a_ + 1) * 64], ss[0:n, a_:a_ + 1], gqk[0:n, a_ * 64:(a_ + 1) * 64],
                        ALU.mult, ALU.mult, [qksb_b, ss_b, gqk_b], [tq_b])
                if stop <= 3:
                    return
                j = ti % 2
                CPY(g, "pool", qkn[j][0:n, :], tq[0:n, :], [tq_b], [qkn_b[j]])
                CPY(g, "act", qkb[0:n, :], tq[0:n, :], [tq_b], [qkb_b])
                CPY(g, "act", vf[j][0:n, :], bk[0:n, 256:384], [bkb], [vf_b[j]])
                CPY(g, "pool", Vt[0:n, ti, :], vf[j][0:n, :], [vf_b[j]], [Vt_b])
                if stop <= 4:
                    return
                bkT, bkT_b = next_bank(g)
                bkT16 = bkT[:].bitcast(BF16)
                TRP(g, bkT16[:, 0:128], qkb[:, 0:128], identb, [qkb_b, cb], [bkT_b])
                TRP(g, bkT16[:, 128:256], qkb[:, 128:256], identb, [qkb_b, cb], [bkT_b])
                CPY(g, "act", QT[:, col0:col0 + n], bkT16[:, 0:n], [bkT_b], [QT_b])
                CPY(g, "act", KT[:, col0:col0 + n], bkT16[:, 128:128 + n], [bkT_b], [KT_b])
                if stop <= 5:
                    return
                if ti <= 16:
                    kd = d["cache_k_p"][2 * p:2 * p + 2, col0:col0 + n, :].rearrange("h t d -> t h d")
                    vd = d["cache_v_p"][2 * p:2 * p + 2, col0:col0 + n, :].rearrange("h t d -> t h d")
                    P.dma("sp", kd, qkn[j][0:n, 128:256].rearrange("p (h d) -> p h d", h=2), reads=[qkn_b[j]])
                    P.dma("sp", vd, vf[j][0:n, :].rearrange("p (h d) -> p h d", h=2), reads=[vf_b[j]])
                else:
                    for sq_ in range(2):
                        r0 = sq_ * 64
                        kd = d["cache_k_s"][sq_, 2 * p:2 * p + 2, :, :].rearrange("h t d -> t h d")
                        vd = d["cache_v_s"][sq_, 2 * p:2 * p + 2, :, :].rearrange("h t d -> t h d")
                        P.dma("sp", kd, qkn[j][r0:r0 + 64, 128:256].rearrange("p (h d) -> p h d", h=2), reads=[qkn_b[j]])
                        P.dma("sp", vd, vf[j][r0:r0 + 64, :].rearrange("p (h d) -> p h d", h=2), reads=[vf_b[j]])
                if stop <= 5.5 or (stop <= 5.7 and ti == 1):
                    return
            if stop <= 6:
                return
            if not do_sample:
                P.op("pool", lambda e: e.memset(oT[:, TP:NTOK], 0.0), writes=[oT_b[NCT - 1]])
            chunks = [(0, 0, 0)] + [(1 + 4 * i, 4 + 4 * i, 1) for i in range(4)]
            for (t_a, t_b, _) in chunks:
                gens = []
                for hh in range(2):
                    h0 = hh * 64
                    hs = slice(h0, h0 + 64)
                    qc0 = TT[t_a][0]
                    nq = TT[t_b][0] + TT[t_b][1] - qc0
                    steps = []
                    for kb in range(t_b, -1, -1):
                        k0, nk = TT[kb]
                        if kb >= t_a:
                            lo = k0 - qc0
                            mask = MS1[0:nk, 0:nk]
                        else:
                            lo = 0
                            mask = None
                        steps.append((KT[hs, k0:k0 + nk], Vt[0:nk, kb, h0:h0 + 64], nk, lo, mask, [KT_b, Vt_b]))
                    ob_ = [oT_b[t] for t, (c0, nn) in enumerate(COLT) if c0 < qc0 + nq and qc0 < c0 + nn]
                    gens.append(attn_steps(hh, qc0, nq, steps, oT[hs, qc0:qc0 + nq], ob_))
                run_interleaved(gens)
                if stop <= 7:
                    return
            if do_sample:
                for sq_ in range(2):
                    for h_ in range(2):
                        P.dma("pool", Kp[:, :, h_, :], d["cache_k_in"][sq_, 2 * p + h_, :, :].rearrange("(t k) d -> k t d", k=128), writes=[Kp_b])
                        P.dma("pool", Vp[:, :, h_, :], d["cache_v_in"][sq_, 2 * p + h_, :, :].rearrange("(t k) d -> k t d", k=128), writes=[Vp_b])
                    for t8 in range(4):
                        bkT, bkT_b = next_bank(g)
                        bkT16 = bkT[:].bitcast(BF16)
                        for j in range(8):
                            t = t8 * 8 + j
                            TRP(g, bkT16[:, j * 128:(j + 1) * 128], Kp[:, t, :, :].rearrange("p h d -> p (h d)"), identb, [Kp_b, cb], [bkT_b])
                        CPY(g, "act", KTp[:, t8 * 1024:(t8 + 1) * 1024], bkT16[:, 0:1024], [bkT_b], [KTp_b])
                    gens = []
                    for hh in range(2):
                        h0 = hh * 64
                        hs = slice(h0, h0 + 64)
                        qc0 = TP + sq_ * 64
                        steps = [(KT[hs, TP:TP + 128], Vt[:, 17, h0:h0 + 64], 128, 0, MS2[:, sq_ * 64:sq_ * 64 + 64], [KT_b, Vt_b])]
                        for t in range(31, -1, -1):
                            steps.append((KTp[hs, t * 128:(t + 1) * 128], Vp[:, t, hh, :], 128, 0, None, [KTp_b, Vp_b]))
                        gens.append(attn_steps(hh, qc0, 64, steps, oT[hs, qc0:qc0 + 64], [oT_b[NCT - 1]], nslots=NSL))
                    run_interleaved(gens)
            for dc in range(NCH):
                for t, (c0, n) in enumerate(COLT):
                    bk, bkb = next_bank(g)
                    MM(g, bk[:, 0:n], wo[s][:, dc * 128:(dc + 1) * 128], oT[:, c0:c0 + n], [wo_b[s], oT_b[t]], [bkb])
                    TTO(g, "dve", g.xT[:, dc, c0:c0 + n], g.xT[:, dc, c0:c0 + n], bk[:, 0:n], ALU.add, [g.xT_b[dc][t], bkb], [g.xT_b[dc][t]])
    P.barrier()


def make_consts():
    c = np.zeros((128, NCONST), np.float32)
    c[:, C_IDENT:C_IDENT + 128] = np.eye(128, dtype=np.float32)
    c[:, C_ONES:C_ONES + 128] = 1.0
    kp = np.arange(128)[:, None]
    k = np.arange(128)[None, :]
    c[:, C_TRI_INCL:C_TRI_INCL + 128] = (kp >= k).astype(np.float32)
    c[:, C_TRI_COMP:C_TRI_COMP + 128] = (kp < k).astype(np.float32)
    c[:, C_EPS_RMS] = RMS_EPS
    c[:, C_ONE] = 1.0
    pp = np.arange(128)
    c[:, C_BD:C_BD + 128] = (pp[:, None] // 64 == pp[None, :] // 64).astype(np.float32)
    for ofs, seg in ((C_M1, np.zeros(128, int)), (C_M2, pp // 64)):
        same = (seg[:, None] == seg[None, :])
        s_lt_t = ((pp[:, None] < pp[None, :]) & same).astype(np.float32)
        s_le_t = ((pp[:, None] <= pp[None, :]) & same).astype(np.float32)
        c[:, ofs:ofs + 128] = -s_lt_t
        c[:, ofs + 128:ofs + 256] = -s_lt_t.T
        c[:, ofs + 256:ofs + 384] = s_lt_t
        c[:, ofs + 384:ofs + 512] = s_le_t
    c[:, C_EPS_GN] = GN_EPS
    return c


def fm(vec):
    return np.ascontiguousarray(np.asarray(vec, np.float32).reshape(NCH, 128).T)


def make_pvec(inp):
    cols = [fm(inp["ffn_norm_g"][0, 0]), fm(inp["ffn_norm_g"][0, 1]), fm(inp["ffn_norm_g"][1, 0]), fm(inp["ffn_norm_g"][1, 1]),
            fm(inp["mix_norm_g"][0]), fm(inp["mix_norm_g"][1])]
    for i in range(6):
        cols.append(fm(inp["rwkv_mu"][i]))
    for nm in ("rwkv_w0", "rwkv_a0", "rwkv_k_k", "rwkv_k_a", "rwkv_r_k", "rwkv_gn_w", "rwkv_gn_b"):
        cols.append(fm(np.asarray(inp[nm]).reshape(-1)))
    return np.ascontiguousarray(np.concatenate(cols, axis=1))


ALL_STAGES = {("ffn", 0, 0), ("ffn", 0, 1), ("ffn", 1, 0), ("ffn", 1, 1), "rwkv", "sb"}
_NC_CACHE = {}


def make_in_maps(inputs, ncores=8):
    consts = make_consts()
    pvec = make_pvec(inputs)
    f = lambda a: np.ascontiguousarray(np.asarray(a, np.float32))
    in_maps = []
    gq = np.asarray(inputs["sb_q_norm_g"], np.float32)
    gk = np.asarray(inputs["sb_k_norm_g"], np.float32)
    gqk = np.ascontiguousarray(np.broadcast_to(np.concatenate([gq, gq, gk, gk])[None, :], (128, 256)))
    for i in range(ncores):
        in_maps.append({
            "x_prompt": f(inputs["x_prompt"][i]),
            "x_sample": f(inputs["x_sample"][2 * i:2 * i + 2]).reshape(2 * DSEQ, D),
            "meta_tokens": f(inputs["meta_tokens"]),
            "consts": consts,
            "pvec": pvec,
            "ffn_w_in": f(inputs["ffn_w_in"]),
            "ffn_w_out": f(inputs["ffn_w_out"]),
            "rwkv_w_rkv": f(inputs["rwkv_w_rkv"]), "rwkv_w_o": f(inputs["rwkv_w_o"]),
            "rwkv_w1": f(inputs["rwkv_w1"]), "rwkv_w2": f(inputs["rwkv_w2"]),
            "rwkv_a1": f(inputs["rwkv_a1"]), "rwkv_a2": f(inputs["rwkv_a2"]),
            "rwkv_g1": f(inputs["rwkv_g1"]), "rwkv_g2": f(inputs["rwkv_g2"]),
            "shift_in": np.ascontiguousarray(np.concatenate([fm(inputs["state_rwkv_shift"][2 * i]), fm(inputs["state_rwkv_shift"][2 * i + 1])], 1)),
            "state_wkv_in": f(inputs["state_rwkv_wkv"][2 * i:2 * i + 2]),
            "sb_w_qkv": f(inputs["sb_w_qkv"]), "sb_w_o": f(inputs["sb_w_o"]),
            "gqk": gqk,
        })
        if "cache_sb_k" in inputs:
            in_maps[-1]["cache_k_in"] = f(inputs["cache_sb_k"][2 * i:2 * i + 2])
            in_maps[-1]["cache_v_in"] = f(inputs["cache_sb_v"][2 * i:2 * i + 2])
    return in_maps


def run(inputs, stages=None, ncores=8, trace=False):
    stages = ALL_STAGES if stages is None else stages
    key = tuple(sorted(stages, key=repr))
    if key not in _NC_CACHE:
        _NC_CACHE[key] = build(stages)
    nc = _NC_CACHE[key]
    in_maps = make_in_maps(inputs, ncores)
    res = run_bass_kernel_spmd(nc, in_maps, core_ids=list(range(ncores)), trace=trace)
    if trace:
        print('exec_time_ns', res.exec_time_ns)
    return res.results


def kernel(**inputs):
    r = run(inputs)
    f32 = np.float32
    y_prompt = np.stack([r[i]["y_prompt"] for i in range(8)], 0).astype(f32)
    y_sample = np.concatenate([r[i]["y_sample"].reshape(2, DSEQ, D) for i in range(8)], 0).astype(f32)
    S_p = np.stack([r[i]["state_wkv_p"] for i in range(8)], 0).astype(f32)
    sh_p = np.stack([r[i]["shift_p"].reshape(D) for i in range(8)], 0).astype(f32)
    k_p = np.stack([r[i]["cache_k_p"] for i in range(8)], 0).astype(f32)
    v_p = np.stack([r[i]["cache_v_p"] for i in range(8)], 0).astype(f32)
    S_s = np.concatenate([r[i]["state_wkv_s"] for i in range(8)], 0).astype(f32)
    sh_s = np.concatenate([r[i]["shift_s"].reshape(2, D) for i in range(8)], 0).astype(f32)
    k_s = np.concatenate([r[i]["cache_k_s"] for i in range(8)], 0).astype(f32)
    v_s = np.concatenate([r[i]["cache_v_s"] for i in range(8)], 0).astype(f32)
    return (y_prompt, y_sample, S_p, sh_p, k_p, v_p, S_s, sh_s, k_s, v_s)
```

```python
import contextlib
import numpy as np
import concourse.bass as bass
import concourse.mybir as mybir
from concourse.bass_utils import run_bass_kernel_spmd

F32 = mybir.dt.float32
BF16 = mybir.dt.bfloat16
AF = mybir.ActivationFunctionType
ALU = mybir.AluOpType
AX = mybir.AxisListType

D = 1024
NCH = 8
SEQ = 2048
NMETA = 16
TP = NMETA + SEQ
DSEQ = 64
NTOK = TP + 2 * DSEQ
DFF = 2752
H = 16
N = 64
PAST = 4096
RMS_EPS = 1e-6
GN_EPS = 64e-5

PV = {}
_pv_names = ["ffn_g00", "ffn_g01", "ffn_g10", "ffn_g11", "mix_g0", "mix_g1",
             "mu0", "mu1", "mu2", "mu3", "mu4", "mu5", "w0", "a0", "k_k", "k_a", "r_k", "gn_w", "gn_b"]
for _i, _n in enumerate(_pv_names):
    PV[_n] = _i
NPV = len(_pv_names)

C_IDENT = 0
C_ONES = 128
C_TRI_INCL = 256
C_TRI_COMP = 384
C_EPS_RMS = 512
C_EPS_GN = 513
C_BD = 520
C_M1 = 648
C_M2 = 1160
C_ONE = 514
NCONST = 1672


_UN = [0]


def g_sbuf(nc, name, shape, dt):
    _UN[0] += 1
    return nc.sbuf_tensor("%s_u%d" % (name, _UN[0]), shape, dt)


class Buf:
    __slots__ = ("name", "w", "rs")

    def __init__(self, name=""):
        self.name = name
        self.w = None
        self.rs = {}


class Sched:
    ENG = ("pe", "dve", "act", "pool", "sp")

    def __init__(self, nc, ring=12):
        self.nc = nc
        self.q = {e: [] for e in self.ENG}
        self.cnt = {e: 0 for e in self.ENG}
        self.seen = {e: {} for e in self.ENG}
        self.sems = {}
        for e in self.ENG:
            self.sems[e] = nc.alloc_semaphore("c_" + e)
        self.ring = ring
        self.dma_n = {}
        self.dma_last = {}
        for qn in ("sp", "pool", "act"):
            self.dma_n[qn] = 0
            for s in range(ring):
                self.sems[("dma", qn, s)] = nc.alloc_semaphore("d_%s_%d" % (qn, s))
        self.ninstr = 0

    def _wait(self, eng, key, val):
        if self.seen[eng].get(key, 0) >= val:
            return
        self.seen[eng][key] = val
        self.q[eng].append(("w", key, val))
        self.ninstr += 1

    def _deps(self, eng, reads, writes):
        deps = {}
        for b in reads:
            if b.w is not None:
                k, v = b.w
                if deps.get(k, 0) < v:
                    deps[k] = v
        for b in writes:
            if b.w is not None:
                k, v = b.w
                if deps.get(k, 0) < v:
                    deps[k] = v
            for k, v in b.rs.items():
                if deps.get(k, 0) < v:
                    deps[k] = v
        for k, v in deps.items():
            if eng == "pe" and k == "pe":
                continue
            self._wait(eng, k, v)

    def _mark(self, tok, reads, writes):
        k, v = tok
        for b in reads:
            if b.rs.get(k, 0) < v:
                b.rs[k] = v
        for b in writes:
            b.w = tok
            b.rs = {}

    def op(self, eng, fn, reads=(), writes=()):
        self._deps(eng, reads, writes)
        self.cnt[eng] += 1
        tok = (eng, self.cnt[eng])
        self.q[eng].append(("op", fn, eng, 1))
        self.ninstr += 1
        self._mark(tok, reads, writes)
        return tok

    def dma(self, qn, out, in_, reads=(), writes=(), **kw):
        self._deps(qn, reads, writes)
        n = self.dma_n[qn]
        s = n % self.ring
        val = 16 * (n // self.ring + 1)
        key = ("dma", qn, s)
        if n >= self.ring:
            self._wait(qn, key, val - 16)
        self.dma_n[qn] = n + 1
        fn = lambda e, out=out, in_=in_, kw=kw: e.dma_start(out=out, in_=in_, **kw)
        self.q[qn].append(("op", fn, key, 16))
        self.ninstr += 1
        tok = (key, val)
        self.dma_last[key] = val
        self._mark(tok, reads, writes)
        return tok

    def barrier(self):
        toks = [(e, self.cnt[e]) for e in self.ENG if self.cnt[e] > 0]
        toks += list(self.dma_last.items())
        for e in self.ENG:
            for k, v in toks:
                if k == e and e == "pe":
                    continue
                self._wait(e, k, v)

    def finish(self):
        for k, v in self.dma_last.items():
            self._wait("sp", k, v)
        for e in self.ENG:
            if e != "sp" and self.cnt[e] > 0:
                self._wait("sp", e, self.cnt[e])

    def replay(self, block):
        sems = self.sems

        def mk(name):
            items = self.q[name]

            def body(e):
                for it in items:
                    if it[0] == "w":
                        e.wait_ge(sems[it[1]], it[2])
                    else:
                        ins = it[1](e)
                        ins.then_inc(sems[it[2]], it[3])
            return body

        block.tensor(mk("pe"))
        block.vector(mk("dve"))
        block.scalar(mk("act"))
        block.gpsimd(mk("pool"))
        block.sync(mk("sp"))


def col_tiles():
    t = []
    c = 0
    while c < NTOK:
        n = min(512, NTOK - c)
        t.append((c, n))
        c += n
    return t


COLT = col_tiles()
NCT = len(COLT)
TT = [(0, NMETA)] + [(NMETA + 128 * i, 128) for i in range(16)] + [(TP, 128)]


class Ctx:
    pass


def build(stages):
    nc = bass.Bass("TRN2", target_bir_lowering=False)
    P = Sched(nc)
    g = Ctx()
    g.nc, g.P = nc, P
    g.rwkv_tiles = None
    g.do_sample = "nosample" not in stages
    g.sb_pairs = None
    for st in stages:
        if isinstance(st, tuple) and st[0] == "sb_pairs":
            g.sb_pairs = list(st[1])
    for st in stages:
        if isinstance(st, tuple) and st[0] == "rwkv_tiles":
            g.rwkv_tiles = list(st[1])
        if isinstance(st, tuple) and st[0] == "sb_stop":
            g.sb_stop = float(st[1])
        if isinstance(st, tuple) and st[0] == "rwkv_stop":
            g.rwkv_stop = float(st[1])
    dram = {}

    def din(name, shape, dt=F32):
        dram[name] = nc.dram_tensor(name, list(shape), dt, kind="ExternalInput").ap()
        return dram[name]

    def dout(name, shape, dt=F32):
        dram[name] = nc.dram_tensor(name, list(shape), dt, kind="ExternalOutput").ap()
        return dram[name]

    g.dram = dram
    din("x_prompt", (SEQ, D))
    din("x_sample", (2 * DSEQ, D))
    din("meta_tokens", (NMETA, D))
    din("consts", (128, NCONST))
    din("pvec", (128, NPV * 8))
    din("ffn_w_in", (2, 2, D, 2 * DFF))
    din("ffn_w_out", (2, 2, DFF, D))
    din("rwkv_w_rkv", (3, D, D))
    din("rwkv_w_o", (D, D))
    din("rwkv_w1", (D, 64))
    din("rwkv_w2", (64, D))
    din("rwkv_a1", (D, 64))
    din("rwkv_a2", (64, D))
    din("rwkv_g1", (D, 128))
    din("rwkv_g2", (128, D))
    din("sb_w_qkv", (D, 3 * D))
    din("sb_w_o", (D, D))
    din("gqk", (128, 256))
    if g.do_sample:
        din("cache_k_in", (2, H, PAST, N))
        din("cache_v_in", (2, H, PAST, N))
    dout("cache_k_p", (H, TP, N))
    dout("cache_v_p", (H, TP, N))
    dout("cache_k_s", (2, H, DSEQ, N))
    dout("cache_v_s", (2, H, DSEQ, N))
    din("shift_in", (128, 16))
    din("state_wkv_in", (2, H, N, N))
    dout("state_wkv_p", (H, N, N))
    dout("shift_p", (8, 128))
    dout("state_wkv_s", (2, H, N, N))
    dout("shift_s", (2, 8, 128))
    dout("y_prompt", (SEQ, D))
    dout("y_sample", (2 * DSEQ, D))

    g.xT = nc.alloc_sbuf_tensor("xT", [128, NCH, NTOK], F32)
    g.xT_b = [[Buf("xT") for _ in range(NCT)] for _ in range(NCH)]
    g.consts = nc.alloc_sbuf_tensor("consts_sb", [128, NCONST], F32)
    g.consts_bf = nc.alloc_sbuf_tensor("consts_bf", [128, 256], BF16)
    g.pvec = nc.alloc_sbuf_tensor("pvec_sb", [128, NPV * 8], F32)
    g.cb = Buf("consts")
    g.banks = [nc.alloc_psum_tensor("bank%d" % i, [128, 512], F32) for i in range(8)]
    g.bank_b = [Buf("bank%d" % i) for i in range(8)]
    g.bank_rr = 0

    P.dma("sp", g.consts[:], dram["consts"], writes=[g.cb])
    P.dma("sp", g.pvec[:], dram["pvec"], writes=[g.cb])
    P.op("dve", lambda e: e.tensor_copy(out=g.consts_bf[:], in_=g.consts[:, 0:256]), reads=[g.cb], writes=[g.cb])

    load_x(g)
    for li in range(2):
        if ("ffn", li, 0) in stages:
            ffn(g, li, 0)
        if li == 0 and "rwkv" in stages:
            rwkv(g, tiles=g.rwkv_tiles)
        if li == 1 and "sb" in stages:
            sb_attn(g, do_sample=g.do_sample, pairs=g.sb_pairs)
        if ("ffn", li, 1) in stages:
            ffn(g, li, 1)
    store_y(g)
    P.finish()
    with nc.Block() as block:
        P.replay(block)
    print("instructions:", P.ninstr)
    return nc


def xt_bufs_for_cols(g, c, col0, n):
    out = []
    for t, (c0, nn) in enumerate(COLT):
        if c0 < col0 + n and col0 < c0 + nn:
            out.append(g.xT_b[c][t])
    return out


def load_x(g):
    nc, P = g.nc, g.P
    ident = g.consts[:, C_IDENT:C_IDENT + 128]
    with contextlib.ExitStack() as es:
        stg = [es.enter_context(g_sbuf(nc, "ldstg%d" % i, [128, D], F32)) for i in range(3)]
        stg_b = [Buf("ldstg") for _ in range(3)]
        for ti, (col0, n) in enumerate(TT):
            s = ti % 3
            if ti == 0:
                src = g.dram["meta_tokens"]
            elif ti <= 16:
                src = g.dram["x_prompt"][(ti - 1) * 128:ti * 128, :]
            else:
                src = g.dram["x_sample"]
            P.dma("sp", stg[s][0:n, :], src, writes=[stg_b[s]])
            for half in range(2):
                bk = (ti * 2 + half) % 2 + 6
                bank = g.banks[bk]
                for cc in range(4):
                    c = half * 4 + cc
                    P.op("pe", lambda e, bank=bank, cc=cc, c=c, s=s, n=n: e.transpose(
                        out=bank[:, cc * 128:cc * 128 + n], in_=stg[s][0:n, c * 128:(c + 1) * 128],
                        identity=ident[0:n, 0:n]),
                        reads=[stg_b[s], g.cb], writes=[g.bank_b[bk]])
                wb = []
                for cc in range(4):
                    wb += xt_bufs_for_cols(g, half * 4 + cc, col0, n)
                src_ap = bank[:].rearrange("p (a b) -> p a b", a=4)[:, :, 0:n]
                dst_ap = g.xT[:, half * 4:half * 4 + 4, col0:col0 + n]
                eng = "act" if half == 0 else "dve"
                if eng == "act":
                    P.op("act", lambda e, d=dst_ap, s_=src_ap: e.copy(out=d, in_=s_), reads=[g.bank_b[bk]], writes=wb)
                else:
                    P.op("dve", lambda e, d=dst_ap, s_=src_ap: e.tensor_copy(out=d, in_=s_), reads=[g.bank_b[bk]], writes=wb)
        P.barrier()


def store_y(g):
    nc, P = g.nc, g.P
    ident = g.consts[:, C_IDENT:C_IDENT + 128]
    with contextlib.ExitStack() as es:
        stg = [es.enter_context(g_sbuf(nc, "ststg%d" % i, [128, D], F32)) for i in range(3)]
        stg_b = [Buf("ststg") for _ in range(3)]
        k = 0
        for ti, (col0, n) in enumerate(TT):
            if ti == 0:
                continue
            s = k % 3
            k += 1
            for half in range(2):
                bk = (ti * 2 + half) % 2 + 6
                bank = g.banks[bk]
                for cc in range(4):
                    c = half * 4 + cc
                    P.op("pe", lambda e, bank=bank, cc=cc, c=c, col0=col0, n=n: e.transpose(
                        out=bank[0:n, cc * 128:(cc + 1) * 128], in_=g.xT[:, c, col0:col0 + n],
                        identity=ident),
                        reads=xt_bufs_for_cols(g, c, col0, n) + [g.cb], writes=[g.bank_b[bk]])
                dst_ap = stg[s][0:n, half * 512:(half + 1) * 512]
                src_ap = bank[0:n, :]
                if half == 0:
                    P.op("act", lambda e, d=dst_ap, s_=src_ap: e.copy(out=d, in_=s_), reads=[g.bank_b[bk]], writes=[stg_b[s]])
                else:
                    P.op("dve", lambda e, d=dst_ap, s_=src_ap: e.tensor_copy(out=d, in_=s_), reads=[g.bank_b[bk]], writes=[stg_b[s]])
            if ti <= 16:
                dst = g.dram["y_prompt"][(ti - 1) * 128:ti * 128, :]
            else:
                dst = g.dram["y_sample"]
            P.dma("sp", dst, stg[s][0:n, :], reads=[stg_b[s]])
        P.barrier()


def rmsnorm_T(g, es, gname, out_dt=BF16):
    nc, P = g.nc, g.P
    xn = es.enter_context(g_sbuf(nc, "xn", [128, NCH, NTOK], out_dt))
    xn_b = [[Buf("xn") for _ in range(NCT)] for _ in range(NCH)]
    sq = [es.enter_context(g_sbuf(nc, "sq%d" % i, [128, 512], BF16)) for i in range(2)]
    sq_b = [Buf("sq") for _ in range(2)]
    rt = [es.enter_context(g_sbuf(nc, "rt%d" % i, [128, 512], F32)) for i in range(2)]
    rt_b = [Buf("rt") for _ in range(2)]
    ones = g.consts_bf[:, C_ONES:C_ONES + 128]
    gcol = PV[gname] * 8
    k = 0
    for t, (c0, n) in enumerate(COLT):
        bk = 6 + (t % 2)
        bank = g.banks[bk]
        for c in range(NCH):
            s = k % 2
            k += 1
            P.op("act", lambda e, s=s, c=c, c0=c0, n=n: e.activation(out=sq[s][:, 0:n], in_=g.xT[:, c, c0:c0 + n], func=AF.Square),
                 reads=[g.xT_b[c][t]], writes=[sq_b[s]])
            P.op("pe", lambda e, s=s, c=c, n=n, bank=bank: e.matmul(bank[:, 0:n], lhsT=ones, rhs=sq[s][:, 0:n], start=(c == 0), stop=(c == NCH - 1)),
                 reads=[sq_b[s], g.cb], writes=[g.bank_b[bk]])
        r = t % 2
        P.op("act", lambda e, r=r, n=n, bank=bank: e.activation(out=rt[r][:, 0:n], in_=bank[:, 0:n], func=AF.Sqrt, scale=1.0 / D, bias=g.consts[:, C_EPS_RMS:C_EPS_RMS + 1]),
             reads=[g.bank_b[bk], g.cb], writes=[rt_b[r]])
        P.op("dve", lambda e, r=r, n=n: e.reciprocal(out=rt[r][:, 0:n], in_=rt[r][:, 0:n]), reads=[rt_b[r]], writes=[rt_b[r]])
        for c in range(NCH):
            P.op("dve", lambda e, r=r, c=c, c0=c0, n=n: e.scalar_tensor_tensor(
                out=xn[:, c, c0:c0 + n], in0=g.xT[:, c, c0:c0 + n], scalar=g.pvec[:, gcol + c:gcol + c + 1],
                in1=rt[r][:, 0:n], op0=ALU.mult, op1=ALU.mult),
                reads=[g.xT_b[c][t], rt_b[r], g.cb], writes=[xn_b[c][t]])
    return xn, xn_b


def ffn(g, li, fi):
    nc, P = g.nc, g.P
    w_in = g.dram["ffn_w_in"][li, fi]
    w_out = g.dram["ffn_w_out"][li, fi]
    GS = 4
    groups = []
    j = 0
    while j < 22:
        groups.append(list(range(j, min(j + GS, 22))))
        j += GS
    csize = lambda j: 128 if j < 21 else 64
    with contextlib.ExitStack() as es:
        xn, xn_b = rmsnorm_T(g, es, "ffn_g%d%d" % (li, fi))
        act = [es.enter_context(g_sbuf(nc, "act%d" % i, [128, GS, NTOK], BF16)) for i in range(2)]
        act_b = [[[Buf("act") for _ in range(NCT)] for _ in range(GS)] for _ in range(2)]
        wi = [es.enter_context(g_sbuf(nc, "wi%d" % i, [128, NCH, 2, GS * 128], BF16)) for i in range(2)]
        wi_b = [Buf("wi") for _ in range(2)]
        wo = [es.enter_context(g_sbuf(nc, "wo%d" % i, [128, GS, D], BF16)) for i in range(2)]
        wo_b = [Buf("wo") for _ in range(2)]
        sl = [es.enter_context(g_sbuf(nc, "sl%d" % i, [128, 512], F32)) for i in range(2)]
        sl_b = [Buf("sl") for _ in range(2)]
        w_in_v = w_in.rearrange("(c p) n -> p c n", p=128)

        def load_w(gi):
            grp = groups[gi]
            s = gi % 2
            col0 = grp[0] * 128
            ncols = sum(csize(j) for j in grp)
            P.dma("pool", wi[s][:, :, 0, 0:ncols], w_in_v[:, :, col0:col0 + ncols], writes=[wi_b[s]])
            P.dma("pool", wi[s][:, :, 1, 0:ncols], w_in_v[:, :, DFF + col0:DFF + col0 + ncols], writes=[wi_b[s]])
            nfull = sum(1 for j in grp if csize(j) == 128)
            if nfull:
                P.dma("pool", wo[s][:, 0:nfull, :],
                      w_out[col0:col0 + nfull * 128, :].rearrange("(g p) n -> p g n", p=128), writes=[wo_b[s]])
            if nfull < len(grp):
                r0 = col0 + nfull * 128
                P.dma("pool", wo[s][0:64, nfull, :], w_out[r0:r0 + 64, :], writes=[wo_b[s]])

        kk = [0]

        def phase_a(gi):
            grp = groups[gi]
            s = gi % 2
            for jj, j in enumerate(grp):
                m = csize(j)
                for t, (c0, n) in enumerate(COLT):
                    q = kk[0] % 2
                    kk[0] += 1
                    bg, bu = g.banks[q * 2], g.banks[q * 2 + 1]
                    for c in range(NCH):
                        P.op("pe", lambda e, bg=bg, c=c, jj=jj, m=m, c0=c0, n=n, s=s: e.matmul(
                            bg[0:m, 0:n], lhsT=wi[s][:, c, 0, jj * 128:jj * 128 + m], rhs=xn[:, c, c0:c0 + n],
                            start=(c == 0), stop=(c == NCH - 1)),
                            reads=[wi_b[s], xn_b[c][t]], writes=[g.bank_b[q * 2]])
                    for c in range(NCH):
                        P.op("pe", lambda e, bu=bu, c=c, jj=jj, m=m, c0=c0, n=n, s=s: e.matmul(
                            bu[0:m, 0:n], lhsT=wi[s][:, c, 1, jj * 128:jj * 128 + m], rhs=xn[:, c, c0:c0 + n],
                            start=(c == 0), stop=(c == NCH - 1)),
                            reads=[wi_b[s], xn_b[c][t]], writes=[g.bank_b[q * 2 + 1]])
                    P.op("act", lambda e, q=q, bg=bg, m=m, n=n: e.activation(out=sl[q][0:m, 0:n], in_=bg[0:m, 0:n], func=AF.Silu),
                         reads=[g.bank_b[q * 2]], writes=[sl_b[q]])
                    P.op("dve", lambda e, q=q, bu=bu, m=m, n=n, s=s, jj=jj, c0=c0: e.tensor_tensor(
                        out=act[s][0:m, jj, c0:c0 + n], in0=sl[q][0:m, 0:n], in1=bu[0:m, 0:n], op=ALU.mult),
                        reads=[sl_b[q], g.bank_b[q * 2 + 1]], writes=[act_b[s][jj][t]])

        ko = [0]

        def phase_b(gi):
            grp = groups[gi]
            s = gi % 2
            for dc in range(NCH):
                for t, (c0, n) in enumerate(COLT):
                    bk = 4 + ko[0] % 2
                    ko[0] += 1
                    bank = g.banks[bk]
                    for jj, j in enumerate(grp):
                        m = csize(j)
                        P.op("pe", lambda e, bank=bank, jj=jj, m=m, dc=dc, c0=c0, n=n, s=s: e.matmul(
                            bank[:, 0:n], lhsT=wo[s][0:m, jj, dc * 128:(dc + 1) * 128], rhs=act[s][0:m, jj, c0:c0 + n],
                            start=(jj == 0), stop=(jj == len(grp) - 1)),
                            reads=[wo_b[s], act_b[s][jj][t]], writes=[g.bank_b[bk]])
                    P.op("dve", lambda e, bank=bank, dc=dc, c0=c0, n=n: e.scalar_tensor_tensor(
                        out=g.xT[:, dc, c0:c0 + n], in0=bank[:, 0:n], scalar=0.5, in1=g.xT[:, dc, c0:c0 + n],
                        op0=ALU.mult, op1=ALU.add),
                        reads=[g.bank_b[bk], g.xT_b[dc][t]], writes=[g.xT_b[dc][t]])

        ng = len(groups)
        load_w(0)
        load_w(1)
        phase_a(0)
        for gi in range(ng):
            if gi + 1 < ng:
                phase_a(gi + 1)
            phase_b(gi)
            if gi + 2 < ng:
                load_w(gi + 2)
        P.barrier()


def MM(g, out, lhsT, rhs, r, w, start=True, stop=True, skip=False):
    return g.P.op("pe", lambda e: e.matmul(out, lhsT=lhsT, rhs=rhs, start=start, stop=stop, skip_group_check=skip), reads=r, writes=w)


def TRP(g, out, in_, ident, r, w):
    return g.P.op("pe", lambda e: e.transpose(out=out, in_=in_, identity=ident), reads=r, writes=w)


def ACTF(g, out, in_, func, r, w, bias=None, scale=None):
    kw = {}
    if bias is not None:
        kw["bias"] = bias
    if scale is not None:
        kw["scale"] = scale
    return g.P.op("act", lambda e: e.activation(out=out, in_=in_, func=func, **kw), reads=r, writes=w)


def CPY(g, eng, out, in_, r, w):
    if eng == "act":
        return g.P.op("act", lambda e: e.copy(out=out, in_=in_), reads=r, writes=w)
    return g.P.op(eng, lambda e: e.tensor_copy(out=out, in_=in_), reads=r, writes=w)


def TTO(g, eng, out, in0, in1, op, r, w):
    return g.P.op(eng, lambda e: e.tensor_tensor(out=out, in0=in0, in1=in1, op=op), reads=r, writes=w)


def TSC(g, eng, out, in0, s1, s2, op0, op1, r, w):
    if s2 is None:
        return g.P.op(eng, lambda e: e.tensor_scalar(out=out, in0=in0, scalar1=s1, scalar2=None, op0=op0), reads=r, writes=w)
    return g.P.op(eng, lambda e: e.tensor_scalar(out=out, in0=in0, scalar1=s1, scalar2=s2, op0=op0, op1=op1), reads=r, writes=w)


def STT(g, out, in0, scalar, in1, op0, op1, r, w):
    return g.P.op("dve", lambda e: e.scalar_tensor_tensor(out=out, in0=in0, scalar=scalar, in1=in1, op0=op0, op1=op1), reads=r, writes=w)


def run_interleaved(gens):
    gens = list(gens)
    while gens:
        for gen in list(gens):
            try:
                next(gen)
            except StopIteration:
                gens.remove(gen)


def next_bank(g):
    i = g.bank_rr % 8
    g.bank_rr += 1
    return g.banks[i], g.bank_b[i]


def pvc(g, name, c):
    j = PV[name] * 8 + c
    return g.pvec[:, j:j + 1]


class StopRwkv(Exception):
    pass


def rwkv(g, tiles=None):
    try:
        _rwkv(g, tiles)
    except StopRwkv:
        pass
    g.P.barrier()


def _rwkv(g, tiles=None):
    stop = getattr(g, "rwkv_stop", 99)
    nc, P, d = g.nc, g.P, g.dram
    RTT = TT[:17] + [(TP, 128), (TP, 128)]
    tiles = list(range(len(RTT))) if tiles is None else tiles
    C0 = float(np.exp(-0.5))
    cb = g.cb
    identb = g.consts_bf[:, C_IDENT:C_IDENT + 128]
    identf = g.consts[:, C_IDENT:C_IDENT + 128]
    onesb = g.consts_bf[:, C_ONES:C_ONES + 128]
    onesf = g.consts[:, C_ONES:C_ONES + 128]
    BDf = g.consts[:, C_BD:C_BD + 128]
    with contextlib.ExitStack() as es:
        def sb(name, shape, dt):
            return es.enter_context(g_sbuf(nc, "rk_" + name, shape, dt))
        wb = Buf("rwkv_w")
        Wr, Wk, Wv, Wo = (sb(nm, [128, NCH, D], BF16) for nm in ("Wr", "Wk", "Wv", "Wo"))
        for i, W in enumerate((Wr, Wk, Wv)):
            P.dma("pool", W[:], d["rwkv_w_rkv"][i].rearrange("(c p) n -> p c n", p=128), writes=[wb])
        P.dma("pool", Wo[:], d["rwkv_w_o"].rearrange("(c p) n -> p c n", p=128), writes=[wb])
        w1 = sb("w1", [128, NCH, 64], BF16)
        a1 = sb("a1", [128, NCH, 64], BF16)
        g1 = sb("g1", [128, NCH, 128], BF16)
        w2 = sb("w2", [64, D], BF16)
        a2 = sb("a2", [64, D], BF16)
        g2 = sb("g2", [128, D], BF16)
        P.dma("pool", w1[:], d["rwkv_w1"].rearrange("(c p) n -> p c n", p=128), writes=[wb])
        P.dma("pool", a1[:], d["rwkv_a1"].rearrange("(c p) n -> p c n", p=128), writes=[wb])
        P.dma("pool", g1[:], d["rwkv_g1"].rearrange("(c p) n -> p c n", p=128), writes=[wb])
        P.dma("pool", w2[:], d["rwkv_w2"], writes=[wb])
        P.dma("pool", a2[:], d["rwkv_a2"], writes=[wb])
        P.dma("pool", g2[:], d["rwkv_g2"], writes=[wb])
        shift_sb = sb("shift", [128, 16], F32)
        P.dma("sp", shift_sb[:], d["shift_in"], writes=[wb])
        omka = sb("omka", [128, 8], F32)
        ka0 = PV["k_a"] * 8
        TSC(g, "dve", omka[:], g.pvec[:, ka0:ka0 + 8], -1.0, 1.0, ALU.mult, ALU.add, [cb], [wb])

        M32 = [sb("M32_%d" % c, [128, 8, 64], F32) for c in range(3)]
        Mb = [sb("Mb_%d" % c, [128, 8, 64], BF16) for c in range(3)]
        M_b = [[Buf("M") for _ in range(8)] for _ in range(3)]
        P.op("pool", lambda e: e.memset(M32[0][:], 0.0), writes=M_b[0])
        P.op("pool", lambda e: e.memset(Mb[0][:], 0.0), writes=M_b[0])
        s0stg = sb("s0stg", [64, 16, 64], F32)
        s0_b = Buf("s0stg")
        for sq_ in range(2):
            P.dma("sp", s0stg[:], d["state_wkv_in"][sq_].rearrange("h v k -> v h k"), writes=[s0_b])
            for p in range(8):
                bk, bb = next_bank(g)
                TRP(g, bk[:, 0:64], s0stg[:, 2 * p:2 * p + 2, :].rearrange("v h k -> v (h k)"), identf[0:64, 0:64], [s0_b, cb], [bb])
                CPY(g, "act", M32[1 + sq_][:, p, :], bk[:, 0:64], [bb], [M_b[1 + sq_][p]])
                CPY(g, "dve", Mb[1 + sq_][:, p, :], bk[:, 0:64], [bb], [M_b[1 + sq_][p]])

        hb1 = sb("hb", [128, NCH, 132], F32)
        hb1_b = Buf("hb")
        hlast = sb("hlast", [128, NCH, 1], F32)
        hlast_b = Buf("hlast")
        dx = sb("dx", [128, NCH, 128], F32)
        dx_b = Buf("dx")
        sqt = sb("sqt", [128, NCH, 128], BF16)
        sq_b = Buf("sq")
        rt = sb("rt", [128, 128], F32)
        rt_b = Buf("rt")
        mixL = sb("mixL", [128, NCH, 128], BF16)
        mixL_b = [Buf("mixL") for _ in range(NCH)]
        mix = {i: sb("mix%d" % i, [128, NCH, 128], BF16) for i in (0, 2, 3)}
        mix_b = {i: [Buf("mix") for _ in range(NCH)] for i in (0, 2, 3)}
        for i in (1, 4, 5):
            mix[i] = mixL
            mix_b[i] = mixL_b
        tw = sb("tw", [64, 128], BF16)
        ta = sb("ta", [64, 128], BF16)
        tg = sb("tg", [128, 128], BF16)
        tw_b, ta_b, tg_b = Buf("tw"), Buf("ta"), Buf("tg")
        ogT = sb("ogT", [128, 8, 128], BF16)
        og_b = [Buf("og") for _ in range(8)]

        def F(name):
            return sb(name, [128, 128], F32), Buf(name)
        sg, sg_b = F("sg")
        av, av_b = F("av")
        gsb, gsb_b = F("gsb")
        vsb, vsb_b = F("vsb")
        kkraw, kkraw_b = F("kkraw")
        kksq, kksq_b = F("kksq")
        nrm, nrm_b = F("nrm")
        kkn, kkn_b = F("kkn")
        t1, t1_b = F("t1")
        kmod, kmod_b = F("kmod")
        bb_, bb_b = F("bb")
        cs, cs_b = F("cs")
        csx, csx_b = F("csx")
        E1, E1_b = F("E1")
        E2, E2_b = F("E2")
        E3, E3_b = F("E3")
        E4, E4_b = F("E4")
        nb = sb("nb", [128, 2], F32)
        nb_b = Buf("nb")
        rkr, rkr_b = F("rkr")
        bonus, bonus_b = F("bonus")
        kr = sb("kr", [128, 2, 128], BF16)
        kr_b = [Buf("kr0"), Buf("kr1")]
        Bh = sb("Bh", [128, 128], BF16)
        Kh = sb("Kh", [128, 128], BF16)
        Bh_b, Kh_b = Buf("Bh"), Buf("Kh")
        src4 = sb("src4", [128, 4, 128], BF16)
        s4_b = [Buf("src4") for _ in range(4)]
        tok4 = sb("tok4", [128, 4, 128], BF16)
        tok4_b = Buf("tok4")
        XX = [[sb("XX%d_%d" % (h, i), [128, 2, 128], BF16) for i in range(2)] for h in range(2)]
        XX_b = [[Buf("XX") for _ in range(2)] for _ in range(2)]
        YY = [[sb("YY%d_%d" % (h, i), [128, 128], BF16) for i in range(2)] for h in range(2)]
        YY_b = [[Buf("YY") for _ in range(2)] for _ in range(2)]
        LkTm = [sb("LkTm%d" % h, [128, 128], BF16) for h in range(2)]
        AbTm = [sb("AbTm%d" % h, [128, 128], BF16) for h in range(2)]
        AkTm = [sb("AkTm%d" % h, [128, 128], BF16) for h in range(2)]
        LkT_b = [Buf("LkT") for _ in range(2)]
        AbT_b = [Buf("AbT") for _ in range(2)]
        AkT_b = [Buf("AkT") for _ in range(2)]
        Wt = [sb("Wt%d" % h, [128, 64], BF16) for h in range(2)]
        Uv = [sb("Uv%d" % h, [128, 64], BF16) for h in range(2)]
        Wt_b = [Buf("Wt") for _ in range(2)]
        Uv_b = [Buf("Uv") for _ in range(2)]
        QtT = sb("QtT", [128, 128], BF16)
        QtT_b = [Buf("QtT") for _ in range(2)]
        Gneg = sb("Gneg", [128, 2, 64], BF16)
        Gneg_b = [Buf("Gneg") for _ in range(2)]
        osb = sb("osb", [128, 2, 128], F32)
        osb_b = Buf("osb")
        gm, gm_b = kkraw, kkraw_b
        gm2, gm2_b = kksq, kksq_b
        gv, gv_b = nrm, nrm_b
        outstg = s0stg
        outstg_b = s0_b
        shstg = sb("shstg", [8, 128], F32)
        shstg_b = Buf("shstg")

        if stop <= 1:
            return
        for ti in tiles:
            col0, n = RTT[ti]
            sample = (ti >= 17)
            segs = [(0, n, ti - 16)] if sample else [(0, n, 0)]
            real0, realn = ((ti - 17) * 64, 64) if sample else (0, n)
            pad0 = (64 - real0) if sample else None
            maxseg = max(s[1] for s in segs)
            L = max(1, int(np.ceil(np.log2(maxseg))))
            mofs = C_M1
            MSN = g.consts[0:n, mofs + 0:mofs + n]
            MSNT = g.consts[0:n, mofs + 128:mofs + 128 + n]
            MS = g.consts[0:n, mofs + 256:mofs + 256 + n]
            MI = g.consts[0:n, mofs + 384:mofs + 384 + n]
            hcur = hb1
            hcur_b = hb1_b
            xb_all = []
            for c in range(NCH):
                xb_all += xt_bufs_for_cols(g, c, col0, n)
            ACTF(g, sqt[:, :, 0:n], g.xT[:, :, col0:col0 + n], AF.Square, xb_all, [sq_b])
            bk, bkb = next_bank(g)
            for c in range(NCH):
                MM(g, bk[:, 0:n], onesb, sqt[:, c, 0:n], [sq_b, cb], [bkb], start=(c == 0), stop=(c == NCH - 1))
            ACTF(g, rt[:, 0:n], bk[:, 0:n], AF.Sqrt, [bkb, cb], [rt_b], bias=g.consts[:, C_EPS_RMS:C_EPS_RMS + 1], scale=1.0 / D)
            P.op("dve", lambda e, n=n: e.reciprocal(out=rt[:, 0:n], in_=rt[:, 0:n]), reads=[rt_b], writes=[rt_b])
            for c in range(NCH):
                STT(g, hcur[:, c, 1:n + 1], g.xT[:, c, col0:col0 + n], pvc(g, "mix_g0", c), rt[:, 0:n], ALU.mult, ALU.mult,
                    xt_bufs_for_cols(g, c, col0, n) + [rt_b, cb], [hcur_b])
            if ti == 0:
                P.op("pool", lambda e, hcur=hcur: e.memset(hcur[:, :, 0:1], 0.0), writes=[hcur_b])
            elif sample:
                CPY(g, "pool", hcur[:, :, 0:1], shift_sb[:, 0:8].rearrange("p (c o) -> p c o", o=1), [wb], [hcur_b])
            else:
                CPY(g, "pool", hcur[:, :, 0:1], hlast[:], [hlast_b], [hcur_b])
            TTO(g, "dve", dx[:, :, 0:n], hcur[:, :, 0:n], hcur[:, :, 1:n + 1], ALU.subtract, [hcur_b], [dx_b])
            if not sample:
                CPY(g, "pool", hlast[:], hcur[:, :, n:n + 1], [hcur_b], [hlast_b])
            if ti == 18:
                TTO(g, "dve", dx[:, :, 64:65], shift_sb[:, 8:16].rearrange("p (c o) -> p c o", o=1), hcur[:, :, 65:66], ALU.subtract,
                    [hcur_b, wb], [dx_b])
            def do_mix(i):
                for c in range(NCH):
                    STT(g, mix[i][:, c, 0:n], dx[:, c, 0:n], pvc(g, "mu%d" % i, c), hcur[:, c, 1:n + 1], ALU.mult, ALU.add,
                        [dx_b, hcur_b, cb], [mix_b[i][c]])
            for i in (0, 2, 3):
                do_mix(i)
            if ti == 16 or sample:
                for (s0, sn, ch) in segs:
                    bk, bkb = next_bank(g)
                    lc = real0 + realn
                    TRP(g, bk[0:8, 0:128], hcur[:, :, lc:lc + 1].rearrange("p c o -> p (c o)"), identf, [hcur_b, cb], [bkb])
                    CPY(g, "act", shstg[:], bk[0:8, 0:128], [bkb], [shstg_b])
                    dst = d["shift_p"] if ch == 0 else d["shift_s"][ch - 1]
                    P.dma("sp", dst, shstg[:], reads=[shstg_b])
            do_mix(1)
            bk, bkb = next_bank(g)
            for c in range(NCH):
                MM(g, bk[0:64, 0:n], w1[:, c, :], mix[1][:, c, 0:n], [wb, mix_b[1][c]], [bkb], start=(c == 0), stop=(c == NCH - 1))
            ACTF(g, tw[:, 0:n], bk[0:64, 0:n], AF.Tanh, [bkb], [tw_b])
            do_mix(4)
            bk, bkb = next_bank(g)
            for c in range(NCH):
                MM(g, bk[0:64, 0:n], a1[:, c, :], mix[4][:, c, 0:n], [wb, mix_b[4][c]], [bkb], start=(c == 0), stop=(c == NCH - 1))
            CPY(g, "act", ta[:, 0:n], bk[0:64, 0:n], [bkb], [ta_b])
            do_mix(5)
            bk, bkb = next_bank(g)
            for c in range(NCH):
                MM(g, bk[:, 0:n], g1[:, c, :], mix[5][:, c, 0:n], [wb, mix_b[5][c]], [bkb], start=(c == 0), stop=(c == NCH - 1))
            ACTF(g, tg[:, 0:n], bk[:, 0:n], AF.Sigmoid, [bkb], [tg_b])

            if stop <= 2:
                return
            for p in range(8):
                pc = p * 128
                bkA, bkA_b = next_bank(g)
                for j, (W, mi) in enumerate(((Wr, 0), (Wk, 2), (Wv, 3))):
                    for c in range(NCH):
                        MM(g, bkA[:, j * 128:j * 128 + n], W[:, c, pc:pc + 128], mix[mi][:, c, 0:n], [wb, mix_b[mi][c]], [bkA_b],
                           start=(c == 0), stop=(c == NCH - 1))
                MM(g, bkA[:, 384:384 + n], w2[:, pc:pc + 128], tw[:, 0:n], [wb, tw_b], [bkA_b])
                bkB, bkB_b = next_bank(g)
                MM(g, bkB[:, 0:n], a2[:, pc:pc + 128], ta[:, 0:n], [wb, ta_b], [bkB_b])
                MM(g, bkB[:, 128:128 + n], g2[:, pc:pc + 128], tg[:, 0:n], [wb, tg_b], [bkB_b])
                if stop <= 2.1:
                    return
                r_ps = bkA[:, 0:n]
                k_ps = bkA[:, 128:128 + n]
                v_ps = bkA[:, 256:256 + n]
                ACTF(g, sg[:, 0:n], bkA[:, 384:384 + n], AF.Sigmoid, [bkA_b, cb], [sg_b], bias=pvc(g, "w0", p))
                if sample:
                    P.op("pool", lambda e, pad0=pad0: e.memset(sg[:, pad0:pad0 + 64], 0.0), writes=[sg_b])
                ACTF(g, av[:, 0:n], bkB[:, 0:n], AF.Sigmoid, [bkB_b, cb], [av_b], bias=pvc(g, "a0", p))
                if stop <= 2.15:
                    return
                CPY(g, "act", gsb[:, 0:n], bkB[:, 128:128 + n], [bkB_b], [gsb_b])
                CPY(g, "act", vsb[:, 0:n], v_ps, [bkA_b], [vsb_b])
                if stop <= 2.17:
                    return
                CPY(g, "pool", src4[:, 3, 0:n], vsb[:, 0:n], [vsb_b], [s4_b[3]])
                if stop <= 2.18:
                    return
                ACTF(g, kkraw[:, 0:n], k_ps, AF.Copy, [bkA_b, cb], [kkraw_b], scale=pvc(g, "k_k", p))
                if stop <= 2.2:
                    return
                ACTF(g, kksq[:, 0:n], kkraw[:, 0:n], AF.Square, [kkraw_b], [kksq_b])
                bkC, bkC_b = next_bank(g)
                MM(g, bkC[:, 0:n], BDf, kksq[:, 0:n], [cb, kksq_b], [bkC_b])
                ACTF(g, nrm[:, 0:n], bkC[:, 0:n], AF.Sqrt, [bkC_b], [nrm_b])
                TSC(g, "dve", nrm[:, 0:n], nrm[:, 0:n], 1e-12, None, ALU.max, None, [nrm_b], [nrm_b])
                P.op("dve", lambda e, n=n: e.reciprocal(out=nrm[:, 0:n], in_=nrm[:, 0:n]), reads=[nrm_b], writes=[nrm_b])
                if stop <= 2.4:
                    return
                TTO(g, "dve", kkn[:, 0:n], kkraw[:, 0:n], nrm[:, 0:n], ALU.mult, [kkraw_b, nrm_b], [kkn_b])
                TSC(g, "dve", t1[:, 0:n], av[:, 0:n], pvc(g, "k_a", p), omka[:, p:p + 1], ALU.mult, ALU.add, [av_b, cb, wb], [t1_b])
                TTO(g, "dve", kmod[:, 0:n], k_ps, t1[:, 0:n], ALU.mult, [bkA_b, t1_b], [kmod_b])
                TTO(g, "pool", bb_[:, 0:n], kkn[:, 0:n], av[:, 0:n], ALU.mult, [kkn_b, av_b], [bb_b])
                if sample:
                    P.op("pool", lambda e, pad0=pad0: e.memset(bb_[:, pad0:pad0 + 64], 0.0), writes=[bb_b])
                    P.op("pool", lambda e, pad0=pad0: e.memset(kmod[:, pad0:pad0 + 64], 0.0), writes=[kmod_b])
                for (s0, sn, ch) in segs:
                    P.op("dve", lambda e, s0=s0, sn=sn: e.tensor_tensor_scan(cs[:, s0:s0 + sn], onesf[:, 0:sn], sg[:, s0:s0 + sn], 0.0, ALU.mult, ALU.add),
                         reads=[sg_b, cb], writes=[cs_b])
                TTO(g, "pool", csx[:, 0:n], cs[:, 0:n], sg[:, 0:n], ALU.subtract, [cs_b, sg_b], [csx_b])
                if stop <= 2.5:
                    return
                ACTF(g, E1[:, 0:n], cs[:, 0:n], AF.Exp, [cs_b], [E1_b], scale=-C0)
                ACTF(g, E2[:, 0:n], csx[:, 0:n], AF.Exp, [csx_b], [E2_b], scale=-C0)
                ACTF(g, E3[:, 0:n], cs[:, 0:n], AF.Exp, [cs_b], [E3_b], scale=C0)
                for si, (s0, sn, ch) in enumerate(segs):
                    TSC(g, "dve", nb[:, si:si + 1], cs[:, s0 + sn - 1:s0 + sn], -C0, None, ALU.mult, None, [cs_b], [nb_b])
                    ACTF(g, E4[:, s0:s0 + sn], cs[:, s0:s0 + sn], AF.Exp, [cs_b, nb_b], [E4_b], bias=nb[:, si:si + 1], scale=C0)
                TTO(g, "dve", kr[:, 0, 0:n], kkn[:, 0:n], E2[:, 0:n], ALU.mult, [kkn_b, E2_b], [kr_b[0]])
                TTO(g, "dve", kr[:, 1, 0:n], r_ps, E1[:, 0:n], ALU.mult, [bkA_b, E1_b], [kr_b[1]])
                TTO(g, "pool", Bh[:, 0:n], bb_[:, 0:n], E3[:, 0:n], ALU.mult, [bb_b, E3_b], [Bh_b])
                TTO(g, "pool", Kh[:, 0:n], kmod[:, 0:n], E3[:, 0:n], ALU.mult, [kmod_b, E3_b], [Kh_b])
                TTO(g, "pool", src4[:, 1, 0:n], bb_[:, 0:n], E4[:, 0:n], ALU.mult, [bb_b, E4_b], [s4_b[1]])
                TTO(g, "pool", src4[:, 2, 0:n], kmod[:, 0:n], E4[:, 0:n], ALU.mult, [kmod_b, E4_b], [s4_b[2]])
                STT(g, rkr[:, 0:n], r_ps, pvc(g, "r_k", p), kmod[:, 0:n], ALU.mult, ALU.mult, [bkA_b, kmod_b, cb], [rkr_b])
                MM(g, bkC[:, 128:128 + n], BDf, rkr[:, 0:n], [cb, rkr_b], [bkC_b])
                TTO(g, "dve", bonus[:, 0:n], bkC[:, 128:128 + n], vsb[:, 0:n], ALU.mult, [bkC_b, vsb_b], [bonus_b])
                if stop <= 2.6:
                    return
                bkT, bkT_b = next_bank(g)
                bkT16 = bkT[:].bitcast(BF16)
                TRP(g, bkT16[0:n, 0:128], kr[:, 0, 0:n], identb, [kr_b[0], cb], [bkT_b])
                for j in (1, 2, 3):
                    TRP(g, bkT16[0:n, j * 128:(j + 1) * 128], src4[:, j, 0:n], identb, [s4_b[j], cb], [bkT_b])
                CPY(g, "act", tok4[0:n, :, :].rearrange("p a b -> p (a b)"), bkT16[0:n, 0:512], [bkT_b], [tok4_b])

                if stop <= 3:
                    return
                def head_gen(hh, p=p, n=n, segs=segs, L=L, MSN=MSN, MSNT=MSNT, MS=MS, MI=MI):
                    h0 = hh * 64
                    hs = slice(h0, h0 + 64)
                    bkU, bkU_b = next_bank(g)
                    bkW, bkW_b = next_bank(g)
                    krh = kr[hs, :, 0:n]
                    for a_ in range(2):
                        MM(g, bkU[0:n, a_ * 128:a_ * 128 + n], Bh[hs, 0:n], kr[hs, a_, 0:n], [Bh_b, kr_b[a_]], [bkU_b])
                        MM(g, bkU[0:n, 256 + a_ * 128:256 + a_ * 128 + n], Kh[hs, 0:n], kr[hs, a_, 0:n], [Kh_b, kr_b[a_]], [bkU_b])
                    MM(g, bkW[0:n, 0:n], kr[hs, 0, 0:n], Bh[hs, 0:n], [Bh_b, kr_b[0]], [bkW_b])
                    X0 = XX[hh][0]
                    TTO(g, "dve", X0[0:n, 1, 0:n], bkU[0:n, 0:n], MSN, ALU.mult, [bkU_b, cb], [XX_b[hh][0]])
                    TTO(g, "dve", X0[0:n, 0, 0:n], bkW[0:n, 0:n], MSNT, ALU.mult, [bkW_b, cb], [XX_b[hh][0]])
                    TTO(g, "dve", AbTm[hh][0:n, 0:n], bkU[0:n, 128:128 + n], MI, ALU.mult, [bkU_b, cb], [AbT_b[hh]])
                    TTO(g, "dve", LkTm[hh][0:n, 0:n], bkU[0:n, 256:256 + n], MS, ALU.mult, [bkU_b, cb], [LkT_b[hh]])
                    TTO(g, "dve", AkTm[hh][0:n, 0:n], bkU[0:n, 384:384 + n], MI, ALU.mult, [bkU_b, cb], [AkT_b[hh]])
                    Vh = tok4[0:n, 3, h0:h0 + 64]
                    Bch = tok4[0:n, 1, h0:h0 + 64]
                    Kch = tok4[0:n, 2, h0:h0 + 64]
                    MM(g, bkW[0:n, 128:192], LkTm[hh][0:n, 0:n], Vh, [LkT_b[hh], tok4_b], [bkW_b])
                    Y = YY[hh][0]
                    CPY(g, "pool", Y[0:n, 0:64], tok4[0:n, 0, h0:h0 + 64], [tok4_b], [YY_b[hh][0]])
                    CPY(g, "act", Y[0:n, 64:128], bkW[0:n, 128:192], [bkW_b], [YY_b[hh][0]])
                    for lv in range(L):
                        yield
                        cur, nxt = lv % 2, (lv + 1) % 2
                        Xc = XX[hh][cur]
                        bkY, bkY_b = next_bank(g)
                        MM(g, bkY[0:n, 0:128], Xc[0:n, 1, 0:n], YY[hh][cur][0:n, :], [XX_b[hh][cur], YY_b[hh][cur]], [bkY_b], start=True, stop=False)
                        MM(g, bkY[0:n, 0:128], identb[0:n, 0:n], YY[hh][cur][0:n, :], [cb, YY_b[hh][cur]], [bkY_b], start=False, stop=True)
                        if lv < L - 1:
                            bkX, bkX_b = next_bank(g)
                            MM(g, bkX[0:n, 0:n], Xc[0:n, 1, 0:n], Xc[0:n, 0, 0:n], [XX_b[hh][cur]], [bkX_b])
                            MM(g, bkX[0:n, 128:128 + n], Xc[0:n, 0, 0:n], Xc[0:n, 1, 0:n], [XX_b[hh][cur]], [bkX_b])
                            CPY(g, "act", YY[hh][nxt][0:n, :], bkY[0:n, 0:128], [bkY_b], [YY_b[hh][nxt]])
                            CPY(g, "dve", XX[hh][nxt][0:n, :, 0:n], bkX[0:n, 0:256].rearrange("p (a b) -> p a b", a=2)[:, :, 0:n], [bkX_b], [XX_b[hh][nxt]])
                        else:
                            CPY(g, "act", Wt[hh][0:n, :], bkY[0:n, 0:64], [bkY_b], [Wt_b[hh]])
                            P.op("act", lambda e, hh=hh, n=n, bkY=bkY: e.mul(Uv[hh][0:n, :], bkY[0:n, 64:128], -1.0), reads=[bkY_b], writes=[Uv_b[hh]])
                    yield
                    if stop <= 4:
                        return
                    bkQ, bkQ_b = next_bank(g)
                    MM(g, bkQ[hs, 0:n], Wt[hh][0:n, :], AbTm[hh][0:n, 0:n], [Wt_b[hh], AbT_b[hh]], [bkQ_b])
                    TTO(g, "dve", QtT[hs, 0:n], kr[hs, 1, 0:n], bkQ[hs, 0:n], ALU.subtract, [kr_b[1], bkQ_b], [QtT_b[hh]])
                    for si, (s0, sn, ch) in enumerate(segs):
                        MM(g, bkQ[hs, 128 + si * 64:192 + si * 64], Wt[hh][s0:s0 + sn, :], tok4[s0:s0 + sn, 1, h0:h0 + 64], [Wt_b[hh], tok4_b], [bkQ_b])
                    nsg = len(segs)
                    P.op("act", lambda e, hs=hs, nsg=nsg, bkQ=bkQ: e.mul(Gneg[hs, 0:nsg, :].rearrange("p a b -> p (a b)"), bkQ[hs, 128:128 + 64 * nsg], -1.0),
                         reads=[bkQ_b], writes=[Gneg_b[hh]])
                    yield
                    if stop <= 5:
                        return
                    bkO, bkO_b = next_bank(g)
                    MM(g, bkO[hs, 0:n], Uv[hh][0:n, :], AbTm[hh][0:n, 0:n], [Uv_b[hh], AbT_b[hh]], [bkO_b], start=True, stop=False)
                    MM(g, bkO[hs, 0:n], Vh, AkTm[hh][0:n, 0:n], [tok4_b, AkT_b[hh]], [bkO_b], start=False, stop=False)
                    for si, (s0, sn, ch) in enumerate(segs):
                        MM(g, bkO[hs, s0:s0 + sn], Mb[ch][hs, p, :], QtT[hs, s0:s0 + sn], [M_b[ch][p], QtT_b[hh]], [bkO_b],
                           start=False, stop=(si == len(segs) - 1))
                    CPY(g, "act", osb[hs, 0, 0:n], bkO[hs, 0:n], [bkO_b], [osb_b])
                    bkS, bkS_b = next_bank(g)
                    for si, (s0, sn, ch) in enumerate(segs):
                        so = bkS[hs, si * 64:si * 64 + 64]
                        MM(g, so, Gneg[hs, si, :], Mb[ch][hs, p, :], [Gneg_b[hh], M_b[ch][p]], [bkS_b], start=True, stop=False)
                        MM(g, so, tok4[s0:s0 + sn, 1, h0:h0 + 64], Uv[hh][s0:s0 + sn, :], [tok4_b, Uv_b[hh]], [bkS_b], start=False, stop=False)
                        MM(g, so, tok4[s0:s0 + sn, 2, h0:h0 + 64], tok4[s0:s0 + sn, 3, h0:h0 + 64], [tok4_b], [bkS_b], start=False, stop=True)
                    for si, (s0, sn, ch) in enumerate(segs):
                        STT(g, M32[ch][hs, p, :], M32[ch][hs, p, :], E1[hs, s0 + sn - 1:s0 + sn], bkS[hs, si * 64:si * 64 + 64], ALU.mult, ALU.add,
                            [M_b[ch][p], E1_b, bkS_b], [M_b[ch][p]])
                        CPY(g, "act", Mb[ch][hs, p, :], M32[ch][hs, p, :], [M_b[ch][p]], [M_b[ch][p]])
                run_interleaved([head_gen(0), head_gen(1)])
                if stop <= 6:
                    return
                ACTF(g, osb[:, 1, 0:n], osb[:, 0, 0:n], AF.Square, [osb_b], [osb_b])
                bkG, bkG_b = next_bank(g)
                for a_ in range(2):
                    MM(g, bkG[:, a_ * 128:a_ * 128 + n], BDf, osb[:, a_, 0:n], [cb, osb_b], [bkG_b])
                P.op("act", lambda e, n=n, bkG=bkG: e.mul(gm[:, 0:n], bkG[:, 0:n], 1.0 / 64), reads=[bkG_b], writes=[gm_b])
                TTO(g, "pool", gm2[:, 0:n], gm[:, 0:n], gm[:, 0:n], ALU.mult, [gm_b], [gm2_b])
                STT(g, gv[:, 0:n], bkG[:, 128:128 + n], 1.0 / 64, gm2[:, 0:n], ALU.mult, ALU.subtract, [bkG_b, gm2_b], [gv_b])
                ACTF(g, gv[:, 0:n], gv[:, 0:n], AF.Sqrt, [gv_b, cb], [gv_b], bias=g.consts[:, C_EPS_GN:C_EPS_GN + 1])
                P.op("dve", lambda e, n=n: e.reciprocal(out=gv[:, 0:n], in_=gv[:, 0:n]), reads=[gv_b], writes=[gv_b])
                TTO(g, "dve", gm2[:, 0:n], osb[:, 0, 0:n], gm[:, 0:n], ALU.subtract, [osb_b, gm_b, gm2_b], [gm2_b])
                TTO(g, "dve", gm2[:, 0:n], gm2[:, 0:n], gv[:, 0:n], ALU.mult, [gm2_b, gv_b], [gm2_b])
                TSC(g, "dve", gm2[:, 0:n], gm2[:, 0:n], pvc(g, "gn_w", p), pvc(g, "gn_b", p), ALU.mult, ALU.add, [gm2_b, cb], [gm2_b])
                TTO(g, "dve", gm2[:, 0:n], gm2[:, 0:n], bonus[:, 0:n], ALU.add, [gm2_b, bonus_b], [gm2_b])
                TTO(g, "dve", ogT[:, p, 0:n], gm2[:, 0:n], gsb[:, 0:n], ALU.mult, [gm2_b, gsb_b], [og_b[p]])
            for dc in range(NCH):
                bk, bkb = next_bank(g)
                for p in range(8):
                    MM(g, bk[:, 0:n], Wo[:, p, dc * 128:(dc + 1) * 128], ogT[:, p, 0:n], [wb, og_b[p]], [bkb], start=(p == 0), stop=(p == 7))
                xb = xt_bufs_for_cols(g, dc, col0, n)
                ca, cn_ = col0 + real0, realn
                TTO(g, "dve", g.xT[:, dc, ca:ca + cn_], g.xT[:, dc, ca:ca + cn_], bk[:, real0:real0 + realn], ALU.add, xb + [bkb], xb)
            if ti == 16 or sample:
                for (s0, sn, ch) in segs:
                    for p in range(8):
                        bk, bkb = next_bank(g)
                        TRP(g, bk[0:64, 0:128], M32[ch][:, p, :], identf, [M_b[ch][p], cb], [bkb])
                        CPY(g, "act", outstg[:, 2 * p:2 * p + 2, :].rearrange("v h k -> v (h k)"), bk[0:64, 0:128], [bkb], [outstg_b])
                    dst = d["state_wkv_p"] if ch == 0 else d["state_wkv_s"][ch - 1]
                    P.dma("sp", dst.rearrange("h v k -> v h k"), outstg[:], reads=[outstg_b])
        P.barrier()


def sb_attn(g, do_sample=True, pairs=None):
    _sb_attn(g, do_sample, pairs)
    g.P.barrier()


def _sb_attn(g, do_sample=True, pairs=None):
    nc, P, d = g.nc, g.P, g.dram
    stop = getattr(g, "sb_stop", 99)
    cb = g.cb
    pairs = list(range(8)) if pairs is None else pairs
    identb = g.consts_bf[:, C_IDENT:C_IDENT + 128]
    TINCL = g.consts[:, C_TRI_INCL:C_TRI_INCL + 128]
    TCOMP = g.consts[:, C_TRI_COMP:C_TRI_COMP + 128]
    MS1 = g.consts[:, C_M1 + 256:C_M1 + 384]
    MS2 = g.consts[:, C_M2 + 256:C_M2 + 384]
    with contextlib.ExitStack() as es:
        def sb(name, shape, dt):
            return es.enter_context(g_sbuf(nc, "sb_" + name, shape, dt))
        xn, xn_b = rmsnorm_T(g, es, "mix_g1")
        gqk = sb("gqk", [128, 256], F32)
        gqk_b = Buf("gqk")
        P.dma("sp", gqk[:], d["gqk"], writes=[gqk_b])
        wq = [sb("wq%d" % i, [128, NCH, 3, 128], BF16) for i in range(2)]
        wq_b = [Buf("wq") for _ in range(2)]
        wo = [sb("wo%d" % i, [128, D], BF16) for i in range(2)]
        wo_b = [Buf("wo") for _ in range(2)]
        QT = sb("QT", [128, NTOK], BF16)
        KT = sb("KT", [128, NTOK], BF16)
        Vt = sb("Vt", [128, 18, 128], BF16)
        oT = sb("oT", [128, NTOK], BF16)
        QT_b, KT_b, Vt_b = Buf("QT"), Buf("KT"), Buf("Vt")
        oT_b = [Buf("oT") for _ in range(NCT)]
        qksb = sb("qksb", [128, 256], F32)
        qksb_b = Buf("qksb")
        sqf = sb("sqf", [128, 256], F32)
        sqf_b = Buf("sqf")
        ss = sb("ss", [128, 4], F32)
        ss_b = Buf("ss")
        tq = sb("tq", [128, 256], F32)
        tq_b = Buf("tq")
        qkn = [sb("qkn%d" % i, [128, 256], F32) for i in range(2)]
        qkn_b = [Buf("qkn") for _ in range(2)]
        qkb = sb("qkb", [128, 256], BF16)
        qkb_b = Buf("qkb")
        P.op("pool", lambda e: e.memset(qkb[:], 0.0), writes=[qkb_b])
        vf = [sb("vf%d" % i, [128, 128], F32) for i in range(2)]
        vf_b = [Buf("vf") for _ in range(2)]
        ef = [sb("ef%d" % i, [128, 512], F32) for i in range(2)]
        ef_b = [Buf("ef") for _ in range(2)]
        spf = [sb("spf%d" % i, [128, 512], F32) for i in range(2)]
        spf_b = [Buf("spf") for _ in range(2)]
        Xf = [sb("Xf%d" % i, [128, 512], F32) for i in range(2)]
        Xf_b = [Buf("Xf") for _ in range(2)]
        Ab = [sb("Ab%d" % i, [128, 512], BF16) for i in range(2)]
        Ab_b = [Buf("Ab") for _ in range(2)]
        if do_sample:
            Kp = sb("Kp", [128, 32, 2, 64], BF16)
            Vp = sb("Vp", [128, 32, 2, 64], BF16)
            KTp = sb("KTp", [128, PAST], BF16)
            Kp_b, Vp_b, KTp_b = Buf("Kp"), Buf("Vp"), Buf("KTp")
        w_qkv = d["sb_w_qkv"].rearrange("(c p) (t n) -> p c t n", p=128, t=3)

        def load_w(p):
            s = p % 2
            pc = p * 128
            for t_ in range(3):
                P.dma("pool", wq[s][:, :, t_, :], w_qkv[:, :, t_, pc:pc + 128], writes=[wq_b[s]])
            P.dma("pool", wo[s][:], d["sb_w_o"][pc:pc + 128, :], writes=[wo_b[s]])

        stepk = [0]

        reserved = set()

        def free_bank():
            bk_, bb_ = next_bank(g)
            while id(bb_) in reserved:
                bk_, bb_ = next_bank(g)
            return bk_, bb_

        NSL = 8
        efs_b = [[Buf("ef") for _ in range(NSL)] for _ in range(2)]
        sps_b = [[Buf("sp") for _ in range(NSL)] for _ in range(2)]
        Xs_b = [[Buf("X") for _ in range(NSL)] for _ in range(2)]
        As_b = [[Buf("A") for _ in range(NSL)] for _ in range(2)]

        def attn_steps(hh, qsrc_cols, nq, steps, out_dst, out_bufs, nslots=1):
            h0 = hh * 64
            hs = slice(h0, h0 + 64)
            accbk, accb = free_bank()
            reserved.add(id(accb))
            obk, obb = free_bank()
            reserved.add(id(obb))
            i = hh
            for si, (kT_ap, v_ap, nk, lo, mask_ap, rds) in enumerate(steps):
                sl = si % nslots
                so = sl * 64 if nslots > 1 else 0
                e_b, s_b, x_b, a_b = efs_b[i][sl], sps_b[i][sl], Xs_b[i][sl], As_b[i][sl]
                zbk, zbb = free_bank()
                MM(g, zbk[0:nk, lo:nq], kT_ap, QT[hs, qsrc_cols + lo:qsrc_cols + nq], rds + [QT_b], [zbb])
                ACTF(g, ef[i][0:nk, so + lo:so + nq], zbk[0:nk, lo:nq], AF.Exp, [zbb], [e_b], scale=0.125)
                if mask_ap is not None:
                    mw = mask_ap.shape[1]
                    TTO(g, "dve", ef[i][0:nk, so + lo:so + lo + mw], ef[i][0:nk, so + lo:so + lo + mw], mask_ap, ALU.mult, [e_b, cb], [e_b])
                ACTF(g, spf[i][0:nk, so + lo:so + nq], ef[i][0:nk, so + lo:so + nq], AF.Ln, [e_b], [s_b], bias=g.consts[0:nk, C_ONE:C_ONE + 1])
                MM(g, accbk[:, lo:nq], TINCL[0:nk, :], spf[i][0:nk, so + lo:so + nq], [cb, s_b], [accb], start=(si == 0), stop=True, skip=True)
                ACTF(g, Xf[i][0:nk, so + lo:so + nq], accbk[0:nk, lo:nq], AF.Exp, [accb], [x_b], scale=-1.0)
                MM(g, accbk[:, lo:nq], TCOMP[0:nk, :], spf[i][0:nk, so + lo:so + nq], [cb, s_b], [accb], start=False, stop=True, skip=True)
                TTO(g, "dve", Ab[i][0:nk, so + lo:so + nq], ef[i][0:nk, so + lo:so + nq], Xf[i][0:nk, so + lo:so + nq], ALU.mult, [e_b, x_b], [a_b])
                MM(g, obk[hs, lo:nq], v_ap, Ab[i][0:nk, so + lo:so + nq], rds + [a_b], [obb], start=(si == 0), stop=True, skip=True)
                yield
            CPY(g, "act", out_dst, obk[hs, 0:nq], [obb], out_bufs)
            reserved.discard(id(accb))
            reserved.discard(id(obb))

        load_w(pairs[0])
        for pi, p in enumerate(pairs):
            s = p % 2
            if pi + 1 < len(pairs):
                load_w(pairs[pi + 1])
            for ti, (col0, n) in enumerate(TT):
                bk, bkb = next_bank(g)
                xb = []
                for c in range(NCH):
                    xb += [xn_b[c][t] for t, (c0, nn) in enumerate(COLT) if c0 < col0 + n and col0 < c0 + nn]
                for c in range(NCH):
                    MM(g, bk[0:n, 0:384], xn[:, c, col0:col0 + n], wq[s][:, c, :, :].rearrange("p t n -> p (t n)"), xb + [wq_b[s]], [bkb],
                       start=(c == 0), stop=(c == NCH - 1))
                if stop <= 1:
                    return
                CPY(g, "act", qksb[0:n, :], bk[0:n, 0:256], [bkb], [qksb_b])
                ACTF(g, sqf[0:n, :], qksb[0:n, :], AF.Square, [qksb_b], [sqf_b])
                P.op("dve", lambda e, n=n: e.tensor_reduce(out=ss[0:n, :], in_=sqf[0:n, :].rearrange("p (a b) -> p a b", a=4), axis=AX.X, op=ALU.add),
                     reads=[sqf_b], writes=[ss_b])
                ACTF(g, ss[0:n, :], ss[0:n, :], AF.Sqrt, [ss_b, cb], [ss_b], bias=g.consts[0:n, C_EPS_RMS:C_EPS_RMS + 1], scale=1.0 / 64)
                P.op("dve", lambda e, n=n: e.reciprocal(out=ss[0:n, :], in_=ss[0:n, :]), reads=[ss_b], writes=[ss_b])
                if stop <= 2:
                    return
                for a_ in range(4):
                    STT(g, tq[0:n, a_ * 64:(a_ + 1) * 64], qksb[0:n, a_ * 64:(a_ + 1) * 64], ss[0:n, a_:a_ + 1], gqk[0:n, a_ * 64:(a_ + 1) * 64],
                        ALU.mult, ALU.mult, [qksb_b, ss_b, gqk_b], [tq_b])
                if stop <= 3:
                    return
                j = ti % 2
                CPY(g, "pool", qkn[j][0:n, :], tq[0:n, :], [tq_b], [qkn_b[j]])
                CPY(g, "act", qkb[0:n, :], tq[0:n, :], [tq_b], [qkb_b])
                CPY(g, "act", vf[j][0:n, :], bk[0:n, 256:384], [bkb], [vf_b[j]])
                CPY(g, "pool", Vt[0:n, ti, :], vf[j][0:n, :], [vf_b[j]], [Vt_b])
                if stop <= 4:
                    return
                bkT, bkT_b = next_bank(g)
                bkT16 = bkT[:].bitcast(BF16)
                TRP(g, bkT16[:, 0:128], qkb[:, 0:128], identb, [qkb_b, cb], [bkT_b])
                TRP(g, bkT16[:, 128:256], qkb[:, 128:256], identb, [qkb_b, cb], [bkT_b])
                CPY(g, "act", QT[:, col0:col0 + n], bkT16[:, 0:n], [bkT_b], [QT_b])
                CPY(g, "act", KT[:, col0:col0 + n], bkT16[:, 128:128 + n], [bkT_b], [KT_b])
                if stop <= 5:
                    return
                if ti <= 16:
                    kd = d["cache_k_p"][2 * p:2 * p + 2, col0:col0 + n, :].rearrange("h t d -> t h d")
                    vd = d["cache_v_p"][2 * p:2 * p + 2, col0:col0 + n, :].rearrange("h t d -> t h d")
                    P.dma("sp", kd, qkn[j][0:n, 128:256].rearrange("p (h d) -> p h d", h=2), reads=[qkn_b[j]])
                    P.dma("sp", vd, vf[j][0:n, :].rearrange("p (h d) -> p h d", h=2), reads=[vf_b[j]])
                else:
                    for sq_ in range(2):
                        r0 = sq_ * 64
                        kd = d["cache_k_s"][sq_, 2 * p:2 * p + 2, :, :].rearrange("h t d -> t h d")
                        vd = d["cache_v_s"][sq_, 2 * p:2 * p + 2, :, :].rearrange("h t d -> t h d")
                        P.dma("sp", kd, qkn[j][r0:r0 + 64, 128:256].rearrange("p (h d) -> p h d", h=2), reads=[qkn_b[j]])
                        P.dma("sp", vd, vf[j][r0:r0 + 64, :].rearrange("p (h d) -> p h d", h=2), reads=[vf_b[j]])
                if stop <= 5.5 or (stop <= 5.7 and ti == 1):
                    return
            if stop <= 6:
                return
            if not do_sample:
                P.op("pool", lambda e: e.memset(oT[:, TP:NTOK], 0.0), writes=[oT_b[NCT - 1]])
            chunks = [(0, 0, 0)] + [(1 + 4 * i, 4 + 4 * i, 1) for i in range(4)]
            for (t_a, t_b, _) in chunks:
                gens = []
                for hh in range(2):
                    h0 = hh * 64
                    hs = slice(h0, h0 + 64)
                    qc0 = TT[t_a][0]
                    nq = TT[t_b][0] + TT[t_b][1] - qc0
                    steps = []
                    for kb in range(t_b, -1, -1):
                        k0, nk = TT[kb]
                        if kb >= t_a:
                            lo = k0 - qc0
                            mask = MS1[0:nk, 0:nk]
                        else:
                            lo = 0
                            mask = None
                        steps.append((KT[hs, k0:k0 + nk], Vt[0:nk, kb, h0:h0 + 64], nk, lo, mask, [KT_b, Vt_b]))
                    ob_ = [oT_b[t] for t, (c0, nn) in enumerate(COLT) if c0 < qc0 + nq and qc0 < c0 + nn]
                    gens.append(attn_steps(hh, qc0, nq, steps, oT[hs, qc0:qc0 + nq], ob_))
                run_interleaved(gens)
                if stop <= 7:
                    return
            if do_sample:
                for sq_ in range(2):
                    for h_ in range(2):
                        P.dma("pool", Kp[:, :, h_, :], d["cache_k_in"][sq_, 2 * p + h_, :, :].rearrange("(t k) d -> k t d", k=128), writes=[Kp_b])
                        P.dma("pool", Vp[:, :, h_, :], d["cache_v_in"][sq_, 2 * p + h_, :, :].rearrange("(t k) d -> k t d", k=128), writes=[Vp_b])
                    for t8 in range(4):
                        bkT, bkT_b = next_bank(g)
                        bkT16 = bkT[:].bitcast(BF16)
                        for j in range(8):
                            t = t8 * 8 + j
                            TRP(g, bkT16[:, j * 128:(j + 1) * 128], Kp[:, t, :, :].rearrange("p h d -> p (h d)"), identb, [Kp_b, cb], [bkT_b])
                        CPY(g, "act", KTp[:, t8 * 1024:(t8 + 1) * 1024], bkT16[:, 0:1024], [bkT_b], [KTp_b])
                    gens = []
                    for hh in range(2):
                        h0 = hh * 64
                        hs = slice(h0, h0 + 64)
                        qc0 = TP + sq_ * 64
                        steps = [(KT[hs, TP:TP + 128], Vt[:, 17, h0:h0 + 64], 128, 0, MS2[:, sq_ * 64:sq_ * 64 + 64], [KT_b, Vt_b])]
                        for t in range(31, -1, -1):
                            steps.append((KTp[hs, t * 128:(t + 1) * 128], Vp[:, t, hh, :], 128, 0, None, [KTp_b, Vp_b]))
                        gens.append(attn_steps(hh, qc0, 64, steps, oT[hs, qc0:qc0 + 64], [oT_b[NCT - 1]], nslots=NSL))
                    run_interleaved(gens)
            for dc in range(NCH):
                for t, (c0, n) in enumerate(COLT):
                    bk, bkb = next_bank(g)
                    MM(g, bk[:, 0:n], wo[s][:, dc * 128:(dc + 1) * 128], oT[:, c0:c0 + n], [wo_b[s], oT_b[t]], [bkb])
                    TTO(g, "dve", g.xT[:, dc, c0:c0 + n], g.xT[:, dc, c0:c0 + n], bk[:, 0:n], ALU.add, [g.xT_b[dc][t], bkb], [g.xT_b[dc][t]])
    P.barrier()


def make_consts():
    c = np.zeros((128, NCONST), np.float32)
    c[:, C_IDENT:C_IDENT + 128] = np.eye(128, dtype=np.float32)
    c[:, C_ONES:C_ONES + 128] = 1.0
    kp = np.arange(128)[:, None]
    k = np.arange(128)[None, :]
    c[:, C_TRI_INCL:C_TRI_INCL + 128] = (kp >= k).astype(np.float32)
    c[:, C_TRI_COMP:C_TRI_COMP + 128] = (kp < k).astype(np.float32)
    c[:, C_EPS_RMS] = RMS_EPS
    c[:, C_ONE] = 1.0
    pp = np.arange(128)
    c[:, C_BD:C_BD + 128] = (pp[:, None] // 64 == pp[None, :] // 64).astype(np.float32)
    for ofs, seg in ((C_M1, np.zeros(128, int)), (C_M2, pp // 64)):
        same = (seg[:, None] == seg[None, :])
        s_lt_t = ((pp[:, None] < pp[None, :]) & same).astype(np.float32)
        s_le_t = ((pp[:, None] <= pp[None, :]) & same).astype(np.float32)
        c[:, ofs:ofs + 128] = -s_lt_t
        c[:, ofs + 128:ofs + 256] = -s_lt_t.T
        c[:, ofs + 256:ofs + 384] = s_lt_t
        c[:, ofs + 384:ofs + 512] = s_le_t
    c[:, C_EPS_GN] = GN_EPS
    return c


def fm(vec):
    return np.ascontiguousarray(np.asarray(vec, np.float32).reshape(NCH, 128).T)


def make_pvec(inp):
    cols = [fm(inp["ffn_norm_g"][0, 0]), fm(inp["ffn_norm_g"][0, 1]), fm(inp["ffn_norm_g"][1, 0]), fm(inp["ffn_norm_g"][1, 1]),
            fm(inp["mix_norm_g"][0]), fm(inp["mix_norm_g"][1])]
    for i in range(6):
        cols.append(fm(inp["rwkv_mu"][i]))
    for nm in ("rwkv_w0", "rwkv_a0", "rwkv_k_k", "rwkv_k_a", "rwkv_r_k", "rwkv_gn_w", "rwkv_gn_b"):
        cols.append(fm(np.asarray(inp[nm]).reshape(-1)))
    return np.ascontiguousarray(np.concatenate(cols, axis=1))


ALL_STAGES = {("ffn", 0, 0), ("ffn", 0, 1), ("ffn", 1, 0), ("ffn", 1, 1), "rwkv", "sb"}
_NC_CACHE = {}


def make_in_maps(inputs, ncores=8):
    consts = make_consts()
    pvec = make_pvec(inputs)
    f = lambda a: np.ascontiguousarray(np.asarray(a, np.float32))
    in_maps = []
    gq = np.asarray(inputs["sb_q_norm_g"], np.float32)
    gk = np.asarray(inputs["sb_k_norm_g"], np.float32)
    gqk = np.ascontiguousarray(np.broadcast_to(np.concatenate([gq, gq, gk, gk])[None, :], (128, 256)))
    for i in range(ncores):
        in_maps.append({
            "x_prompt": f(inputs["x_prompt"][i]),
            "x_sample": f(inputs["x_sample"][2 * i:2 * i + 2]).reshape(2 * DSEQ, D),
            "meta_tokens": f(inputs["meta_tokens"]),
            "consts": consts,
            "pvec": pvec,
            "ffn_w_in": f(inputs["ffn_w_in"]),
            "ffn_w_out": f(inputs["ffn_w_out"]),
            "rwkv_w_rkv": f(inputs["rwkv_w_rkv"]), "rwkv_w_o": f(inputs["rwkv_w_o"]),
            "rwkv_w1": f(inputs["rwkv_w1"]), "rwkv_w2": f(inputs["rwkv_w2"]),
            "rwkv_a1": f(inputs["rwkv_a1"]), "rwkv_a2": f(inputs["rwkv_a2"]),
            "rwkv_g1": f(inputs["rwkv_g1"]), "rwkv_g2": f(inputs["rwkv_g2"]),
            "shift_in": np.ascontiguousarray(np.concatenate([fm(inputs["state_rwkv_shift"][2 * i]), fm(inputs["state_rwkv_shift"][2 * i + 1])], 1)),
            "state_wkv_in": f(inputs["state_rwkv_wkv"][2 * i:2 * i + 2]),
            "sb_w_qkv": f(inputs["sb_w_qkv"]), "sb_w_o": f(inputs["sb_w_o"]),
            "gqk": gqk,
        })
        if "cache_sb_k" in inputs:
            in_maps[-1]["cache_k_in"] = f(inputs["cache_sb_k"][2 * i:2 * i + 2])
            in_maps[-1]["cache_v_in"] = f(inputs["cache_sb_v"][2 * i:2 * i + 2])
    return in_maps


def run(inputs, stages=None, ncores=8, trace=False):
    stages = ALL_STAGES if stages is None else stages
    key = tuple(sorted(stages, key=repr))
    if key not in _NC_CACHE:
        _NC_CACHE[key] = build(stages)
    nc = _NC_CACHE[key]
    in_maps = make_in_maps(inputs, ncores)
    res = run_bass_kernel_spmd(nc, in_maps, core_ids=list(range(ncores)), trace=trace)
    if trace:
        print('exec_time_ns', res.exec_time_ns)
    return res.results


def kernel(**inputs):
    r = run(inputs)
    f32 = np.float32
    y_prompt = np.stack([r[i]["y_prompt"] for i in range(8)], 0).astype(f32)
    y_sample = np.concatenate([r[i]["y_sample"].reshape(2, DSEQ, D) for i in range(8)], 0).astype(f32)
    S_p = np.stack([r[i]["state_wkv_p"] for i in range(8)], 0).astype(f32)
    sh_p = np.stack([r[i]["shift_p"].reshape(D) for i in range(8)], 0).astype(f32)
    k_p = np.stack([r[i]["cache_k_p"] for i in range(8)], 0).astype(f32)
    v_p = np.stack([r[i]["cache_v_p"] for i in range(8)], 0).astype(f32)
    S_s = np.concatenate([r[i]["state_wkv_s"] for i in range(8)], 0).astype(f32)
    sh_s = np.concatenate([r[i]["shift_s"].reshape(2, D) for i in range(8)], 0).astype(f32)
    k_s = np.concatenate([r[i]["cache_k_s"] for i in range(8)], 0).astype(f32)
    v_s = np.concatenate([r[i]["cache_v_s"] for i in range(8)], 0).astype(f32)
    return (y_prompt, y_sample, S_p, sh_p, k_p, v_p, S_s, sh_s, k_s, v_s)
```

```python
import contextlib
import numpy as np
import concourse.bass as bass
import concourse.mybir as mybir
from concourse.bass_utils import run_bass_kernel_spmd

F32 = mybir.dt.float32
BF16 = mybir.dt.bfloat16
AF = mybir.ActivationFunctionType
ALU = mybir.AluOpType
AX = mybir.AxisListType

D = 1024
NCH = 8
SEQ = 2048
NMETA = 16
TP = NMETA + SEQ
DSEQ = 64
NTOK = TP + 2 * DSEQ
DFF = 2752
H = 16
N = 64
PAST = 4096
RMS_EPS = 1e-6
GN_EPS = 64e-5

PV = {}
_pv_names = ["ffn_g00", "ffn_g01", "ffn_g10", "ffn_g11", "mix_g0", "mix_g1",
             "mu0", "mu1", "mu2", "mu3", "mu4", "mu5", "w0", "a0", "k_k", "k_a", "r_k", "gn_w", "gn_b"]
for _i, _n in enumerate(_pv_names):
    PV[_n] = _i
NPV = len(_pv_names)

C_IDENT = 0
C_ONES = 128
C_TRI_INCL = 256
C_TRI_COMP = 384
C_EPS_RMS = 512
C_EPS_GN = 513
C_BD = 520
C_M1 = 648
C_M2 = 1160
C_ONE = 514
NCONST = 1672


_UN = [0]


def g_sbuf(nc, name, shape, dt):
    _UN[0] += 1
    return nc.sbuf_tensor("%s_u%d" % (name, _UN[0]), shape, dt)


class Buf:
    __slots__ = ("name", "w", "rs")

    def __init__(self, name=""):
        self.name = name
        self.w = None
        self.rs = {}


class Sched:
    ENG = ("pe", "dve", "act", "pool", "sp")

    def __init__(self, nc, ring=12):
        self.nc = nc
        self.q = {e: [] for e in self.ENG}
        self.cnt = {e: 0 for e in self.ENG}
        self.seen = {e: {} for e in self.ENG}
        self.sems = {}
        for e in self.ENG:
            self.sems[e] = nc.alloc_semaphore("c_" + e)
        self.ring = ring
        self.dma_n = {}
        self.dma_last = {}
        for qn in ("sp", "pool", "act"):
            self.dma_n[qn] = 0
            for s in range(ring):
                self.sems[("dma", qn, s)] = nc.alloc_semaphore("d_%s_%d" % (qn, s))
        self.ninstr = 0

    def _wait(self, eng, key, val):
        if self.seen[eng].get(key, 0) >= val:
            return
        self.seen[eng][key] = val
        self.q[eng].append(("w", key, val))
        self.ninstr += 1

    def _deps(self, eng, reads, writes):
        deps = {}
        for b in reads:
            if b.w is not None:
                k, v = b.w
                if deps.get(k, 0) < v:
                    deps[k] = v
        for b in writes:
            if b.w is not None:
                k, v = b.w
                if deps.get(k, 0) < v:
                    deps[k] = v
            for k, v in b.rs.items():
                if deps.get(k, 0) < v:
                    deps[k] = v
        for k, v in deps.items():
            if eng == "pe" and k == "pe":
                continue
            self._wait(eng, k, v)

    def _mark(self, tok, reads, writes):
        k, v = tok
        for b in reads:
            if b.rs.get(k, 0) < v:
                b.rs[k] = v
        for b in writes:
            b.w = tok
            b.rs = {}

    def op(self, eng, fn, reads=(), writes=()):
        self._deps(eng, reads, writes)
        self.cnt[eng] += 1
        tok = (eng, self.cnt[eng])
        self.q[eng].append(("op", fn, eng, 1))
        self.ninstr += 1
        self._mark(tok, reads, writes)
        return tok

    def dma(self, qn, out, in_, reads=(), writes=(), **kw):
        self._deps(qn, reads, writes)
        n = self.dma_n[qn]
        s = n % self.ring
        val = 16 * (n // self.ring + 1)
        key = ("dma", qn, s)
        if n >= self.ring:
            self._wait(qn, key, val - 16)
        self.dma_n[qn] = n + 1
        fn = lambda e, out=out, in_=in_, kw=kw: e.dma_start(out=out, in_=in_, **kw)
        self.q[qn].append(("op", fn, key, 16))
        self.ninstr += 1
        tok = (key, val)
        self.dma_last[key] = val
        self._mark(tok, reads, writes)
        return tok

    def barrier(self):
        toks = [(e, self.cnt[e]) for e in self.ENG if self.cnt[e] > 0]
        toks += list(self.dma_last.items())
        for e in self.ENG:
            for k, v in toks:
                if k == e and e == "pe":
                    continue
                self._wait(e, k, v)

    def finish(self):
        for k, v in self.dma_last.items():
            self._wait("sp", k, v)
        for e in self.ENG:
            if e != "sp" and self.cnt[e] > 0:
                self._wait("sp", e, self.cnt[e])

    def replay(self, block):
        sems = self.sems

        def mk(name):
            items = self.q[name]

            def body(e):
                for it in items:
                    if it[0] == "w":
                        e.wait_ge(sems[it[1]], it[2])
                    else:
                        ins = it[1](e)
                        ins.then_inc(sems[it[2]], it[3])
            return body

        block.tensor(mk("pe"))
        block.vector(mk("dve"))
        block.scalar(mk("act"))
        block.gpsimd(mk("pool"))
        block.sync(mk("sp"))


def col_tiles():
    t = []
    c = 0
    while c < NTOK:
        n = min(512, NTOK - c)
        t.append((c, n))
        c += n
    return t


COLT = col_tiles()
NCT = len(COLT)
TT = [(0, NMETA)] + [(NMETA + 128 * i, 128) for i in range(16)] + [(TP, 128)]


class Ctx:
    pass


def build(stages):
    nc = bass.Bass("TRN2", target_bir_lowering=False)
    P = Sched(nc)
    g = Ctx()
    g.nc, g.P = nc, P
    g.rwkv_tiles = None
    g.do_sample = "nosample" not in stages
    g.sb_pairs = None
    for st in stages:
        if isinstance(st, tuple) and st[0] == "sb_pairs":
            g.sb_pairs = list(st[1])
    for st in stages:
        if isinstance(st, tuple) and st[0] == "rwkv_tiles":
            g.rwkv_tiles = list(st[1])
        if isinstance(st, tuple) and st[0] == "sb_stop":
            g.sb_stop = float(st[1])
        if isinstance(st, tuple) and st[0] == "rwkv_stop":
            g.rwkv_stop = float(st[1])
    dram = {}

    def din(name, shape, dt=F32):
        dram[name] = nc.dram_tensor(name, list(shape), dt, kind="ExternalInput").ap()
        return dram[name]

    def dout(name, shape, dt=F32):
        dram[name] = nc.dram_tensor(name, list(shape), dt, kind="ExternalOutput").ap()
        return dram[name]

    g.dram = dram
    din("x_prompt", (SEQ, D))
    din("x_sample", (2 * DSEQ, D))
    din("meta_tokens", (NMETA, D))
    din("consts", (128, NCONST))
    din("pvec", (128, NPV * 8))
    din("ffn_w_in", (2, 2, D, 2 * DFF))
    din("ffn_w_out", (2, 2, DFF, D))
    din("rwkv_w_rkv", (3, D, D))
    din("rwkv_w_o", (D, D))
    din("rwkv_w1", (D, 64))
    din("rwkv_w2", (64, D))
    din("rwkv_a1", (D, 64))
    din("rwkv_a2", (64, D))
    din("rwkv_g1", (D, 128))
    din("rwkv_g2", (128, D))
    din("sb_w_qkv", (D, 3 * D))
    din("sb_w_o", (D, D))
    din("gqk", (128, 256))
    if g.do_sample:
        din("cache_k_in", (2, H, PAST, N))
        din("cache_v_in", (2, H, PAST, N))
    dout("cache_k_p", (H, TP, N))
    dout("cache_v_p", (H, TP, N))
    dout("cache_k_s", (2, H, DSEQ, N))
    dout("cache_v_s", (2, H, DSEQ, N))
    din("shift_in", (128, 16))
    din("state_wkv_in", (2, H, N, N))
    dout("state_wkv_p", (H, N, N))
    dout("shift_p", (8, 128))
    dout("state_wkv_s", (2, H, N, N))
    dout("shift_s", (2, 8, 128))
    dout("y_prompt", (SEQ, D))
    dout("y_sample", (2 * DSEQ, D))

    g.xT = nc.alloc_sbuf_tensor("xT", [128, NCH, NTOK], F32)
    g.xT_b = [[Buf("xT") for _ in range(NCT)] for _ in range(NCH)]
    g.consts = nc.alloc_sbuf_tensor("consts_sb", [128, NCONST], F32)
    g.consts_bf = nc.alloc_sbuf_tensor("consts_bf", [128, 256], BF16)
    g.pvec = nc.alloc_sbuf_tensor("pvec_sb", [128, NPV * 8], F32)
    g.cb = Buf("consts")
    g.banks = [nc.alloc_psum_tensor("bank%d" % i, [128, 512], F32) for i in range(8)]
    g.bank_b = [Buf("bank%d" % i) for i in range(8)]
    g.bank_rr = 0

    P.dma("sp", g.consts[:], dram["consts"], writes=[g.cb])
    P.dma("sp", g.pvec[:], dram["pvec"], writes=[g.cb])
    P.op("dve", lambda e: e.tensor_copy(out=g.consts_bf[:], in_=g.consts[:, 0:256]), reads=[g.cb], writes=[g.cb])

    load_x(g)
    for li in range(2):
        if ("ffn", li, 0) in stages:
            ffn(g, li, 0)
        if li == 0 and "rwkv" in stages:
            rwkv(g, tiles=g.rwkv_tiles)
        if li == 1 and "sb" in stages:
            sb_attn(g, do_sample=g.do_sample, pairs=g.sb_pairs)
        if ("ffn", li, 1) in stages:
            ffn(g, li, 1)
    store_y(g)
    P.finish()
    with nc.Block() as block:
        P.replay(block)
    print("instructions:", P.ninstr)
    return nc


def xt_bufs_for_cols(g, c, col0, n):
    out = []
    for t, (c0, nn) in enumerate(COLT):
        if c0 < col0 + n and col0 < c0 + nn:
            out.append(g.xT_b[c][t])
    return out


def load_x(g):
    nc, P = g.nc, g.P
    ident = g.consts[:, C_IDENT:C_IDENT + 128]
    with contextlib.ExitStack() as es:
        stg = [es.enter_context(g_sbuf(nc, "ldstg%d" % i, [128, D], F32)) for i in range(3)]
        stg_b = [Buf("ldstg") for _ in range(3)]
        for ti, (col0, n) in enumerate(TT):
            s = ti % 3
            if ti == 0:
                src = g.dram["meta_tokens"]
            elif ti <= 16:
                src = g.dram["x_prompt"][(ti - 1) * 128:ti * 128, :]
            else:
                src = g.dram["x_sample"]
            P.dma("sp", stg[s][0:n, :], src, writes=[stg_b[s]])
            for half in range(2):
                bk = (ti * 2 + half) % 2 + 6
                bank = g.banks[bk]
                for cc in range(4):
                    c = half * 4 + cc
                    P.op("pe", lambda e, bank=bank, cc=cc, c=c, s=s, n=n: e.transpose(
                        out=bank[:, cc * 128:cc * 128 + n], in_=stg[s][0:n, c * 128:(c + 1) * 128],
                        identity=ident[0:n, 0:n]),
                        reads=[stg_b[s], g.cb], writes=[g.bank_b[bk]])
                wb = []
                for cc in range(4):
                    wb += xt_bufs_for_cols(g, half * 4 + cc, col0, n)
                src_ap = bank[:].rearrange("p (a b) -> p a b", a=4)[:, :, 0:n]
                dst_ap = g.xT[:, half * 4:half * 4 + 4, col0:col0 + n]
                eng = "act" if half == 0 else "dve"
                if eng == "act":
                    P.op("act", lambda e, d=dst_ap, s_=src_ap: e.copy(out=d, in_=s_), reads=[g.bank_b[bk]], writes=wb)
                else:
                    P.op("dve", lambda e, d=dst_ap, s_=src_ap: e.tensor_copy(out=d, in_=s_), reads=[g.bank_b[bk]], writes=wb)
        P.barrier()


def store_y(g):
    nc, P = g.nc, g.P
    ident = g.consts[:, C_IDENT:C_IDENT + 128]
    with contextlib.ExitStack() as es:
        stg = [es.enter_context(g_sbuf(nc, "ststg%d" % i, [128, D], F32)) for i in range(3)]
        stg_b = [Buf("ststg") for _ in range(3)]
        k = 0
        for ti, (col0, n) in enumerate(TT):
            if ti == 0:
                continue
            s = k % 3
            k += 1
            for half in range(2):
                bk = (ti * 2 + half) % 2 + 6
                bank = g.banks[bk]
                for cc in range(4):
                    c = half * 4 + cc
                    P.op("pe", lambda e, bank=bank, cc=cc, c=c, col0=col0, n=n: e.transpose(
                        out=bank[0:n, cc * 128:(cc + 1) * 128], in_=g.xT[:, c, col0:col0 + n],
                        identity=ident),
                        reads=xt_bufs_for_cols(g, c, col0, n) + [g.cb], writes=[g.bank_b[bk]])
                dst_ap = stg[s][0:n, half * 512:(half + 1) * 512]
                src_ap = bank[0:n, :]
                if half == 0:
                    P.op("act", lambda e, d=dst_ap, s_=src_ap: e.copy(out=d, in_=s_), reads=[g.bank_b[bk]], writes=[stg_b[s]])
                else:
                    P.op("dve", lambda e, d=dst_ap, s_=src_ap: e.tensor_copy(out=d, in_=s_), reads=[g.bank_b[bk]], writes=[stg_b[s]])
            if ti <= 16:
                dst = g.dram["y_prompt"][(ti - 1) * 128:ti * 128, :]
            else:
                dst = g.dram["y_sample"]
            P.dma("sp", dst, stg[s][0:n, :], reads=[stg_b[s]])
        P.barrier()


def rmsnorm_T(g, es, gname, out_dt=BF16):
    nc, P = g.nc, g.P
    xn = es.enter_context(g_sbuf(nc, "xn", [128, NCH, NTOK], out_dt))
    xn_b = [[Buf("xn") for _ in range(NCT)] for _ in range(NCH)]
    sq = [es.enter_context(g_sbuf(nc, "sq%d" % i, [128, 512], BF16)) for i in range(2)]
    sq_b = [Buf("sq") for _ in range(2)]
    rt = [es.enter_context(g_sbuf(nc, "rt%d" % i, [128, 512], F32)) for i in range(2)]
    rt_b = [Buf("rt") for _ in range(2)]
    ones = g.consts_bf[:, C_ONES:C_ONES + 128]
    gcol = PV[gname] * 8
    k = 0
    for t, (c0, n) in enumerate(COLT):
        bk = 6 + (t % 2)
        bank = g.banks[bk]
        for c in range(NCH):
            s = k % 2
            k += 1
            P.op("act", lambda e, s=s, c=c, c0=c0, n=n: e.activation(out=sq[s][:, 0:n], in_=g.xT[:, c, c0:c0 + n], func=AF.Square),
                 reads=[g.xT_b[c][t]], writes=[sq_b[s]])
            P.op("pe", lambda e, s=s, c=c, n=n, bank=bank: e.matmul(bank[:, 0:n], lhsT=ones, rhs=sq[s][:, 0:n], start=(c == 0), stop=(c == NCH - 1)),
                 reads=[sq_b[s], g.cb], writes=[g.bank_b[bk]])
        r = t % 2
        P.op("act", lambda e, r=r, n=n, bank=bank: e.activation(out=rt[r][:, 0:n], in_=bank[:, 0:n], func=AF.Sqrt, scale=1.0 / D, bias=g.consts[:, C_EPS_RMS:C_EPS_RMS + 1]),
             reads=[g.bank_b[bk], g.cb], writes=[rt_b[r]])
        P.op("dve", lambda e, r=r, n=n: e.reciprocal(out=rt[r][:, 0:n], in_=rt[r][:, 0:n]), reads=[rt_b[r]], writes=[rt_b[r]])
        for c in range(NCH):
            P.op("dve", lambda e, r=r, c=c, c0=c0, n=n: e.scalar_tensor_tensor(
                out=xn[:, c, c0:c0 + n], in0=g.xT[:, c, c0:c0 + n], scalar=g.pvec[:, gcol + c:gcol + c + 1],
                in1=rt[r][:, 0:n], op0=ALU.mult, op1=ALU.mult),
                reads=[g.xT_b[c][t], rt_b[r], g.cb], writes=[xn_b[c][t]])
    return xn, xn_b


def ffn(g, li, fi):
    nc, P = g.nc, g.P
    w_in = g.dram["ffn_w_in"][li, fi]
    w_out = g.dram["ffn_w_out"][li, fi]
    GS = 4
    groups = []
    j = 0
    while j < 22:
        groups.append(list(range(j, min(j + GS, 22))))
        j += GS
    csize = lambda j: 128 if j < 21 else 64
    with contextlib.ExitStack() as es:
        xn, xn_b = rmsnorm_T(g, es, "ffn_g%d%d" % (li, fi))
        act = [es.enter_context(g_sbuf(nc, "act%d" % i, [128, GS, NTOK], BF16)) for i in range(2)]
        act_b = [[[Buf("act") for _ in range(NCT)] for _ in range(GS)] for _ in range(2)]
        wi = [es.enter_context(g_sbuf(nc, "wi%d" % i, [128, NCH, 2, GS * 128], BF16)) for i in range(2)]
        wi_b = [Buf("wi") for _ in range(2)]
        wo = [es.enter_context(g_sbuf(nc, "wo%d" % i, [128, GS, D], BF16)) for i in range(2)]
        wo_b = [Buf("wo") for _ in range(2)]
        sl = [es.enter_context(g_sbuf(nc, "sl%d" % i, [128, 512], F32)) for i in range(2)]
        sl_b = [Buf("sl") for _ in range(2)]
        w_in_v = w_in.rearrange("(c p) n -> p c n", p=128)

        def load_w(gi):
            grp = groups[gi]
            s = gi % 2
            col0 = grp[0] * 128
            ncols = sum(csize(j) for j in grp)
            P.dma("pool", wi[s][:, :, 0, 0:ncols], w_in_v[:, :, col0:col0 + ncols], writes=[wi_b[s]])
            P.dma("pool", wi[s][:, :, 1, 0:ncols], w_in_v[:, :, DFF + col0:DFF + col0 + ncols], writes=[wi_b[s]])
            nfull = sum(1 for j in grp if csize(j) == 128)
            if nfull:
                P.dma("pool", wo[s][:, 0:nfull, :],
                      w_out[col0:col0 + nfull * 128, :].rearrange("(g p) n -> p g n", p=128), writes=[wo_b[s]])
            if nfull < len(grp):
                r0 = col0 + nfull * 128
                P.dma("pool", wo[s][0:64, nfull, :], w_out[r0:r0 + 64, :], writes=[wo_b[s]])

        kk = [0]

        def phase_a(gi):
            grp = groups[gi]
            s = gi % 2
            for jj, j in enumerate(grp):
                m = csize(j)
                for t, (c0, n) in enumerate(COLT):
                    q = kk[0] % 2
                    kk[0] += 1
                    bg, bu = g.banks[q * 2], g.banks[q * 2 + 1]
                    for c in range(NCH):
                        P.op("pe", lambda e, bg=bg, c=c, jj=jj, m=m, c0=c0, n=n, s=s: e.matmul(
                            bg[0:m, 0:n], lhsT=wi[s][:, c, 0, jj * 128:jj * 128 + m], rhs=xn[:, c, c0:c0 + n],
                            start=(c == 0), stop=(c == NCH - 1)),
                            reads=[wi_b[s], xn_b[c][t]], writes=[g.bank_b[q * 2]])
                    for c in range(NCH):
                        P.op("pe", lambda e, bu=bu, c=c, jj=jj, m=m, c0=c0, n=n, s=s: e.matmul(
                            bu[0:m, 0:n], lhsT=wi[s][:, c, 1, jj * 128:jj * 128 + m], rhs=xn[:, c, c0:c0 + n],
                            start=(c == 0), stop=(c == NCH - 1)),
                            reads=[wi_b[s], xn_b[c][t]], writes=[g.bank_b[q * 2 + 1]])
                    P.op("act", lambda e, q=q, bg=bg, m=m, n=n: e.activation(out=sl[q][0:m, 0:n], in_=bg[0:m, 0:n], func=AF.Silu),
                         reads=[g.bank_b[q * 2]], writes=[sl_b[q]])
                    P.op("dve", lambda e, q=q, bu=bu, m=m, n=n, s=s, jj=jj, c0=c0: e.tensor_tensor(
                        out=act[s][0:m, jj, c0:c0 + n], in0=sl[q][0:m, 0:n], in1=bu[0:m, 0:n], op=ALU.mult),
                        reads=[sl_b[q], g.bank_b[q * 2 + 1]], writes=[act_b[s][jj][t]])

        ko = [0]

        def phase_b(gi):
            grp = groups[gi]
            s = gi % 2
            for dc in range(NCH):
                for t, (c0, n) in enumerate(COLT):
                    bk = 4 + ko[0] % 2
                    ko[0] += 1
                    bank = g.banks[bk]
                    for jj, j in enumerate(grp):
                        m = csize(j)
                        P.op("pe", lambda e, bank=bank, jj=jj, m=m, dc=dc, c0=c0, n=n, s=s: e.matmul(
                            bank[:, 0:n], lhsT=wo[s][0:m, jj, dc * 128:(dc + 1) * 128], rhs=act[s][0:m, jj, c0:c0 + n],
                            start=(jj == 0), stop=(jj == len(grp) - 1)),
                            reads=[wo_b[s], act_b[s][jj][t]], writes=[g.bank_b[bk]])
                    P.op("dve", lambda e, bank=bank, dc=dc, c0=c0, n=n: e.scalar_tensor_tensor(
                        out=g.xT[:, dc, c0:c0 + n], in0=bank[:, 0:n], scalar=0.5, in1=g.xT[:, dc, c0:c0 + n],
                        op0=ALU.mult, op1=ALU.add),
                        reads=[g.bank_b[bk], g.xT_b[dc][t]], writes=[g.xT_b[dc][t]])

        ng = len(groups)
        load_w(0)
        load_w(1)
        phase_a(0)
        for gi in range(ng):
            if gi + 1 < ng:
                phase_a(gi + 1)
            phase_b(gi)
            if gi + 2 < ng:
                load_w(gi + 2)
        P.barrier()


def MM(g, out, lhsT, rhs, r, w, start=True, stop=True, skip=False):
    return g.P.op("pe", lambda e: e.matmul(out, lhsT=lhsT, rhs=rhs, start=start, stop=stop, skip_group_check=skip), reads=r, writes=w)


def TRP(g, out, in_, ident, r, w):
    return g.P.op("pe", lambda e: e.transpose(out=out, in_=in_, identity=ident), reads=r, writes=w)


def ACTF(g, out, in_, func, r, w, bias=None, scale=None):
    kw = {}
    if bias is not None:
        kw["bias"] = bias
    if scale is not None:
        kw["scale"] = scale
    return g.P.op("act", lambda e: e.activation(out=out, in_=in_, func=func, **kw), reads=r, writes=w)


def CPY(g, eng, out, in_, r, w):
    if eng == "act":
        return g.P.op("act", lambda e: e.copy(out=out, in_=in_), reads=r, writes=w)
    return g.P.op(eng, lambda e: e.tensor_copy(out=out, in_=in_), reads=r, writes=w)


def TTO(g, eng, out, in0, in1, op, r, w):
    return g.P.op(eng, lambda e: e.tensor_tensor(out=out, in0=in0, in1=in1, op=op), reads=r, writes=w)


def TSC(g, eng, out, in0, s1, s2, op0, op1, r, w):
    if s2 is None:
        return g.P.op(eng, lambda e: e.tensor_scalar(out=out, in0=in0, scalar1=s1, scalar2=None, op0=op0), reads=r, writes=w)
    return g.P.op(eng, lambda e: e.tensor_scalar(out=out, in0=in0, scalar1=s1, scalar2=s2, op0=op0, op1=op1), reads=r, writes=w)


def STT(g, out, in0, scalar, in1, op0, op1, r, w):
    return g.P.op("dve", lambda e: e.scalar_tensor_tensor(out=out, in0=in0, scalar=scalar, in1=in1, op0=op0, op1=op1), reads=r, writes=w)


def run_interleaved(gens):
    gens = list(gens)
    while gens:
        for gen in list(gens):
            try:
                next(gen)
            except StopIteration:
                gens.remove(gen)


def next_bank(g):
    i = g.bank_rr % 8
    g.bank_rr += 1
    return g.banks[i], g.bank_b[i]


def pvc(g, name, c):
    j = PV[name] * 8 + c
    return g.pvec[:, j:j + 1]


class StopRwkv(Exception):
    pass


def rwkv(g, tiles=None):
    try:
        _rwkv(g, tiles)
    except StopRwkv:
        pass
    g.P.barrier()


def _rwkv(g, tiles=None):
    stop = getattr(g, "rwkv_stop", 99)
    nc, P, d = g.nc, g.P, g.dram
    RTT = TT[:17] + [(TP, 128), (TP, 128)]
    tiles = list(range(len(RTT))) if tiles is None else tiles
    C0 = float(np.exp(-0.5))
    cb = g.cb
    identb = g.consts_bf[:, C_IDENT:C_IDENT + 128]
    identf = g.consts[:, C_IDENT:C_IDENT + 128]
    onesb = g.consts_bf[:, C_ONES:C_ONES + 128]
    onesf = g.consts[:, C_ONES:C_ONES + 128]
    BDf = g.consts[:, C_BD:C_BD + 128]
    with contextlib.ExitStack() as es:
        def sb(name, shape, dt):
            return es.enter_context(g_sbuf(nc, "rk_" + name, shape, dt))
        wb = Buf("rwkv_w")
        Wr, Wk, Wv, Wo = (sb(nm, [128, NCH, D], BF16) for nm in ("Wr", "Wk", "Wv", "Wo"))
        for i, W in enumerate((Wr, Wk, Wv)):
            P.dma("pool", W[:], d["rwkv_w_rkv"][i].rearrange("(c p) n -> p c n", p=128), writes=[wb])
        P.dma("pool", Wo[:], d["rwkv_w_o"].rearrange("(c p) n -> p c n", p=128), writes=[wb])
        w1 = sb("w1", [128, NCH, 64], BF16)
        a1 = sb("a1", [128, NCH, 64], BF16)
        g1 = sb("g1", [128, NCH, 128], BF16)
        w2 = sb("w2", [64, D], BF16)
        a2 = sb("a2", [64, D], BF16)
        g2 = sb("g2", [128, D], BF16)
        P.dma("pool", w1[:], d["rwkv_w1"].rearrange("(c p) n -> p c n", p=128), writes=[wb])
        P.dma("pool", a1[:], d["rwkv_a1"].rearrange("(c p) n -> p c n", p=128), writes=[wb])
        P.dma("pool", g1[:], d["rwkv_g1"].rearrange("(c p) n -> p c n", p=128), writes=[wb])
        P.dma("pool", w2[:], d["rwkv_w2"], writes=[wb])
        P.dma("pool", a2[:], d["rwkv_a2"], writes=[wb])
        P.dma("pool", g2[:], d["rwkv_g2"], writes=[wb])
        shift_sb = sb("shift", [128, 16], F32)
        P.dma("sp", shift_sb[:], d["shift_in"], writes=[wb])
        omka = sb("omka", [128, 8], F32)
        ka0 = PV["k_a"] * 8
        TSC(g, "dve", omka[:], g.pvec[:, ka0:ka0 + 8], -1.0, 1.0, ALU.mult, ALU.add, [cb], [wb])

        M32 = [sb("M32_%d" % c, [128, 8, 64], F32) for c in range(3)]
        Mb = [sb("Mb_%d" % c, [128, 8, 64], BF16) for c in range(3)]
        M_b = [[[Buf("M"), Buf("M")] for _ in range(8)] for _ in range(3)]
        _m0 = [b for pr in M_b[0] for b in pr]
        P.op("pool", lambda e: e.memset(M32[0][:], 0.0), writes=_m0)
        P.op("pool", lambda e: e.memset(Mb[0][:], 0.0), writes=_m0)
        s0stg = sb("s0stg", [64, 16, 64], F32)
        s0_b = Buf("s0stg")
        for sq_ in range(2):
            P.dma("sp", s0stg[:], d["state_wkv_in"][sq_].rearrange("h v k -> v h k"), writes=[s0_b])
            for p in range(8):
                bk, bb = next_bank(g)
                TRP(g, bk[:, 0:64], s0stg[:, 2 * p:2 * p + 2, :].rearrange("v h k -> v (h k)"), identf[0:64, 0:64], [s0_b, cb], [bb])
                CPY(g, "act", M32[1 + sq_][:, p, :], bk[:, 0:64], [bb], M_b[1 + sq_][p])
                CPY(g, "dve", Mb[1 + sq_][:, p, :], bk[:, 0:64], [bb], M_b[1 + sq_][p])

        hb1 = sb("hb", [128, NCH, 132], F32)
        hb1_b = Buf("hb")
        hlast = sb("hlast", [128, NCH, 1], F32)
        hlast_b = Buf("hlast")
        dx = sb("dx", [128, NCH, 128], F32)
        dx_b = Buf("dx")
        sqt = sb("sqt", [128, NCH, 128], BF16)
        sq_b = Buf("sq")
        rt = sb("rt", [128, 128], F32)
        rt_b = Buf("rt")
        mixL = sb("mixL", [128, NCH, 128], BF16)
        mixL_b = [Buf("mixL") for _ in range(NCH)]
        mix = {i: sb("mix%d" % i, [128, NCH, 128], BF16) for i in (0, 2, 3)}
        mix_b = {i: [Buf("mix") for _ in range(NCH)] for i in (0, 2, 3)}
        for i in (1, 4, 5):
            mix[i] = mixL
            mix_b[i] = mixL_b
        tw = sb("tw", [64, 128], BF16)
        ta = sb("ta", [64, 128], BF16)
        tg = sb("tg", [128, 128], BF16)
        tw_b, ta_b, tg_b = Buf("tw"), Buf("ta"), Buf("tg")
        ogT = sb("ogT", [128, 8, 128], BF16)
        og_b = [Buf("og") for _ in range(8)]

        def F(name):
            return sb(name, [128, 128], F32), Buf(name)
        sg, sg_b = F("sg")
        av, av_b = F("av")
        gsb, gsb_b = F("gsb")
        vsb, vsb_b = F("vsb")
        kkraw, kkraw_b = F("kkraw")
        kksq, kksq_b = F("kksq")
        nrm, nrm_b = F("nrm")
        kkn, kkn_b = F("kkn")
        t1, t1_b = F("t1")
        kmod, kmod_b = F("kmod")
        bb_, bb_b = F("bb")
        cs, cs_b = F("cs")
        csx, csx_b = F("csx")
        E1, E1_b = F("E1")
        E2, E2_b = F("E2")
        E3, E3_b = F("E3")
        E4, E4_b = F("E4")
        nb = sb("nb", [128, 2], F32)
        nb_b = Buf("nb")
        rkr, rkr_b = F("rkr")
        bonus, bonus_b = F("bonus")
        kr = sb("kr", [128, 2, 128], BF16)
        kr_b = [Buf("kr0"), Buf("kr1")]
        Bh = sb("Bh", [128, 128], BF16)
        Kh = sb("Kh", [128, 128], BF16)
        Bh_b, Kh_b = Buf("Bh"), Buf("Kh")
        src4 = sb("src4", [128, 4, 128], BF16)
        s4_b = [Buf("src4") for _ in range(4)]
        tok4 = sb("tok4", [128, 4, 128], BF16)
        tok4_b = Buf("tok4")
        XX = [[sb("XX%d_%d" % (h, i), [128, 2, 128], BF16) for i in range(2)] for h in range(2)]
        XX_b = [[Buf("XX") for _ in range(2)] for _ in range(2)]
        YY = [[sb("YY%d_%d" % (h, i), [128, 128], BF16) for i in range(2)] for h in range(2)]
        YY_b = [[Buf("YY") for _ in range(2)] for _ in range(2)]
        LkTm = [sb("LkTm%d" % h, [128, 128], BF16) for h in range(2)]
        AbTm = [sb("AbTm%d" % h, [128, 128], BF16) for h in range(2)]
        AkTm = [sb("AkTm%d" % h, [128, 128], BF16) for h in range(2)]
        LkT_b = [Buf("LkT") for _ in range(2)]
        AbT_b = [Buf("AbT") for _ in range(2)]
        AkT_b = [Buf("AkT") for _ in range(2)]
        Wt = [sb("Wt%d" % h, [128, 64], BF16) for h in range(2)]
        Uv = [sb("Uv%d" % h, [128, 64], BF16) for h in range(2)]
        Wt_b = [Buf("Wt") for _ in range(2)]
        Uv_b = [Buf("Uv") for _ in range(2)]
        QtT = sb("QtT", [128, 128], BF16)
        QtT_b = [Buf("QtT") for _ in range(2)]
        Gneg = sb("Gneg", [128, 2, 64], BF16)
        Gneg_b = [Buf("Gneg") for _ in range(2)]
        osb = sb("osb", [128, 2, 128], F32)
        osb_b = Buf("osb")
        gm, gm_b = kkraw, kkraw_b
        gm2, gm2_b = kksq, kksq_b
        gv, gv_b = nrm, nrm_b
        outstg = s0stg
        outstg_b = s0_b
        shstg = sb("shstg", [8, 128], F32)
        shstg_b = Buf("shstg")

        if stop <= 1:
            return
        for ti in tiles:
            col0, n = RTT[ti]
            sample = (ti >= 17)
            segs = [(0, n, ti - 16)] if sample else [(0, n, 0)]
            real0, realn = ((ti - 17) * 64, 64) if sample else (0, n)
            pad0 = (64 - real0) if sample else None
            maxseg = max(s[1] for s in segs)
            L = max(1, int(np.ceil(np.log2(maxseg))))
            mofs = C_M1
            MSN = g.consts[0:n, mofs + 0:mofs + n]
            MSNT = g.consts[0:n, mofs + 128:mofs + 128 + n]
            MS = g.consts[0:n, mofs + 256:mofs + 256 + n]
            MI = g.consts[0:n, mofs + 384:mofs + 384 + n]
            hcur = hb1
            hcur_b = hb1_b
            xb_all = []
            for c in range(NCH):
                xb_all += xt_bufs_for_cols(g, c, col0, n)
            ACTF(g, sqt[:, :, 0:n], g.xT[:, :, col0:col0 + n], AF.Square, xb_all, [sq_b])
            bk, bkb = next_bank(g)
            for c in range(NCH):
                MM(g, bk[:, 0:n], onesb, sqt[:, c, 0:n], [sq_b, cb], [bkb], start=(c == 0), stop=(c == NCH - 1))
            ACTF(g, rt[:, 0:n], bk[:, 0:n], AF.Sqrt, [bkb, cb], [rt_b], bias=g.consts[:, C_EPS_RMS:C_EPS_RMS + 1], scale=1.0 / D)
            P.op("dve", lambda e, n=n: e.reciprocal(out=rt[:, 0:n], in_=rt[:, 0:n]), reads=[rt_b], writes=[rt_b])
            for c in range(NCH):
                STT(g, hcur[:, c, 1:n + 1], g.xT[:, c, col0:col0 + n], pvc(g, "mix_g0", c), rt[:, 0:n], ALU.mult, ALU.mult,
                    xt_bufs_for_cols(g, c, col0, n) + [rt_b, cb], [hcur_b])
            if ti == 0:
                P.op("pool", lambda e, hcur=hcur: e.memset(hcur[:, :, 0:1], 0.0), writes=[hcur_b])
            elif sample:
                CPY(g, "pool", hcur[:, :, 0:1], shift_sb[:, 0:8].rearrange("p (c o) -> p c o", o=1), [wb], [hcur_b])
            else:
                CPY(g, "pool", hcur[:, :, 0:1], hlast[:], [hlast_b], [hcur_b])
            TTO(g, "dve", dx[:, :, 0:n], hcur[:, :, 0:n], hcur[:, :, 1:n + 1], ALU.subtract, [hcur_b], [dx_b])
            if not sample:
                CPY(g, "pool", hlast[:], hcur[:, :, n:n + 1], [hcur_b], [hlast_b])
            if ti == 18:
                TTO(g, "dve", dx[:, :, 64:65], shift_sb[:, 8:16].rearrange("p (c o) -> p c o", o=1), hcur[:, :, 65:66], ALU.subtract,
                    [hcur_b, wb], [dx_b])
            def do_mix(i):
                for c in range(NCH):
                    STT(g, mix[i][:, c, 0:n], dx[:, c, 0:n], pvc(g, "mu%d" % i, c), hcur[:, c, 1:n + 1], ALU.mult, ALU.add,
                        [dx_b, hcur_b, cb], [mix_b[i][c]])
            for i in (0, 2, 3):
                do_mix(i)
            if ti == 16 or sample:
                for (s0, sn, ch) in segs:
                    bk, bkb = next_bank(g)
                    lc = real0 + realn
                    TRP(g, bk[0:8, 0:128], hcur[:, :, lc:lc + 1].rearrange("p c o -> p (c o)"), identf, [hcur_b, cb], [bkb])
                    CPY(g, "act", shstg[:], bk[0:8, 0:128], [bkb], [shstg_b])
                    dst = d["shift_p"] if ch == 0 else d["shift_s"][ch - 1]
                    P.dma("sp", dst, shstg[:], reads=[shstg_b])
            do_mix(1)
            bk, bkb = next_bank(g)
            for c in range(NCH):
                MM(g, bk[0:64, 0:n], w1[:, c, :], mix[1][:, c, 0:n], [wb, mix_b[1][c]], [bkb], start=(c == 0), stop=(c == NCH - 1))
            ACTF(g, tw[:, 0:n], bk[0:64, 0:n], AF.Tanh, [bkb], [tw_b])
            do_mix(4)
            bk, bkb = next_bank(g)
            for c in range(NCH):
                MM(g, bk[0:64, 0:n], a1[:, c, :], mix[4][:, c, 0:n], [wb, mix_b[4][c]], [bkb], start=(c == 0), stop=(c == NCH - 1))
            CPY(g, "act", ta[:, 0:n], bk[0:64, 0:n], [bkb], [ta_b])
            do_mix(5)
            bk, bkb = next_bank(g)
            for c in range(NCH):
                MM(g, bk[:, 0:n], g1[:, c, :], mix[5][:, c, 0:n], [wb, mix_b[5][c]], [bkb], start=(c == 0), stop=(c == NCH - 1))
            ACTF(g, tg[:, 0:n], bk[:, 0:n], AF.Sigmoid, [bkb], [tg_b])

            if stop <= 2:
                return
            for p in range(8):
                pc = p * 128
                bkA, bkA_b = next_bank(g)
                for j, (W, mi) in enumerate(((Wr, 0), (Wk, 2), (Wv, 3))):
                    for c in range(NCH):
                        MM(g, bkA[:, j * 128:j * 128 + n], W[:, c, pc:pc + 128], mix[mi][:, c, 0:n], [wb, mix_b[mi][c]], [bkA_b],
                           start=(c == 0), stop=(c == NCH - 1))
                MM(g, bkA[:, 384:384 + n], w2[:, pc:pc + 128], tw[:, 0:n], [wb, tw_b], [bkA_b])
                bkB, bkB_b = next_bank(g)
                MM(g, bkB[:, 0:n], a2[:, pc:pc + 128], ta[:, 0:n], [wb, ta_b], [bkB_b])
                MM(g, bkB[:, 128:128 + n], g2[:, pc:pc + 128], tg[:, 0:n], [wb, tg_b], [bkB_b])
                if stop <= 2.1:
                    return
                r_ps = bkA[:, 0:n]
                k_ps = bkA[:, 128:128 + n]
                v_ps = bkA[:, 256:256 + n]
                ACTF(g, sg[:, 0:n], bkA[:, 384:384 + n], AF.Sigmoid, [bkA_b, cb], [sg_b], bias=pvc(g, "w0", p))
                if sample:
                    P.op("pool", lambda e, pad0=pad0: e.memset(sg[:, pad0:pad0 + 64], 0.0), writes=[sg_b])
                ACTF(g, av[:, 0:n], bkB[:, 0:n], AF.Sigmoid, [bkB_b, cb], [av_b], bias=pvc(g, "a0", p))
                if stop <= 2.15:
                    return
                CPY(g, "act", gsb[:, 0:n], bkB[:, 128:128 + n], [bkB_b], [gsb_b])
                CPY(g, "act", vsb[:, 0:n], v_ps, [bkA_b], [vsb_b])
                if stop <= 2.17:
                    return
                CPY(g, "pool", src4[:, 3, 0:n], vsb[:, 0:n], [vsb_b], [s4_b[3]])
                if stop <= 2.18:
                    return
                ACTF(g, kkraw[:, 0:n], k_ps, AF.Copy, [bkA_b, cb], [kkraw_b], scale=pvc(g, "k_k", p))
                if stop <= 2.2:
                    return
                ACTF(g, kksq[:, 0:n], kkraw[:, 0:n], AF.Square, [kkraw_b], [kksq_b])
                bkC, bkC_b = next_bank(g)
                MM(g, bkC[:, 0:n], BDf, kksq[:, 0:n], [cb, kksq_b], [bkC_b])
                ACTF(g, nrm[:, 0:n], bkC[:, 0:n], AF.Sqrt, [bkC_b], [nrm_b])
                TSC(g, "dve", nrm[:, 0:n], nrm[:, 0:n], 1e-12, None, ALU.max, None, [nrm_b], [nrm_b])
                P.op("dve", lambda e, n=n: e.reciprocal(out=nrm[:, 0:n], in_=nrm[:, 0:n]), reads=[nrm_b], writes=[nrm_b])
                if stop <= 2.4:
                    return
                TTO(g, "dve", kkn[:, 0:n], kkraw[:, 0:n], nrm[:, 0:n], ALU.mult, [kkraw_b, nrm_b], [kkn_b])
                TSC(g, "dve", t1[:, 0:n], av[:, 0:n], pvc(g, "k_a", p), omka[:, p:p + 1], ALU.mult, ALU.add, [av_b, cb, wb], [t1_b])
                TTO(g, "dve", kmod[:, 0:n], k_ps, t1[:, 0:n], ALU.mult, [bkA_b, t1_b], [kmod_b])
                TTO(g, "pool", bb_[:, 0:n], kkn[:, 0:n], av[:, 0:n], ALU.mult, [kkn_b, av_b], [bb_b])
                if sample:
                    P.op("pool", lambda e, pad0=pad0: e.memset(bb_[:, pad0:pad0 + 64], 0.0), writes=[bb_b])
                    P.op("pool", lambda e, pad0=pad0: e.memset(kmod[:, pad0:pad0 + 64], 0.0), writes=[kmod_b])
                for (s0, sn, ch) in segs:
                    P.op("dve", lambda e, s0=s0, sn=sn: e.tensor_tensor_scan(cs[:, s0:s0 + sn], onesf[:, 0:sn], sg[:, s0:s0 + sn], 0.0, ALU.mult, ALU.add),
                         reads=[sg_b, cb], writes=[cs_b])
                TTO(g, "pool", csx[:, 0:n], cs[:, 0:n], sg[:, 0:n], ALU.subtract, [cs_b, sg_b], [csx_b])
                if stop <= 2.5:
                    return
                ACTF(g, E1[:, 0:n], cs[:, 0:n], AF.Exp, [cs_b], [E1_b], scale=-C0)
                ACTF(g, E2[:, 0:n], csx[:, 0:n], AF.Exp, [csx_b], [E2_b], scale=-C0)
                ACTF(g, E3[:, 0:n], cs[:, 0:n], AF.Exp, [cs_b], [E3_b], scale=C0)
                for si, (s0, sn, ch) in enumerate(segs):
                    TSC(g, "dve", nb[:, si:si + 1], cs[:, s0 + sn - 1:s0 + sn], -C0, None, ALU.mult, None, [cs_b], [nb_b])
                    ACTF(g, E4[:, s0:s0 + sn], cs[:, s0:s0 + sn], AF.Exp, [cs_b, nb_b], [E4_b], bias=nb[:, si:si + 1], scale=C0)
                TTO(g, "dve", kr[:, 0, 0:n], kkn[:, 0:n], E2[:, 0:n], ALU.mult, [kkn_b, E2_b], [kr_b[0]])
                TTO(g, "dve", kr[:, 1, 0:n], r_ps, E1[:, 0:n], ALU.mult, [bkA_b, E1_b], [kr_b[1]])
                TTO(g, "pool", Bh[:, 0:n], bb_[:, 0:n], E3[:, 0:n], ALU.mult, [bb_b, E3_b], [Bh_b])
                TTO(g, "pool", Kh[:, 0:n], kmod[:, 0:n], E3[:, 0:n], ALU.mult, [kmod_b, E3_b], [Kh_b])
                TTO(g, "pool", src4[:, 1, 0:n], bb_[:, 0:n], E4[:, 0:n], ALU.mult, [bb_b, E4_b], [s4_b[1]])
                TTO(g, "pool", src4[:, 2, 0:n], kmod[:, 0:n], E4[:, 0:n], ALU.mult, [kmod_b, E4_b], [s4_b[2]])
                STT(g, rkr[:, 0:n], r_ps, pvc(g, "r_k", p), kmod[:, 0:n], ALU.mult, ALU.mult, [bkA_b, kmod_b, cb], [rkr_b])
                MM(g, bkC[:, 128:128 + n], BDf, rkr[:, 0:n], [cb, rkr_b], [bkC_b])
                TTO(g, "dve", bonus[:, 0:n], bkC[:, 128:128 + n], vsb[:, 0:n], ALU.mult, [bkC_b, vsb_b], [bonus_b])
                if stop <= 2.6:
                    return
                bkT, bkT_b = next_bank(g)
                bkT16 = bkT[:].bitcast(BF16)
                TRP(g, bkT16[0:n, 0:128], kr[:, 0, 0:n], identb, [kr_b[0], cb], [bkT_b])
                for j in (1, 2, 3):
                    TRP(g, bkT16[0:n, j * 128:(j + 1) * 128], src4[:, j, 0:n], identb, [s4_b[j], cb], [bkT_b])
                CPY(g, "act", tok4[0:n, :, :].rearrange("p a b -> p (a b)"), bkT16[0:n, 0:512], [bkT_b], [tok4_b])

                if stop <= 3:
                    return
                def head_gen(hh, p=p, n=n, segs=segs, L=L, MSN=MSN, MSNT=MSNT, MS=MS, MI=MI):
                    h0 = hh * 64
                    hs = slice(h0, h0 + 64)
                    bkU, bkU_b = next_bank(g)
                    bkW, bkW_b = next_bank(g)
                    krh = kr[hs, :, 0:n]
                    for a_ in range(2):
                        MM(g, bkU[0:n, a_ * 128:a_ * 128 + n], Bh[hs, 0:n], kr[hs, a_, 0:n], [Bh_b, kr_b[a_]], [bkU_b])
                        MM(g, bkU[0:n, 256 + a_ * 128:256 + a_ * 128 + n], Kh[hs, 0:n], kr[hs, a_, 0:n], [Kh_b, kr_b[a_]], [bkU_b])
                    MM(g, bkW[0:n, 0:n], kr[hs, 0, 0:n], Bh[hs, 0:n], [Bh_b, kr_b[0]], [bkW_b])
                    X0 = XX[hh][0]
                    TTO(g, "dve", X0[0:n, 1, 0:n], bkU[0:n, 0:n], MSN, ALU.mult, [bkU_b, cb], [XX_b[hh][0]])
                    TTO(g, "dve", X0[0:n, 0, 0:n], bkW[0:n, 0:n], MSNT, ALU.mult, [bkW_b, cb], [XX_b[hh][0]])
                    TTO(g, "dve", AbTm[hh][0:n, 0:n], bkU[0:n, 128:128 + n], MI, ALU.mult, [bkU_b, cb], [AbT_b[hh]])
                    TTO(g, "dve", LkTm[hh][0:n, 0:n], bkU[0:n, 256:256 + n], MS, ALU.mult, [bkU_b, cb], [LkT_b[hh]])
                    TTO(g, "dve", AkTm[hh][0:n, 0:n], bkU[0:n, 384:384 + n], MI, ALU.mult, [bkU_b, cb], [AkT_b[hh]])
                    Vh = tok4[0:n, 3, h0:h0 + 64]
                    Bch = tok4[0:n, 1, h0:h0 + 64]
                    Kch = tok4[0:n, 2, h0:h0 + 64]
                    MM(g, bkW[0:n, 128:192], LkTm[hh][0:n, 0:n], Vh, [LkT_b[hh], tok4_b], [bkW_b])
                    Y = YY[hh][0]
                    CPY(g, "pool", Y[0:n, 0:64], tok4[0:n, 0, h0:h0 + 64], [tok4_b], [YY_b[hh][0]])
                    CPY(g, "act", Y[0:n, 64:128], bkW[0:n, 128:192], [bkW_b], [YY_b[hh][0]])
                    for lv in range(L):
                        yield
                        cur, nxt = lv % 2, (lv + 1) % 2
                        Xc = XX[hh][cur]
                        bkY, bkY_b = next_bank(g)
                        MM(g, bkY[0:n, 0:128], Xc[0:n, 1, 0:n], YY[hh][cur][0:n, :], [XX_b[hh][cur], YY_b[hh][cur]], [bkY_b], start=True, stop=False)
                        MM(g, bkY[0:n, 0:128], identb[0:n, 0:n], YY[hh][cur][0:n, :], [cb, YY_b[hh][cur]], [bkY_b], start=False, stop=True)
                        if lv < L - 1:
                            bkX, bkX_b = next_bank(g)
                            MM(g, bkX[0:n, 0:n], Xc[0:n, 1, 0:n], Xc[0:n, 0, 0:n], [XX_b[hh][cur]], [bkX_b])
                            MM(g, bkX[0:n, 128:128 + n], Xc[0:n, 0, 0:n], Xc[0:n, 1, 0:n], [XX_b[hh][cur]], [bkX_b])
                            CPY(g, "act", YY[hh][nxt][0:n, :], bkY[0:n, 0:128], [bkY_b], [YY_b[hh][nxt]])
                            CPY(g, "dve", XX[hh][nxt][0:n, :, 0:n], bkX[0:n, 0:256].rearrange("p (a b) -> p a b", a=2)[:, :, 0:n], [bkX_b], [XX_b[hh][nxt]])
                        else:
                            CPY(g, "act", Wt[hh][0:n, :], bkY[0:n, 0:64], [bkY_b], [Wt_b[hh]])
                            P.op("act", lambda e, hh=hh, n=n, bkY=bkY: e.mul(Uv[hh][0:n, :], bkY[0:n, 64:128], -1.0), reads=[bkY_b], writes=[Uv_b[hh]])
                    yield
                    if stop <= 4:
                        return
                    bkQ, bkQ_b = next_bank(g)
                    MM(g, bkQ[hs, 0:n], Wt[hh][0:n, :], AbTm[hh][0:n, 0:n], [Wt_b[hh], AbT_b[hh]], [bkQ_b])
                    TTO(g, "dve", QtT[hs, 0:n], kr[hs, 1, 0:n], bkQ[hs, 0:n], ALU.subtract, [kr_b[1], bkQ_b], [QtT_b[hh]])
                    for si, (s0, sn, ch) in enumerate(segs):
                        MM(g, bkQ[hs, 128 + si * 64:192 + si * 64], Wt[hh][s0:s0 + sn, :], tok4[s0:s0 + sn, 1, h0:h0 + 64], [Wt_b[hh], tok4_b], [bkQ_b])
                    nsg = len(segs)
                    P.op("act", lambda e, hs=hs, nsg=nsg, bkQ=bkQ: e.mul(Gneg[hs, 0:nsg, :].rearrange("p a b -> p (a b)"), bkQ[hs, 128:128 + 64 * nsg], -1.0),
                         reads=[bkQ_b], writes=[Gneg_b[hh]])
                    yield
                    if stop <= 5:
                        return
                    bkO, bkO_b = next_bank(g)
                    MM(g, bkO[hs, 0:n], Uv[hh][0:n, :], AbTm[hh][0:n, 0:n], [Uv_b[hh], AbT_b[hh]], [bkO_b], start=True, stop=False)
                    MM(g, bkO[hs, 0:n], Vh, AkTm[hh][0:n, 0:n], [tok4_b, AkT_b[hh]], [bkO_b], start=False, stop=False)
                    for si, (s0, sn, ch) in enumerate(segs):
                        MM(g, bkO[hs, s0:s0 + sn], Mb[ch][hs, p, :], QtT[hs, s0:s0 + sn], [M_b[ch][p][hh], QtT_b[hh]], [bkO_b],
                           start=False, stop=(si == len(segs) - 1))
                    CPY(g, "act", osb[hs, 0, 0:n], bkO[hs, 0:n], [bkO_b], [osb_b])
                    bkS, bkS_b = next_bank(g)
                    for si, (s0, sn, ch) in enumerate(segs):
                        so = bkS[hs, si * 64:si * 64 + 64]
                        MM(g, so, Gneg[hs, si, :], Mb[ch][hs, p, :], [Gneg_b[hh], M_b[ch][p][hh]], [bkS_b], start=True, stop=False)
                        MM(g, so, tok4[s0:s0 + sn, 1, h0:h0 + 64], Uv[hh][s0:s0 + sn, :], [tok4_b, Uv_b[hh]], [bkS_b], start=False, stop=False)
                        MM(g, so, tok4[s0:s0 + sn, 2, h0:h0 + 64], tok4[s0:s0 + sn, 3, h0:h0 + 64], [tok4_b], [bkS_b], start=False, stop=True)
                    for si, (s0, sn, ch) in enumerate(segs):
                        STT(g, M32[ch][hs, p, :], M32[ch][hs, p, :], E1[hs, s0 + sn - 1:s0 + sn], bkS[hs, si * 64:si * 64 + 64], ALU.mult, ALU.add,
                            [M_b[ch][p][hh], E1_b, bkS_b], [M_b[ch][p][hh]])
                        CPY(g, "act", Mb[ch][hs, p, :], M32[ch][hs, p, :], [M_b[ch][p][hh]], [M_b[ch][p][hh]])
                run_interleaved([head_gen(0), head_gen(1)])
                if stop <= 6:
                    return
                ACTF(g, osb[:, 1, 0:n], osb[:, 0, 0:n], AF.Square, [osb_b], [osb_b])
                bkG, bkG_b = next_bank(g)
                for a_ in range(2):
                    MM(g, bkG[:, a_ * 128:a_ * 128 + n], BDf, osb[:, a_, 0:n], [cb, osb_b], [bkG_b])
                P.op("act", lambda e, n=n, bkG=bkG: e.mul(gm[:, 0:n], bkG[:, 0:n], 1.0 / 64), reads=[bkG_b], writes=[gm_b])
                TTO(g, "pool", gm2[:, 0:n], gm[:, 0:n], gm[:, 0:n], ALU.mult, [gm_b], [gm2_b])
                STT(g, gv[:, 0:n], bkG[:, 128:128 + n], 1.0 / 64, gm2[:, 0:n], ALU.mult, ALU.subtract, [bkG_b, gm2_b], [gv_b])
                ACTF(g, gv[:, 0:n], gv[:, 0:n], AF.Sqrt, [gv_b, cb], [gv_b], bias=g.consts[:, C_EPS_GN:C_EPS_GN + 1])
                P.op("dve", lambda e, n=n: e.reciprocal(out=gv[:, 0:n], in_=gv[:, 0:n]), reads=[gv_b], writes=[gv_b])
                TTO(g, "dve", gm2[:, 0:n], osb[:, 0, 0:n], gm[:, 0:n], ALU.subtract, [osb_b, gm_b, gm2_b], [gm2_b])
                TTO(g, "dve", gm2[:, 0:n], gm2[:, 0:n], gv[:, 0:n], ALU.mult, [gm2_b, gv_b], [gm2_b])
                TSC(g, "dve", gm2[:, 0:n], gm2[:, 0:n], pvc(g, "gn_w", p), pvc(g, "gn_b", p), ALU.mult, ALU.add, [gm2_b, cb], [gm2_b])
                TTO(g, "dve", gm2[:, 0:n], gm2[:, 0:n], bonus[:, 0:n], ALU.add, [gm2_b, bonus_b], [gm2_b])
                TTO(g, "dve", ogT[:, p, 0:n], gm2[:, 0:n], gsb[:, 0:n], ALU.mult, [gm2_b, gsb_b], [og_b[p]])
            for dc in range(NCH):
                bk, bkb = next_bank(g)
                for p in range(8):
                    MM(g, bk[:, 0:n], Wo[:, p, dc * 128:(dc + 1) * 128], ogT[:, p, 0:n], [wb, og_b[p]], [bkb], start=(p == 0), stop=(p == 7))
                xb = xt_bufs_for_cols(g, dc, col0, n)
                ca, cn_ = col0 + real0, realn
                TTO(g, "dve", g.xT[:, dc, ca:ca + cn_], g.xT[:, dc, ca:ca + cn_], bk[:, real0:real0 + realn], ALU.add, xb + [bkb], xb)
            if ti == 16 or sample:
                for (s0, sn, ch) in segs:
                    for p in range(8):
                        bk, bkb = next_bank(g)
                        TRP(g, bk[0:64, 0:128], M32[ch][:, p, :], identf, M_b[ch][p] + [cb], [bkb])
                        CPY(g, "act", outstg[:, 2 * p:2 * p + 2, :].rearrange("v h k -> v (h k)"), bk[0:64, 0:128], [bkb], [outstg_b])
                    dst = d["state_wkv_p"] if ch == 0 else d["state_wkv_s"][ch - 1]
                    P.dma("sp", dst.rearrange("h v k -> v h k"), outstg[:], reads=[outstg_b])
        P.barrier()


def sb_attn(g, do_sample=True, pairs=None):
    _sb_attn(g, do_sample, pairs)
    g.P.barrier()


def _sb_attn(g, do_sample=True, pairs=None):
    nc, P, d = g.nc, g.P, g.dram
    stop = getattr(g, "sb_stop", 99)
    cb = g.cb
    pairs = list(range(8)) if pairs is None else pairs
    identb = g.consts_bf[:, C_IDENT:C_IDENT + 128]
    TINCL = g.consts[:, C_TRI_INCL:C_TRI_INCL + 128]
    TCOMP = g.consts[:, C_TRI_COMP:C_TRI_COMP + 128]
    MS1 = g.consts[:, C_M1 + 256:C_M1 + 384]
    MS2 = g.consts[:, C_M2 + 256:C_M2 + 384]
    with contextlib.ExitStack() as es:
        def sb(name, shape, dt):
            return es.enter_context(g_sbuf(nc, "sb_" + name, shape, dt))
        xn, xn_b = rmsnorm_T(g, es, "mix_g1")
        gqk = sb("gqk", [128, 256], F32)
        gqk_b = Buf("gqk")
        P.dma("sp", gqk[:], d["gqk"], writes=[gqk_b])
        wq = [sb("wq%d" % i, [128, NCH, 3, 128], BF16) for i in range(2)]
        wq_b = [Buf("wq") for _ in range(2)]
        wo = [sb("wo%d" % i, [128, D], BF16) for i in range(2)]
        wo_b = [Buf("wo") for _ in range(2)]
        QT = sb("QT", [128, NTOK], BF16)
        KT = sb("KT", [128, NTOK], BF16)
        Vt = sb("Vt", [128, 18, 128], BF16)
        oT = sb("oT", [128, NTOK], BF16)
        QT_b, KT_b, Vt_b = Buf("QT"), Buf("KT"), Buf("Vt")
        oT_b = [Buf("oT") for _ in range(NCT)]
        qksb = sb("qksb", [128, 256], F32)
        qksb_b = Buf("qksb")
        sqf = sb("sqf", [128, 256], F32)
        sqf_b = Buf("sqf")
        ss = sb("ss", [128, 4], F32)
        ss_b = Buf("ss")
        tq = sb("tq", [128, 256], F32)
        tq_b = [Buf("tq") for _ in range(4)]
        qkn = [sb("qkn%d" % i, [128, 256], F32) for i in range(2)]
        qkn_b = [Buf("qkn") for _ in range(2)]
        qkb = sb("qkb", [128, 256], BF16)
        qkb_b = Buf("qkb")
        P.op("pool", lambda e: e.memset(qkb[:], 0.0), writes=[qkb_b])
        vf = [sb("vf%d" % i, [128, 128], F32) for i in range(2)]
        vf_b = [Buf("vf") for _ in range(2)]
        ef = [sb("ef%d" % i, [128, 512], F32) for i in range(2)]
        ef_b = [Buf("ef") for _ in range(2)]
        spf = [sb("spf%d" % i, [128, 512], F32) for i in range(2)]
        spf_b = [Buf("spf") for _ in range(2)]
        Xf = [sb("Xf%d" % i, [128, 512], F32) for i in range(2)]
        Xf_b = [Buf("Xf") for _ in range(2)]
        Ab = [sb("Ab%d" % i, [128, 512], BF16) for i in range(2)]
        Ab_b = [Buf("Ab") for _ in range(2)]
        if do_sample:
            Kp = sb("Kp", [128, 32, 2, 64], BF16)
            Vp = sb("Vp", [128, 32, 2, 64], BF16)
            KTp = sb("KTp", [128, PAST], BF16)
            Kp_b, Vp_b, KTp_b = Buf("Kp"), Buf("Vp"), Buf("KTp")
        w_qkv = d["sb_w_qkv"].rearrange("(c p) (t n) -> p c t n", p=128, t=3)

        def load_w(p):
            s = p % 2
            pc = p * 128
            for t_ in range(3):
                P.dma("pool", wq[s][:, :, t_, :], w_qkv[:, :, t_, pc:pc + 128], writes=[wq_b[s]])
            P.dma("pool", wo[s][:], d["sb_w_o"][pc:pc + 128, :], writes=[wo_b[s]])

        stepk = [0]

        reserved = set()

        def free_bank():
            bk_, bb_ = next_bank(g)
            while id(bb_) in reserved:
                bk_, bb_ = next_bank(g)
            return bk_, bb_

        NSL = 8
        efs_b = [[Buf("ef") for _ in range(NSL)] for _ in range(2)]
        sps_b = [[Buf("sp") for _ in range(NSL)] for _ in range(2)]
        Xs_b = [[Buf("X") for _ in range(NSL)] for _ in range(2)]
        As_b = [[Buf("A") for _ in range(NSL)] for _ in range(2)]

        def attn_steps(hh, qsrc_cols, nq, steps, out_dst, out_bufs, nslots=1):
            h0 = hh * 64
            hs = slice(h0, h0 + 64)
            accbk, accb = free_bank()
            reserved.add(id(accb))
            obk, obb = free_bank()
            reserved.add(id(obb))
            i = hh
            for si, (kT_ap, v_ap, nk, lo, mask_ap, rds) in enumerate(steps):
                sl = si % nslots
                so = sl * 64 if nslots > 1 else 0
                e_b, s_b, x_b, a_b = efs_b[i][sl], sps_b[i][sl], Xs_b[i][sl], As_b[i][sl]
                zbk, zbb = free_bank()
                MM(g, zbk[0:nk, lo:nq], kT_ap, QT[hs, qsrc_cols + lo:qsrc_cols + nq], rds + [QT_b], [zbb])
                ACTF(g, ef[i][0:nk, so + lo:so + nq], zbk[0:nk, lo:nq], AF.Exp, [zbb], [e_b], scale=0.125)
                if mask_ap is not None:
                    mw = mask_ap.shape[1]
                    TTO(g, "dve", ef[i][0:nk, so + lo:so + lo + mw], ef[i][0:nk, so + lo:so + lo + mw], mask_ap, ALU.mult, [e_b, cb], [e_b])
                ACTF(g, spf[i][0:nk, so + lo:so + nq], ef[i][0:nk, so + lo:so + nq], AF.Ln, [e_b], [s_b], bias=g.consts[0:nk, C_ONE:C_ONE + 1])
                MM(g, accbk[:, lo:nq], TINCL[0:nk, :], spf[i][0:nk, so + lo:so + nq], [cb, s_b], [accb], start=(si == 0), stop=True, skip=True)
                ACTF(g, Xf[i][0:nk, so + lo:so + nq], accbk[0:nk, lo:nq], AF.Exp, [accb], [x_b], scale=-1.0)
                MM(g, accbk[:, lo:nq], TCOMP[0:nk, :], spf[i][0:nk, so + lo:so + nq], [cb, s_b], [accb], start=False, stop=True, skip=True)
                TTO(g, "dve", Ab[i][0:nk, so + lo:so + nq], ef[i][0:nk, so + lo:so + nq], Xf[i][0:nk, so + lo:so + nq], ALU.mult, [e_b, x_b], [a_b])
                MM(g, obk[hs, lo:nq], v_ap, Ab[i][0:nk, so + lo:so + nq], rds + [a_b], [obb], start=(si == 0), stop=True, skip=True)
                yield
            CPY(g, "act", out_dst, obk[hs, 0:nq], [obb], out_bufs)
            reserved.discard(id(accb))
            reserved.discard(id(obb))

        load_w(pairs[0])
        for pi, p in enumerate(pairs):
            s = p % 2
            if pi + 1 < len(pairs):
                load_w(pairs[pi + 1])
            for ti, (col0, n) in enumerate(TT):
                bk, bkb = next_bank(g)
                xb = []
                for c in range(NCH):
                    xb += [xn_b[c][t] for t, (c0, nn) in enumerate(COLT) if c0 < col0 + n and col0 < c0 + nn]
                for c in range(NCH):
                    MM(g, bk[0:n, 0:384], xn[:, c, col0:col0 + n], wq[s][:, c, :, :].rearrange("p t n -> p (t n)"), xb + [wq_b[s]], [bkb],
                       start=(c == 0), stop=(c == NCH - 1))
                if stop <= 1:
                    return
                CPY(g, "act", qksb[0:n, :], bk[0:n, 0:256], [bkb], [qksb_b])
                ACTF(g, sqf[0:n, :], qksb[0:n, :], AF.Square, [qksb_b], [sqf_b])
                P.op("dve", lambda e, n=n: e.tensor_reduce(out=ss[0:n, :], in_=sqf[0:n, :].rearrange("p (a b) -> p a b", a=4), axis=AX.X, op=ALU.add),
                     reads=[sqf_b], writes=[ss_b])
                ACTF(g, ss[0:n, :], ss[0:n, :], AF.Sqrt, [ss_b, cb], [ss_b], bias=g.consts[0:n, C_EPS_RMS:C_EPS_RMS + 1], scale=1.0 / 64)
                P.op("dve", lambda e, n=n: e.reciprocal(out=ss[0:n, :], in_=ss[0:n, :]), reads=[ss_b], writes=[ss_b])
                if stop <= 2:
                    return
                for a_ in range(4):
                    STT(g, tq[0:n, a_ * 64:(a_ + 1) * 64], qksb[0:n, a_ * 64:(a_ + 1) * 64], ss[0:n, a_:a_ + 1], gqk[0:n, a_ * 64:(a_ + 1) * 64],
                        ALU.mult, ALU.mult, [qksb_b, ss_b, gqk_b], [tq_b[a_]])
                if stop <= 3:
                    return
                j = ti % 2
                CPY(g, "pool", qkn[j][0:n, :], tq[0:n, :], tq_b, [qkn_b[j]])
                CPY(g, "act", qkb[0:n, :], tq[0:n, :], tq_b, [qkb_b])
                CPY(g, "act", vf[j][0:n, :], bk[0:n, 256:384], [bkb], [vf_b[j]])
                CPY(g, "pool", Vt[0:n, ti, :], vf[j][0:n, :], [vf_b[j]], [Vt_b])
                if stop <= 4:
                    return
                bkT, bkT_b = next_bank(g)
                bkT16 = bkT[:].bitcast(BF16)
                TRP(g, bkT16[:, 0:128], qkb[:, 0:128], identb, [qkb_b, cb], [bkT_b])
                TRP(g, bkT16[:, 128:256], qkb[:, 128:256], identb, [qkb_b, cb], [bkT_b])
                CPY(g, "act", QT[:, col0:col0 + n], bkT16[:, 0:n], [bkT_b], [QT_b])
                CPY(g, "act", KT[:, col0:col0 + n], bkT16[:, 128:128 + n], [bkT_b], [KT_b])
                if stop <= 5:
                    return
                if ti <= 16:
                    kd = d["cache_k_p"][2 * p:2 * p + 2, col0:col0 + n, :].rearrange("h t d -> t h d")
                    vd = d["cache_v_p"][2 * p:2 * p + 2, col0:col0 + n, :].rearrange("h t d -> t h d")
                    P.dma("sp", kd, qkn[j][0:n, 128:256].rearrange("p (h d) -> p h d", h=2), reads=[qkn_b[j]])
                    P.dma("sp", vd, vf[j][0:n, :].rearrange("p (h d) -> p h d", h=2), reads=[vf_b[j]])
                else:
                    for sq_ in range(2):
                        r0 = sq_ * 64
                        kd = d["cache_k_s"][sq_, 2 * p:2 * p + 2, :, :].rearrange("h t d -> t h d")
                        vd = d["cache_v_s"][sq_, 2 * p:2 * p + 2, :, :].rearrange("h t d -> t h d")
                        P.dma("sp", kd, qkn[j][r0:r0 + 64, 128:256].rearrange("p (h d) -> p h d", h=2), reads=[qkn_b[j]])
                        P.dma("sp", vd, vf[j][r0:r0 + 64, :].rearrange("p (h d) -> p h d", h=2), reads=[vf_b[j]])
                if stop <= 5.5 or (stop <= 5.7 and ti == 1):
                    return
            if stop <= 6:
                return
            if not do_sample:
                P.op("pool", lambda e: e.memset(oT[:, TP:NTOK], 0.0), writes=[oT_b[NCT - 1]])
            chunks = [(0, 0, 0)] + [(1 + 4 * i, 4 + 4 * i, 1) for i in range(4)]
            for (t_a, t_b, _) in chunks:
                gens = []
                for hh in range(2):
                    h0 = hh * 64
                    hs = slice(h0, h0 + 64)
                    qc0 = TT[t_a][0]
                    nq = TT[t_b][0] + TT[t_b][1] - qc0
                    steps = []
                    for kb in range(t_b, -1, -1):
                        k0, nk = TT[kb]
                        if kb >= t_a:
                            lo = k0 - qc0
                            mask = MS1[0:nk, 0:nk]
                        else:
                            lo = 0
                            mask = None
                        steps.append((KT[hs, k0:k0 + nk], Vt[0:nk, kb, h0:h0 + 64], nk, lo, mask, [KT_b, Vt_b]))
                    ob_ = [oT_b[t] for t, (c0, nn) in enumerate(COLT) if c0 < qc0 + nq and qc0 < c0 + nn]
                    gens.append(attn_steps(hh, qc0, nq, steps, oT[hs, qc0:qc0 + nq], ob_))
                run_interleaved(gens)
                if stop <= 7:
                    return
            if do_sample:
                for sq_ in range(2):
                    for h_ in range(2):
                        P.dma("pool", Kp[:, :, h_, :], d["cache_k_in"][sq_, 2 * p + h_, :, :].rearrange("(t k) d -> k t d", k=128), writes=[Kp_b])
                        P.dma("pool", Vp[:, :, h_, :], d["cache_v_in"][sq_, 2 * p + h_, :, :].rearrange("(t k) d -> k t d", k=128), writes=[Vp_b])
                    for t8 in range(4):
                        bkT, bkT_b = next_bank(g)
                        bkT16 = bkT[:].bitcast(BF16)
                        for j in range(8):
                            t = t8 * 8 + j
                            TRP(g, bkT16[:, j * 128:(j + 1) * 128], Kp[:, t, :, :].rearrange("p h d -> p (h d)"), identb, [Kp_b, cb], [bkT_b])
                        CPY(g, "act", KTp[:, t8 * 1024:(t8 + 1) * 1024], bkT16[:, 0:1024], [bkT_b], [KTp_b])
                    gens = []
                    for hh in range(2):
                        h0 = hh * 64
                        hs = slice(h0, h0 + 64)
                        qc0 = TP + sq_ * 64
                        steps = [(KT[hs, TP:TP + 128], Vt[:, 17, h0:h0 + 64], 128, 0, MS2[:, sq_ * 64:sq_ * 64 + 64], [KT_b, Vt_b])]
                        for t in range(31, -1, -1):
                            steps.append((KTp[hs, t * 128:(t + 1) * 128], Vp[:, t, hh, :], 128, 0, None, [KTp_b, Vp_b]))
                        gens.append(attn_steps(hh, qc0, 64, steps, oT[hs, qc0:qc0 + 64], [oT_b[NCT - 1]], nslots=NSL))
                    run_interleaved(gens)
            for dc in range(NCH):
                for t, (c0, n) in enumerate(COLT):
                    bk, bkb = next_bank(g)
                    MM(g, bk[:, 0:n], wo[s][:, dc * 128:(dc + 1) * 128], oT[:, c0:c0 + n], [wo_b[s], oT_b[t]], [bkb])
                    TTO(g, "dve", g.xT[:, dc, c0:c0 + n], g.xT[:, dc, c0:c0 + n], bk[:, 0:n], ALU.add, [g.xT_b[dc][t], bkb], [g.xT_b[dc][t]])
    P.barrier()


def make_consts():
    c = np.zeros((128, NCONST), np.float32)
    c[:, C_IDENT:C_IDENT + 128] = np.eye(128, dtype=np.float32)
    c[:, C_ONES:C_ONES + 128] = 1.0
    kp = np.arange(128)[:, None]
    k = np.arange(128)[None, :]
    c[:, C_TRI_INCL:C_TRI_INCL + 128] = (kp >= k).astype(np.float32)
    c[:, C_TRI_COMP:C_TRI_COMP + 128] = (kp < k).astype(np.float32)
    c[:, C_EPS_RMS] = RMS_EPS
    c[:, C_ONE] = 1.0
    pp = np.arange(128)
    c[:, C_BD:C_BD + 128] = (pp[:, None] // 64 == pp[None, :] // 64).astype(np.float32)
    for ofs, seg in ((C_M1, np.zeros(128, int)), (C_M2, pp // 64)):
        same = (seg[:, None] == seg[None, :])
        s_lt_t = ((pp[:, None] < pp[None, :]) & same).astype(np.float32)
        s_le_t = ((pp[:, None] <= pp[None, :]) & same).astype(np.float32)
        c[:, ofs:ofs + 128] = -s_lt_t
        c[:, ofs + 128:ofs + 256] = -s_lt_t.T
        c[:, ofs + 256:ofs + 384] = s_lt_t
        c[:, ofs + 384:ofs + 512] = s_le_t
    c[:, C_EPS_GN] = GN_EPS
    return c


def fm(vec):
    return np.ascontiguousarray(np.asarray(vec, np.float32).reshape(NCH, 128).T)


def make_pvec(inp):
    cols = [fm(inp["ffn_norm_g"][0, 0]), fm(inp["ffn_norm_g"][0, 1]), fm(inp["ffn_norm_g"][1, 0]), fm(inp["ffn_norm_g"][1, 1]),
            fm(inp["mix_norm_g"][0]), fm(inp["mix_norm_g"][1])]
    for i in range(6):
        cols.append(fm(inp["rwkv_mu"][i]))
    for nm in ("rwkv_w0", "rwkv_a0", "rwkv_k_k", "rwkv_k_a", "rwkv_r_k", "rwkv_gn_w", "rwkv_gn_b"):
        cols.append(fm(np.asarray(inp[nm]).reshape(-1)))
    return np.ascontiguousarray(np.concatenate(cols, axis=1))


ALL_STAGES = {("ffn", 0, 0), ("ffn", 0, 1), ("ffn", 1, 0), ("ffn", 1, 1), "rwkv", "sb"}
_NC_CACHE = {}


def make_in_maps(inputs, ncores=8):
    consts = make_consts()
    pvec = make_pvec(inputs)
    f = lambda a: np.ascontiguousarray(np.asarray(a, np.float32))
    in_maps = []
    gq = np.asarray(inputs["sb_q_norm_g"], np.float32)
    gk = np.asarray(inputs["sb_k_norm_g"], np.float32)
    gqk = np.ascontiguousarray(np.broadcast_to(np.concatenate([gq, gq, gk, gk])[None, :], (128, 256)))
    for i in range(ncores):
        in_maps.append({
            "x_prompt": f(inputs["x_prompt"][i]),
            "x_sample": f(inputs["x_sample"][2 * i:2 * i + 2]).reshape(2 * DSEQ, D),
            "meta_tokens": f(inputs["meta_tokens"]),
            "consts": consts,
            "pvec": pvec,
            "ffn_w_in": f(inputs["ffn_w_in"]),
            "ffn_w_out": f(inputs["ffn_w_out"]),
            "rwkv_w_rkv": f(inputs["rwkv_w_rkv"]), "rwkv_w_o": f(inputs["rwkv_w_o"]),
            "rwkv_w1": f(inputs["rwkv_w1"]), "rwkv_w2": f(inputs["rwkv_w2"]),
            "rwkv_a1": f(inputs["rwkv_a1"]), "rwkv_a2": f(inputs["rwkv_a2"]),
            "rwkv_g1": f(inputs["rwkv_g1"]), "rwkv_g2": f(inputs["rwkv_g2"]),
            "shift_in": np.ascontiguousarray(np.concatenate([fm(inputs["state_rwkv_shift"][2 * i]), fm(inputs["state_rwkv_shift"][2 * i + 1])], 1)),
            "state_wkv_in": f(inputs["state_rwkv_wkv"][2 * i:2 * i + 2]),
            "sb_w_qkv": f(inputs["sb_w_qkv"]), "sb_w_o": f(inputs["sb_w_o"]),
            "gqk": gqk,
        })
        if "cache_sb_k" in inputs:
            in_maps[-1]["cache_k_in"] = f(inputs["cache_sb_k"][2 * i:2 * i + 2])
            in_maps[-1]["cache_v_in"] = f(inputs["cache_sb_v"][2 * i:2 * i + 2])
    return in_maps


def run(inputs, stages=None, ncores=8, trace=False):
    stages = ALL_STAGES if stages is None else stages
    key = tuple(sorted(stages, key=repr))
    if key not in _NC_CACHE:
        _NC_CACHE[key] = build(stages)
    nc = _NC_CACHE[key]
    in_maps = make_in_maps(inputs, ncores)
    res = run_bass_kernel_spmd(nc, in_maps, core_ids=list(range(ncores)), trace=trace)
    if trace:
        print('exec_time_ns', res.exec_time_ns)
    return res.results


def kernel(**inputs):
    r = run(inputs)
    f32 = np.float32
    y_prompt = np.stack([r[i]["y_prompt"] for i in range(8)], 0).astype(f32)
    y_sample = np.concatenate([r[i]["y_sample"].reshape(2, DSEQ, D) for i in range(8)], 0).astype(f32)
    S_p = np.stack([r[i]["state_wkv_p"] for i in range(8)], 0).astype(f32)
    sh_p = np.stack([r[i]["shift_p"].reshape(D) for i in range(8)], 0).astype(f32)
    k_p = np.stack([r[i]["cache_k_p"] for i in range(8)], 0).astype(f32)
    v_p = np.stack([r[i]["cache_v_p"] for i in range(8)], 0).astype(f32)
    S_s = np.concatenate([r[i]["state_wkv_s"] for i in range(8)], 0).astype(f32)
    sh_s = np.concatenate([r[i]["shift_s"].reshape(2, D) for i in range(8)], 0).astype(f32)
    k_s = np.concatenate([r[i]["cache_k_s"] for i in range(8)], 0).astype(f32)
    v_s = np.concatenate([r[i]["cache_v_s"] for i in range(8)], 0).astype(f32)
    return (y_prompt, y_sample, S_p, sh_p, k_p, v_p, S_s, sh_s, k_s, v_s)
```
